# Optimizing a Trainium2 kernel written in Bass

```python
import math
import jax, jax.numpy as jnp
from jax import lax
import numpy as np

D_MODEL = 1024
BATCH = 4
SEQ = 4096
DEPTH = 2
DEC_BATCH = 128
DEC_SEQ = 1
PAST_LEN = 8192
PAGE_SIZE = 128

BRANCH_WIDTH = D_MODEL // 2
N_BRANCH = 3
ML_HEADS = 4
ML_DK = BRANCH_WIDTH // ML_HEADS
ML_DV = BRANCH_WIDTH // ML_HEADS
ML_CHUNK = 128
SWA_HEAD_DIM = 64
SWA_HEADS = BRANCH_WIDTH // SWA_HEAD_DIM
SWA_KV_HEADS = 2
SWA_GROUP = SWA_HEADS // SWA_KV_HEADS
WINDOW = 128
ROT_DIM = SWA_HEAD_DIM // 4
ROPE_THETA = 500000.0
MEM_TOKENS = 256
X_HEADS = 4
X_HEAD_DIM = BRANCH_WIDTH // X_HEADS
D_FF = -(-8 * D_MODEL // (3 * 256)) * 256
LN_EPS = 1e-5
HEAD_NORM_EPS = 1e-6
DEEPNORM_ALPHA = (2 * DEPTH) ** 0.25
DEEPNORM_BETA = (8 * DEPTH) ** -0.25
NEG_INF = -1e30
D_IN = (2 * ML_HEADS * ML_DK + 2 * ML_HEADS * ML_DV + 2 * ML_HEADS
        + SWA_HEADS * SWA_HEAD_DIM + 2 * SWA_KV_HEADS * SWA_HEAD_DIM
        + X_HEADS * X_HEAD_DIM + N_BRANCH * D_MODEL)

kernel_name = "hybrid_mlstm_swa_memory_decoder_step"


def _in_split_points():
    sizes = [ML_HEADS * ML_DK, ML_HEADS * ML_DK, ML_HEADS * ML_DV, ML_HEADS * ML_DV,
             ML_HEADS, ML_HEADS,
             SWA_HEADS * SWA_HEAD_DIM, SWA_KV_HEADS * SWA_HEAD_DIM, SWA_KV_HEADS * SWA_HEAD_DIM,
             X_HEADS * X_HEAD_DIM]
    pts = []
    acc = 0
    for s in sizes:
        acc += s
        pts.append(acc)
    return pts


def layer_norm(x, g, b):
    xf = x.astype(jnp.float32)
    mu = xf.mean(-1, keepdims=True)
    var = jnp.square(xf - mu).mean(-1, keepdims=True)
    return ((xf - mu) * lax.rsqrt(var + LN_EPS) * g.astype(jnp.float32) + b.astype(jnp.float32)).astype(x.dtype)


def partial_rope(x, positions):
    half = ROT_DIM // 2
    inv_freq = ROPE_THETA ** (-jnp.arange(half, dtype=jnp.float32) / half)
    ang = positions.astype(jnp.float32)[:, None] * inv_freq[None, :]
    cos = jnp.cos(ang)[None, :, None, :]
    sin = jnp.sin(ang)[None, :, None, :]
    x1 = x[..., :half].astype(jnp.float32)
    x2 = x[..., half:ROT_DIM].astype(jnp.float32)
    rot = jnp.concatenate([x1 * cos - x2 * sin, x2 * cos + x1 * sin], axis=-1).astype(x.dtype)
    return jnp.concatenate([rot, x[..., ROT_DIM:]], axis=-1)


def mlstm_chunk_step(carry, inp):
    c, n, m = carry
    q, k, v, li, lf = inp
    L = q.shape[2]
    b = jnp.cumsum(lf, axis=-1)
    causal = jnp.tril(jnp.ones((L, L), dtype=bool))
    d = jnp.where(causal, b[..., :, None] - b[..., None, :] + li[..., None, :], -jnp.inf)
    inter = b + m[..., None]
    m_t = jnp.maximum(inter, d.max(-1))
    w = jnp.exp(d - m_t[..., None])
    a = jnp.exp(inter - m_t)
    sw = jnp.einsum('bhtd,bhsd->bhts', q, k) * w
    num = jnp.einsum('bhts,bhsv->bhtv', sw, v) + a[..., None] * jnp.einsum('bhvd,bhtd->bhtv', c, q)
    den = sw.sum(-1) + a * jnp.einsum('bhd,bhtd->bht', n, q)
    h = num / jnp.maximum(jnp.abs(den), jnp.exp(-m_t))[..., None]
    b_last = b[..., -1]
    ws = b_last[..., None] - b + li
    m_new = jnp.maximum(b_last + m, ws.max(-1))
    ec = jnp.exp(b_last + m - m_new)
    es = jnp.exp(ws - m_new[..., None])
    c_new = ec[..., None, None] * c + jnp.einsum('bhs,bhsv,bhsd->bhvd', es, v, k)
    n_new = ec[..., None] * n + jnp.einsum('bhs,bhsd->bhd', es, k)
    return (c_new, n_new, m_new), h


def mlstm(q, k, v, li, lf, c0, n0, m0):
    B, T = q.shape[0], q.shape[1]
    L = min(T, ML_CHUNK)
    nc = T // L

    def chunks(t):
        t = jnp.moveaxis(t, 2, 1)
        t = t.reshape(t.shape[:2] + (nc, L) + t.shape[3:])
        return jnp.moveaxis(t, 2, 0)

    carry0 = (c0.astype(jnp.float32), n0.astype(jnp.float32), m0.astype(jnp.float32))
    xs = (chunks(q), chunks(k), chunks(v), chunks(li), chunks(lf))
    (c, n, m), h = lax.scan(mlstm_chunk_step, carry0, xs)
    h = jnp.moveaxis(h, 0, 2).reshape(B, ML_HEADS, T, ML_DV)
    return jnp.moveaxis(h, 1, 2), c, n, m


def sliding_window_attention(q, k, v, past_k, past_v, pos0, sinks):
    B, T = q.shape[0], q.shape[1]
    L = min(T, WINDOW)
    nb = T // L
    full_k = jnp.concatenate([past_k.astype(k.dtype), k], axis=1)
    full_v = jnp.concatenate([past_v.astype(v.dtype), v], axis=1)
    if nb == 1:
        ctx_k = full_k[:, None]
        ctx_v = full_v[:, None]
    else:
        fk = full_k.reshape(B, nb + 1, WINDOW, SWA_KV_HEADS, SWA_HEAD_DIM)
        fv = full_v.reshape(B, nb + 1, WINDOW, SWA_KV_HEADS, SWA_HEAD_DIM)
        ctx_k = jnp.concatenate([fk[:, :-1], fk[:, 1:]], axis=2)
        ctx_v = jnp.concatenate([fv[:, :-1], fv[:, 1:]], axis=2)
    qb = q.reshape(B, nb, L, SWA_KV_HEADS, SWA_GROUP, SWA_HEAD_DIM)
    starts = pos0 + L * jnp.arange(nb)
    qpos = starts[:, None] + jnp.arange(L)[None, :]
    kpos = starts[:, None] - WINDOW + jnp.arange(WINDOW + L)[None, :]
    kp = kpos[:, None, :]
    qp = qpos[:, :, None]
    allowed = (kp <= qp) & (kp >= qp - WINDOW) & (kp >= 0)
    s = jnp.einsum('bnqkgd,bnskd->bnkgqs', qb, ctx_k).astype(jnp.float32) * (SWA_HEAD_DIM ** -0.5)
    s = jnp.where(allowed[None, :, None, None], s, NEG_INF)
    sink = jnp.broadcast_to(sinks.astype(jnp.float32).reshape(1, 1, SWA_KV_HEADS, SWA_GROUP, 1, 1),
                            s.shape[:-1] + (1,))
    p = jax.nn.softmax(jnp.concatenate([s, sink], axis=-1), axis=-1)[..., :-1]
    o = jnp.einsum('bnkgqs,bnskd->bnqkgd', p.astype(ctx_v.dtype), ctx_v)
    return o.reshape(B, T, SWA_HEADS * SWA_HEAD_DIM), full_k[:, -WINDOW:], full_v[:, -WINDOW:]


def memory_cross_attention(q, mem_k, mem_v):
    s = jnp.einsum('bthd,bmhd->bhtm', q, mem_k.astype(q.dtype)).astype(jnp.float32) * (X_HEAD_DIM ** -0.5)
    p = jax.nn.softmax(s, axis=-1)
    o = jnp.einsum('bhtm,bmhd->bthd', p.astype(q.dtype), mem_v.astype(q.dtype))
    return o.reshape(q.shape[0], q.shape[1], X_HEADS * X_HEAD_DIM)


def trunk_layer(x, pos0, past_k, past_v, c0, n0, m0, mem_k, mem_v,
                w_in, b_gates, mlstm_norm_g, swa_sinks, w_branch, w_mix_out,
                ln1_g, ln1_b, w_ffn_in, w_ffn_out, ln2_g, ln2_b):
    B, T, _ = x.shape
    f32 = jnp.float32
    proj = jnp.einsum('btd,de->bte', x, w_in)
    mq, mk, mv, mo, mi, mf, sq, sk, sv, xq, gl = jnp.split(proj, _in_split_points(), axis=-1)

    q_a = mq.reshape(B, T, ML_HEADS, ML_DK).astype(f32)
    k_a = mk.reshape(B, T, ML_HEADS, ML_DK).astype(f32) * (ML_DK ** -0.5)
    v_a = mv.reshape(B, T, ML_HEADS, ML_DV).astype(f32)
    gate_pre = jnp.concatenate([mi, mf], axis=-1).astype(f32) + b_gates.astype(f32)
    li = gate_pre[..., :ML_HEADS]
    lf = jax.nn.log_sigmoid(gate_pre[..., ML_HEADS:])
    h_a, c_new, n_new, m_new = mlstm(q_a, k_a, v_a, li, lf, c0, n0, m0)
    mu = h_a.mean(-1, keepdims=True)
    var = jnp.square(h_a - mu).mean(-1, keepdims=True)
    h_a = ((h_a - mu) * lax.rsqrt(var + HEAD_NORM_EPS)).reshape(B, T, ML_HEADS * ML_DV) * mlstm_norm_g.astype(f32)
    y_a = (jax.nn.sigmoid(mo.astype(f32)) * h_a).astype(x.dtype)

    positions = pos0 + jnp.arange(T)
    q_b = partial_rope(sq.reshape(B, T, SWA_HEADS, SWA_HEAD_DIM), positions)
    k_b = partial_rope(sk.reshape(B, T, SWA_KV_HEADS, SWA_HEAD_DIM), positions)
    v_b = sv.reshape(B, T, SWA_KV_HEADS, SWA_HEAD_DIM)
    y_b, k_buf, v_buf = sliding_window_attention(q_b, k_b, v_b, past_k, past_v, pos0, swa_sinks)

    y_c = memory_cross_attention(xq.reshape(B, T, X_HEADS, X_HEAD_DIM), mem_k, mem_v)

    branches = jnp.stack([y_a, y_b.astype(x.dtype), y_c], axis=2)
    gates = jax.nn.sigmoid(gl.reshape(B, T, N_BRANCH, D_MODEL))
    widened = jnp.einsum('btrc,rcd->btrd', branches, w_branch)
    mixed = jnp.einsum('btd,de->bte', (gates * widened).sum(axis=2), w_mix_out)
    x = layer_norm(DEEPNORM_ALPHA * x + mixed, ln1_g, ln1_b)

    gu = jnp.einsum('btd,df->btf', x, w_ffn_in)
    g, u = jnp.split(gu, 2, axis=-1)
    ffn = jnp.einsum('btf,fd->btd', jax.nn.silu(g) * u, w_ffn_out)
    x = layer_norm(DEEPNORM_ALPHA * x + ffn, ln2_g, ln2_b)
    return x, k_buf, v_buf, c_new, n_new, m_new


def setup_inputs(seed: int = 0) -> dict:
    key = jax.random.key(seed)
    ks = jax.random.split(key, 24)
    f32 = jnp.float32

    def nrm(k, shape, scale=1.0):
        return scale * jax.random.normal(k, shape, f32)

    b_i = nrm(ks[12], (DEPTH, ML_HEADS), 0.1)
    b_f = 3.0 + 3.0 * jax.random.uniform(ks[13], (DEPTH, ML_HEADS), f32)
    return {
        "x_prompt": nrm(ks[0], (BATCH, SEQ, D_MODEL)),
        "x_sample": nrm(ks[1], (DEC_BATCH, DEC_SEQ, D_MODEL)),
        "mem_prompt": nrm(ks[2], (BATCH, MEM_TOKENS, D_MODEL)),
        "cache_swa_k": nrm(ks[3], (DEPTH, DEC_BATCH, WINDOW, SWA_KV_HEADS, SWA_HEAD_DIM)),
        "cache_swa_v": nrm(ks[4], (DEPTH, DEC_BATCH, WINDOW, SWA_KV_HEADS, SWA_HEAD_DIM)),
        "cache_mem_k": nrm(ks[5], (DEPTH, DEC_BATCH, MEM_TOKENS, X_HEADS, X_HEAD_DIM)),
        "cache_mem_v": nrm(ks[6], (DEPTH, DEC_BATCH, MEM_TOKENS, X_HEADS, X_HEAD_DIM)),
        "state_mlstm_c": nrm(ks[7], (DEPTH, DEC_BATCH, ML_HEADS, ML_DV, ML_DK), 0.1),
        "state_mlstm_n": nrm(ks[8], (DEPTH, DEC_BATCH, ML_HEADS, ML_DK), 0.5),
        "state_mlstm_m": nrm(ks[9], (DEPTH, DEC_BATCH, ML_HEADS)),
        "w_in": nrm(ks[10], (DEPTH, D_MODEL, D_IN), D_MODEL ** -0.5),
        "b_gates": jnp.concatenate([b_i, b_f], axis=-1),
        "mlstm_norm_g": 1.0 + nrm(ks[14], (DEPTH, ML_HEADS * ML_DV), 0.02),
        "swa_sinks": nrm(ks[15], (DEPTH, SWA_HEADS), 0.5),
        "w_mem_kv": nrm(ks[16], (DEPTH, D_MODEL, 2 * X_HEADS * X_HEAD_DIM), D_MODEL ** -0.5),
        "w_branch": nrm(ks[17], (DEPTH, N_BRANCH, BRANCH_WIDTH, D_MODEL), DEEPNORM_BETA * BRANCH_WIDTH ** -0.5),
        "w_mix_out": nrm(ks[18], (DEPTH, D_MODEL, D_MODEL), DEEPNORM_BETA * D_MODEL ** -0.5),
        "ln1_g": 1.0 + nrm(ks[19], (DEPTH, D_MODEL), 0.02),
        "ln1_b": nrm(ks[20], (DEPTH, D_MODEL), 0.02),
        "w_ffn_in": nrm(ks[21], (DEPTH, D_MODEL, 2 * D_FF), D_MODEL ** -0.5),
        "w_ffn_out": nrm(ks[22], (DEPTH, D_FF, D_MODEL), DEEPNORM_BETA * D_FF ** -0.5),
        "ln2_g": 1.0 + nrm(ks[23], (DEPTH, D_MODEL), 0.02),
        "ln2_b": nrm(ks[11], (DEPTH, D_MODEL), 0.02),
    }


def reference(x_prompt, x_sample, mem_prompt, cache_swa_k, cache_swa_v, cache_mem_k, cache_mem_v,
              state_mlstm_c, state_mlstm_n, state_mlstm_m, w_in, b_gates, mlstm_norm_g, swa_sinks,
              w_mem_kv, w_branch, w_mix_out, ln1_g, ln1_b, w_ffn_in, w_ffn_out, ln2_g, ln2_b):
    f32 = jnp.float32
    B = x_prompt.shape[0]
    yp = x_prompt
    ys = x_sample
    swa_k_p, swa_v_p, swa_k_s, swa_v_s = [], [], [], []
    mem_k_p, mem_v_p = [], []
    c_p, n_p, m_p, c_s, n_s, m_s = [], [], [], [], [], []
    for l in range(DEPTH):
        weights = (w_in[l], b_gates[l], mlstm_norm_g[l], swa_sinks[l], w_branch[l], w_mix_out[l],
                   ln1_g[l], ln1_b[l], w_ffn_in[l], w_ffn_out[l], ln2_g[l], ln2_b[l])
        mem_kv = jnp.einsum('bmd,de->bme', mem_prompt, w_mem_kv[l]).reshape(B, MEM_TOKENS, 2, X_HEADS, X_HEAD_DIM)
        mk = mem_kv[:, :, 0]
        mv = mem_kv[:, :, 1]
        zero_kv = jnp.zeros((B, WINDOW, SWA_KV_HEADS, SWA_HEAD_DIM), x_prompt.dtype)
        zero_c = jnp.zeros((B, ML_HEADS, ML_DV, ML_DK), f32)
        zero_n = jnp.zeros((B, ML_HEADS, ML_DK), f32)
        zero_m = jnp.zeros((B, ML_HEADS), f32)
        yp, kb, vb, cn, nn_, mn = trunk_layer(yp, 0, zero_kv, zero_kv, zero_c, zero_n, zero_m, mk, mv, *weights)
        swa_k_p.append(kb); swa_v_p.append(vb)
        mem_k_p.append(mk); mem_v_p.append(mv)
        c_p.append(cn); n_p.append(nn_); m_p.append(mn)
        ys, kb, vb, cn, nn_, mn = trunk_layer(ys, PAST_LEN, cache_swa_k[l], cache_swa_v[l],
                                             state_mlstm_c[l], state_mlstm_n[l], state_mlstm_m[l],
                                             cache_mem_k[l], cache_mem_v[l], *weights)
        swa_k_s.append(kb); swa_v_s.append(vb)
        c_s.append(cn); n_s.append(nn_); m_s.append(mn)
    return (yp, ys,
            jnp.stack(swa_k_p), jnp.stack(swa_v_p), jnp.stack(swa_k_s), jnp.stack(swa_v_s),
            jnp.stack(mem_k_p), jnp.stack(mem_v_p),
            jnp.stack(c_p), jnp.stack(n_p), jnp.stack(m_p),
            jnp.stack(c_s), jnp.stack(n_s), jnp.stack(m_s))
```

```python
import math
from contextlib import ExitStack

import numpy as np
import concourse.bass as bass
import concourse.mybir as mybir
from concourse.bass_utils import run_bass_kernel_spmd

F32 = mybir.dt.float32
BF16 = mybir.dt.bfloat16
AF = mybir.ActivationFunctionType
ALU = mybir.AluOpType
AX = mybir.AxisListType

D = 1024
SEQ = 4096
T = 512
DEPTH = 2
NS = 16
D_IN = 6408
D_FF = 2816
NFC = 22
PAST_LEN = 8192
ALPHA = (2 * DEPTH) ** 0.25
LN_EPS = 1e-5
HN_EPS = 1e-6
MAGIC = 12582912.0
TWO_PI = 2.0 * math.pi

ENG_NAMES = ["pe", "act", "dve", "pool", "sp"]

C_MQ, C_MK, C_MV, C_MO, C_G, C_SQ, C_SK, C_SV, C_XQ, C_GL = 0, 512, 1024, 1536, 2048, 2056, 2568, 2696, 2824, 3336


class Prog:
    def __init__(self, nc, n_dma_sems=32):
        self.nc = nc
        self.ops = {e: [] for e in ENG_NAMES}
        self.cnt = {e: 0 for e in ENG_NAMES}
        self.pending = {e: False for e in ENG_NAMES}
        self.sem = {}
        self.dma_sems = []
        self.dma_cnt = []
        self.n_dma_sems = n_dma_sems
        self.dma_rr = 0
        self.cast_rr = 0
        self.pool_rr = 0
        self.waited = {}
        self.last_w = {}
        self.readers = {}
        self.stopped = False

    def open(self, stack):
        for e in ENG_NAMES:
            self.sem[e] = stack.enter_context(self.nc.semaphore("s_" + e))
        for i in range(self.n_dma_sems):
            self.dma_sems.append(stack.enter_context(self.nc.semaphore(f"s_dma{i}")))
            self.dma_cnt.append(0)

    def _need(self, e, ev, waits):
        if ev is None:
            return
        semkey, val = ev
        if semkey == e and val > self.cnt[e]:
            assert e == "pe", (e, ev)
            return
        if self.waited.get((e, semkey), 0) >= val:
            return
        if val > waits.get(semkey, 0):
            waits[semkey] = val

    def _deps(self, e, reads, writes, acc):
        waits = {}
        for k in reads:
            self._need(e, self.last_w.get(k), waits)
            if isinstance(k, tuple) and k[0] == "bank":
                for ev in self.readers.get(k, []):
                    if ev[0] != e:
                        self._need(e, ev, waits)
        for k in writes:
            lw = self.last_w.get(k)
            if not (acc and lw is not None and lw[0] == e):
                self._need(e, lw, waits)
            for ev in self.readers.get(k, []):
                if ev[0] == e and e == "pe":
                    continue
                self._need(e, ev, waits)
        for semkey, val in waits.items():
            self.waited[(e, semkey)] = val
        return waits

    def _commit(self, ev, reads, writes):
        for k in reads:
            self.readers.setdefault(k, []).append(ev)
        for k in writes:
            self.last_w[k] = ev
            self.readers[k] = []

    def op(self, e, fn, reads=(), writes=(), acc=False, inc=True):
        if self.stopped:
            return None
        waits = self._deps(e, reads, writes, acc)
        if inc:
            self.cnt[e] += 1
            ev = (e, self.cnt[e])
            self.pending[e] = False
        else:
            ev = (e, self.cnt[e] + 1)
            self.pending[e] = True
        self.ops[e].append((fn, waits, ("eng", e) if inc else None))
        self._commit(ev, reads, writes)
        return ev

    def dma(self, q, fn, reads=(), writes=(), cast=False):
        if self.stopped:
            return None
        if cast:
            i = self.cast_rr
            self.cast_rr = (self.cast_rr + 1) % 4
        elif q == "pool":
            i = 4 + self.pool_rr
            self.pool_rr = (self.pool_rr + 1) % 4
        else:
            i = 8 + self.dma_rr
            self.dma_rr = (self.dma_rr + 1) % (self.n_dma_sems - 8)
        semkey = ("dma", i)
        waits = self._deps(q, reads, writes, False)
        if self.dma_cnt[i] > 0:
            self._need(q, (semkey, self.dma_cnt[i]), waits)
            self.waited[(q, semkey)] = max(self.waited.get((q, semkey), 0), self.dma_cnt[i])
        self.dma_cnt[i] += 16
        ev = (semkey, self.dma_cnt[i])
        self.ops[q].append((fn, waits, ("dma", i)))
        self._commit(ev, reads, writes)
        return ev

    def _semh(self, semkey):
        if isinstance(semkey, tuple):
            return self.dma_sems[semkey[1]]
        return self.sem[semkey]

    def barrier(self):
        if self.stopped:
            return
        for e in ENG_NAMES:
            assert not self.pending[e], e
        for e in ENG_NAMES:
            waits = {}
            for i in range(self.n_dma_sems):
                if self.dma_cnt[i] > 0:
                    self._need(e, (("dma", i), self.dma_cnt[i]), waits)
            for en in ENG_NAMES:
                if en != e and self.cnt[en] > 0:
                    self._need(e, (en, self.cnt[en]), waits)
            for semkey, val in waits.items():
                self.waited[(e, semkey)] = val
            self.ops[e].append((None, waits, None))

    def final_wait(self, e="sp"):
        waits = {}
        for i in range(self.n_dma_sems):
            if self.dma_cnt[i] > 0:
                waits[("dma", i)] = self.dma_cnt[i]
        for en in ENG_NAMES:
            assert not self.pending[en], en
            if en != e and self.cnt[en] > 0:
                waits[en] = self.cnt[en]
        self.ops[e].append((None, waits, None))

    def emit(self, block):
        prog = self

        def body(ename):
            def f(engine):
                for fn, waits, kind in prog.ops[ename]:
                    for semkey, val in waits.items():
                        engine.wait_ge(prog._semh(semkey), val)
                    if fn is None:
                        continue
                    inst = fn(engine)
                    if kind is None:
                        continue
                    if kind[0] == "eng":
                        inst.then_inc(prog.sem[ename], 1)
                    else:
                        inst.then_inc(prog.dma_sems[kind[1]], 16)
            return f

        block.tensor(body("pe"))
        block.scalar(body("act"))
        block.vector(body("dve"))
        block.gpsimd(body("pool"))
        block.sync(body("sp"))


CF_IDENT, CF_ONES, CF_TRI, CF_RR, CF_MLE, CF_MGE, CF_IOTA, CF_INVF = 0, 128, 256, 384, 512, 640, 768, 1280
CF_W = 1281


def make_consts():
    c = np.zeros((128, CF_W), np.float32)
    i = np.arange(128)
    c[:, CF_IDENT:CF_IDENT + 128] = np.eye(128, dtype=np.float32)
    c[:, CF_ONES:CF_ONES + 128] = 1.0
    c[:, CF_TRI:CF_TRI + 128] = (i[:, None] <= i[None, :]).astype(np.float32)
    rr = np.zeros((128, 128), np.float32)
    for base in (0, 64):
        for j in range(8):
            rr[base + j + 8, base + j] = -1.0
            rr[base + j, base + j + 8] = 1.0
    c[:, CF_RR:CF_RR + 128] = rr
    c[:, CF_MLE:CF_MLE + 128] = np.where(i[:, None] <= i[None, :], 0.0, -30000.0)
    c[:, CF_MGE:CF_MGE + 128] = np.where(i[:, None] >= i[None, :], 0.0, -30000.0)
    c[:, CF_IOTA:CF_IOTA + 512] = np.arange(512, dtype=np.float32)[None, :]
    invf = np.zeros(128, np.float32)
    for base in (0, 64):
        for j in range(8):
            f = np.float32(500000.0) ** (-np.float32(j) / np.float32(8))
            invf[base + j] = f
            invf[base + j + 8] = f
    c[:, CF_INVF] = invf
    return c


class _Stop(Exception):
    pass


def build_program(NT=8, WITH_SAMPLE=True, NRING=4, STOP=None):
    nc = bass.Bass("TRN2", target_bir_lowering=False)

    stage_ref = []

    def stage(n):
        if STOP is not None and n >= STOP:
            stage_ref[0].stopped = True

    def din(name, shape):
        return nc.dram_tensor(name, list(shape), F32, kind="ExternalInput").ap()

    def dout(name, shape):
        return nc.dram_tensor(name, list(shape), F32, kind="ExternalOutput").ap()

    def dscr(name, shape, dt=BF16):
        return nc.dram_tensor(name, list(shape), dt, kind="Internal").ap()

    xp = din("xp", [SEQ, D])
    xs = din("xs", [NS, D])
    memp = din("memp", [256, D])
    csk = din("csk", [DEPTH, NS, 128, 128])
    csv = din("csv", [DEPTH, NS, 128, 128])
    cmk = din("cmk", [DEPTH, NS, 256, 512])
    cmv = din("cmv", [DEPTH, NS, 256, 512])
    stc = din("stc", [DEPTH, NS, 4, 128, 128])
    stn = din("stn", [DEPTH, NS, 4, 128])
    stm = din("stm", [DEPTH, NS, 4])
    w_in = din("w_in", [DEPTH, D, D_IN])
    b_gates = din("b_gates", [DEPTH, 8])
    normg_d = din("mlstm_norm_g", [DEPTH, 512])
    sinks_d = din("swa_sinks", [DEPTH, 8])
    w_mem = din("w_mem_kv", [DEPTH, D, 1024])
    w_br = din("w_branch", [DEPTH, 3, 512, D])
    w_mix = din("w_mix_out", [DEPTH, D, D])
    ln1g = din("ln1_g", [DEPTH, D])
    ln1b = din("ln1_b", [DEPTH, D])
    w_fi = din("w_ffn_in", [DEPTH, D, 2 * D_FF])
    w_fo = din("w_ffn_out", [DEPTH, D_FF, D])
    ln2g = din("ln2_g", [DEPTH, D])
    ln2b = din("ln2_b", [DEPTH, D])
    cf_d = din("cf_in", [128, CF_W])

    yp = dout("yp", [SEQ, D])
    ys = dout("ys", [NS, D])
    o_skp = dout("o_skp", [DEPTH, 128, 128])
    o_svp = dout("o_svp", [DEPTH, 128, 128])
    o_sks = dout("o_sks", [DEPTH, NS, 128, 128])
    o_svs = dout("o_svs", [DEPTH, NS, 128, 128])
    o_mkp = dout("o_mkp", [DEPTH, 256, 512])
    o_mvp = dout("o_mvp", [DEPTH, 256, 512])
    o_cp = dout("o_cp", [DEPTH, 4, 128, 128])
    o_np = dout("o_np", [DEPTH, 4, 128])
    o_mp = dout("o_mp", [DEPTH, 4])
    o_cs = dout("o_cs", [DEPTH, NS, 4, 128, 128])
    o_ns = dout("o_ns", [DEPTH, NS, 4, 128])
    o_ms = dout("o_ms", [DEPTH, NS, 4])

    wb_in = dscr("wb_in", [DEPTH, 128, 8, D_IN])
    wb_squ = dscr("wb_squ", [DEPTH, 128, 8, 512])
    wb_mem = dscr("wb_mem", [DEPTH, 128, 8, 1024])
    wb_br = dscr("wb_br", [DEPTH, 3, 128, 4, D])
    wb_mix = dscr("wb_mix", [DEPTH, 128, 8, D])
    wb_fi = dscr("wb_fi", [DEPTH, 128, 8, 2 * D_FF])
    wb_fo = dscr("wb_fo", [DEPTH, 128, NFC, D])

    st = ExitStack()
    with st:
        P = Prog(nc)
        P.open(st)
        stage_ref.append(P)

        def sb(name, shape, dt=F32):
            return st.enter_context(nc.sbuf_tensor(name, list(shape), dt))

        cf = sb("cf", [128, CF_W])
        identb = sb("identb", [128, 128], BF16)
        onesb = sb("onesb", [128, 128], BF16)
        invdb = sb("invdb", [128, 128], BF16)
        trib = sb("trib", [128, 128], BF16)
        mneg = sb("mneg", [128, 2, 4, 128], BF16)
        onespad = sb("onespad", [128, 2, 128], BF16)
        prow = sb("prow", [72, 128])
        pcol = sb("pcol", [128, 72])
        bgate = sb("bgate", [128, DEPTH, 8])
        sinkexp = sb("sinkexp", [128, DEPTH, 4])
        wg = sb("wg", [128, DEPTH, 8, 8], BF16)
        memT = sb("memT", [128, 8, 256], BF16)
        memKT = sb("memKT", [128, DEPTH, 4, 256], BF16)
        memV = sb("memV", [128, DEPTH, 2, 512], BF16)
        CT = sb("CT", [128, DEPTH, 4, 129])
        CTs = sb("CTs", [128, 4, 129])
        CTb = sb("CTb", [128, 4, 129], BF16)
        mst = sb("mst", [128, DEPTH, 4])
        skT = sb("skT", [128, DEPTH, 5, 128], BF16)
        svp = sb("svp", [128, DEPTH, 5, 2, 128], BF16)
        xres = sb("xres", [128, 8, T])
        xb = sb("xb", [128, 8, T], BF16)
        fp = sb("fp", [128, 8, T])
        bp = sb("bp", [128, 24, T], BF16)
        bq = sb("bq", [128, 16, T], BF16)
        yabc = sb("yabc", [128, 3, 4, T], BF16)
        vaug = sb("vaug", [128, 4, 4, 129], BF16)
        smt = sb("smt", [128, 2, 4, 128], BF16)
        gsm = sb("gsm", [128, 16, 16])
        hh = sb("hh", [128, 2, 4, 128])
        bnst = sb("bnst", [128, 4, 6])
        bnag = sb("bnag", [128, 4, 2])
        tmpf = sb("tmpf", [128, 2, T])
        ring = sb("ring", [128, NRING, 4096], BF16)
        rtab = sb("rtab", [128, 2, T])
        onespadf = sb("onespadf", [128, 2, 128])
        bdf = sb("bdf", [128, 128])
        sx = sb("sx", [16, 48])
        sD = sb("sD", [16, 16, 12])
        sR = sb("sR", [128, 16, 12])
        sS = sb("sS", [128, 24, 16, 4])
        sP = sb("sP", [128, 2, 2, 8])
        sT = sb("sT", [64, 4, 128])
        sB = sb("sB", [128, 4, 16, 4], BF16)
        sPb = sb("sPb", [128, 2, 2, 8], BF16)

        banks = [st.enter_context(nc.psum_tensor(f"bank{i}", [128, 512], F32)) for i in range(8)]
        block = st.enter_context(nc.Block())

        bank_rr = [0]
        reserved = set()

        def nb():
            while True:
                i = bank_rr[0]
                bank_rr[0] = (i + 1) % 8
                if i not in reserved:
                    return i

        def BK(i):
            return ("bank", i)

        def V(fn, reads=(), writes=(), inc=True):
            return P.op("dve", fn, reads, writes, inc=inc)

        def A(fn, reads=(), writes=(), inc=True):
            return P.op("act", fn, reads, writes, inc=inc)

        def G(fn, reads=(), writes=(), inc=True):
            return P.op("pool", fn, reads, writes, inc=inc)

        def MM(out, lhsT, rhs, start, stop, reads, bank, inc=None):
            if inc is None:
                inc = stop
            return P.op("pe", lambda t: t.matmul(out, lhsT=lhsT, rhs=rhs, start=start, stop=stop),
                        reads=reads, writes=[BK(bank)], acc=not start, inc=inc)

        def TR(out, in_, ident, reads, bank, inc=True, first=True):
            return P.op("pe", lambda t: t.transpose(out=out, in_=in_, identity=ident),
                        reads=reads, writes=[BK(bank)], acc=not first, inc=inc)

        evac_rr = [0]

        def evac_copy(out, in_, reads, writes):
            evac_rr[0] ^= 1
            if evac_rr[0]:
                V(lambda v: v.tensor_copy(out=out, in_=in_), reads, writes)
            else:
                A(lambda a: a.copy(out=out, in_=in_), reads, writes)

        ring_rr = [0]

        cast_reg = {}
        cast_pending = []
        cast_tickc = [0]

        def reg_cast(name, c0, c1, key):
            cast_reg.setdefault(name, []).append((c0, c1, key))

        def cast_covered(name, c0, c1):
            iv = sorted((a, b) for (a, b, k) in cast_reg.get(name, []) if a < c1 and b > c0)
            pos = c0
            for a, b in iv:
                if a > pos:
                    return False
                pos = max(pos, b)
            return pos >= c1

        def cast_keys(name, c0, c1):
            while not cast_covered(name, c0, c1):
                assert cast_pending, (name, c0, c1)
                cast_pending.pop(0)()
            return [k for (a, b, k) in cast_reg[name] if a < c1 and b > c0]

        def cast_tick(every=3):
            if cast_pending:
                cast_tickc[0] += 1
                if cast_tickc[0] % every == 0:
                    cast_pending.pop(0)()

        def wload(src_ap, shape, srckeys):
            s = ring_rr[0]
            ring_rr[0] = (s + 1) % NRING
            a, b = shape
            dst = ring[:, s, 0:a * b].rearrange("p (a b) -> p a b", a=a)
            P.dma("sp", lambda q: q.dma_start(out=dst, in_=src_ap), reads=list(srckeys), writes=[("ring", s)])
            return dst, ("ring", s)

        P.dma("sp", lambda q: q.dma_start(out=cf[:], in_=cf_d[:, :]), writes=["cf"])
        identf = cf[:, CF_IDENT:CF_IDENT + 128]
        onesf = cf[:, CF_ONES:CF_ONES + 128]
        trif = cf[:, CF_TRI:CF_TRI + 128]
        rrf = cf[:, CF_RR:CF_RR + 128]
        iotaf = cf[:, CF_IOTA:CF_IOTA + 512]
        invf = cf[:, CF_INVF:CF_INVF + 1]
        V(lambda v: v.tensor_copy(out=identb[:], in_=identf), ["cf"], ["identb"])
        V(lambda v: v.tensor_copy(out=onesb[:], in_=onesf), ["cf"], ["onesb"])
        V(lambda v: v.tensor_scalar(out=invdb[:], in0=onesf, scalar1=1.0 / D, scalar2=None, op0=ALU.mult), ["cf"], ["invdb"])
        V(lambda v: v.tensor_copy(out=trib[:], in_=trif), ["cf"], ["trib"])
        for k, off in enumerate((CF_MLE, CF_MGE)):
            V(lambda v, k=k, off=off: v.tensor_copy(
                out=mneg[:, k], in_=cf[:, off:off + 128].unsqueeze(1).to_broadcast([128, 4, 128])),
              ["cf"], ["mneg"])
        G(lambda g: g.memset(onespad[:], 0.0), [], ["onespad"])
        G(lambda g: g.memset(onespad[:, 0, 0:64], 1.0), ["onespad"], ["onespad"])
        G(lambda g: g.memset(onespad[:, 1, 64:128], 1.0), ["onespad"], ["onespad"])
        G(lambda g: g.memset(onespadf[:], 0.0), [], ["onespadf"])
        G(lambda g: g.memset(onespadf[:, 0, 0:64], 1.0), ["onespadf"], ["onespadf"])
        G(lambda g: g.memset(onespadf[:, 1, 64:128], 1.0), ["onespadf"], ["onespadf"])
        G(lambda g: g.memset(bdf[:], 0.0), [], ["bdf"])
        G(lambda g: g.memset(bdf[0:64, 0:64], 1.0), ["bdf"], ["bdf"])
        G(lambda g: g.memset(bdf[64:128, 64:128], 1.0), ["bdf"], ["bdf"])
        G(lambda g: g.memset(CT[:], 0.0), [], ["CT"])
        G(lambda g: g.memset(mst[:], 0.0), [], ["mst"])
        G(lambda g: g.memset(skT[:], 0.0), [], ["skT"])
        G(lambda g: g.memset(svp[:], 0.0), [], ["svp"])

        for l in range(DEPTH):
            for k, src in enumerate((ln1g, ln1b, ln2g, ln2b)):
                r0 = (l * 4 + k) * 8
                P.dma("sp", lambda q, src=src, l=l, r0=r0: q.dma_start(
                    out=prow[r0:r0 + 8, :], in_=src[l, :].rearrange("(c p) -> c p", p=128)), writes=["prow"])
            P.dma("sp", lambda q, l=l: q.dma_start(
                out=prow[64 + l * 4:64 + l * 4 + 4, :], in_=normg_d[l, :].rearrange("(h p) -> h p", p=128)),
                writes=["prow"])
            P.dma("sp", lambda q, l=l: q.dma_start(out=bgate[:, l, :], in_=b_gates[l:l + 1, :].to_broadcast([128, 8])),
                  writes=["bgate"])
            for kv in range(2):
                P.dma("sp", lambda q, l=l, kv=kv: q.dma_start(
                    out=sinkexp[kv * 64:(kv + 1) * 64, l, :],
                    in_=sinks_d[l:l + 1, kv * 4:(kv + 1) * 4].to_broadcast([64, 4])), writes=["sinkexp"])
            P.dma("pool", lambda q, l=l: q.dma_start(
                out=wg[:, l], in_=w_in[l, :, C_G:C_G + 8].rearrange("(c p) j -> p c j", p=128)), writes=["wg"])
        A(lambda a: a.activation(out=sinkexp[:], in_=sinkexp[:], func=AF.Exp), ["sinkexp"], ["sinkexp"])
        b0 = nb()
        TR(banks[b0][:, 0:72], prow[:, :], identf[0:72, 0:72], ["prow", "cf"], b0)
        V(lambda v: v.tensor_copy(out=pcol[:], in_=banks[b0][:, 0:72]), [BK(b0)], ["pcol"])

        def lnp(l, k, c):
            i = (l * 4 + k) * 8 + c
            return pcol[:, i:i + 1]

        def normg(l, h):
            i = 64 + l * 4 + h
            return pcol[:, i:i + 1]

        stage(1)
        def cast_cols(name, dst, src, c0, c1, eager=False):
            def emit():
                key = ("cast", name, c0)
                P.dma("pool", lambda q: q.dma_start(
                    out=dst[:, :, c0:c1], in_=src[:, c0:c1].rearrange("(c p) j -> p c j", p=128)), writes=[key], cast=True)
                reg_cast(name, c0, c1, key)
            if eager:
                emit()
            else:
                cast_pending.append(emit)

        for l in range(DEPTH):
            for c0 in range(0, 1024, 512):
                cast_cols(("mem", l), wb_mem[l], w_mem[l], c0, c0 + 512, eager=True)
        for l in range(DEPTH):
            P.dma("pool", lambda q, l=l: q.dma_start(
                out=wb_squ[l], in_=w_in[l, :, C_SQ:C_SQ + 512].rearrange("(c p) j -> p c j", p=128)),
                writes=[("cast", "squ", l)], cast=True)
        for l in range(DEPTH):
            s1 = bq[:, 0:8, :]
            s2 = bq[:, 8:16, :]
            k1 = [("bq", c) for c in range(8)]
            k2 = [("bq", 8 + c) for c in range(8)]
            P.dma("pool", lambda q, l=l, s1=s1: q.dma_start(out=s1, in_=wb_squ[l]), reads=[("cast", "squ", l)], writes=k1)
            for kv in range(2):
                G(lambda g, s1=s1, s2=s2, kv=kv: g.tensor_copy(
                    out=s2.rearrange("p c (g k e) -> p c g k e", g=4, k=2)[:, :, :, kv, :],
                    in_=s1.rearrange("p c (k g e) -> p c k g e", k=2, g=4)[:, :, kv]), k1, k2)
            key = ("cast", ("in", l), C_SQ, "perm")
            P.dma("pool", lambda q, l=l, s2=s2: q.dma_start(out=wb_in[l][:, :, C_SQ:C_SQ + 512], in_=s2), reads=k2, writes=[key])
            reg_cast(("in", l), C_SQ, C_SQ + 512, key)
        for l in range(DEPTH):
            for (c0, c1) in ((0, 512), (512, 1024), (1024, 1536), (1536, 2048)):
                cast_cols(("in", l), wb_in[l], w_in[l], c0, c1)
            cast_cols(("in", l), wb_in[l], w_in[l], C_SK, C_SK + 256)
            cast_cols(("in", l), wb_in[l], w_in[l], C_XQ, C_XQ + 512)
            for a0 in range(C_GL, D_IN, 512):
                cast_cols(("in", l), wb_in[l], w_in[l], a0, a0 + 512)
            for r in range(3):
                for hseg in range(2):
                    if r != 1:
                        cast_cols(("br", l, r), wb_br[l, r], w_br[l, r], hseg * 512, (hseg + 1) * 512)
                    else:
                        def emit_b1(l=l, hseg=hseg):
                            for kv in range(2):
                                key = ("cast", ("br", l, 1), hseg, kv)
                                P.dma("pool", lambda q, kv=kv: q.dma_start(
                                    out=wb_br[l, 1][kv * 64:(kv + 1) * 64, :, hseg * 512:(hseg + 1) * 512],
                                    in_=w_br[l, 1, kv * 256:(kv + 1) * 256, hseg * 512:(hseg + 1) * 512].rearrange(
                                        "(g e) j -> e g j", g=4)), writes=[key], cast=True)
                                reg_cast(("br", l, 1), hseg * 512 + kv, (hseg + 1) * 512 if kv == 1 else hseg * 512 + 1, key)
                        cast_pending.append(emit_b1)
            for c0 in range(0, D, 512):
                cast_cols(("mix", l), wb_mix[l], w_mix[l], c0, c0 + 512)
            for c0 in range(0, 2 * D_FF, 256):
                cast_cols(("fi", l), wb_fi[l], w_fi[l], c0, c0 + 256)
            for c0 in range(0, D, 256):
                for kh in range(2):
                    def emit_fo(l=l, c0=c0, kh=kh):
                        key = ("cast", ("fo", l, kh), c0)
                        P.dma("pool", lambda q: q.dma_start(
                            out=wb_fo[l][:, kh * 11:(kh + 1) * 11, c0:c0 + 256],
                            in_=w_fo[l, kh * 11 * 128:(kh + 1) * 11 * 128, c0:c0 + 256].rearrange("(c p) j -> p c j", p=128)),
                            writes=[key], cast=True)
                        reg_cast(("fo", l, kh), c0, c0 + 256, key)
                    cast_pending.append(emit_fo)

        for _ in range(7):
            cast_pending.pop(0)()
        stage(2)
        for mb in range(2):
            P.dma("sp", lambda q, mb=mb: q.dma_start(out=fp[:, 2 * mb:2 * mb + 2, :].rearrange("p a b -> p (a b)"),
                                                     in_=memp[mb * 128:(mb + 1) * 128, :]),
                  writes=[("fp", 2 * mb), ("fp", 2 * mb + 1)])
        for c in range(8):
            b = nb()
            for mb in range(2):
                src = fp[:, 2 * mb:2 * mb + 2, :].rearrange("p a b -> p (a b)")[:, c * 128:(c + 1) * 128]
                TR(banks[b][:, mb * 128:(mb + 1) * 128], src, identf, [("fp", 2 * mb), ("fp", 2 * mb + 1), "cf"], b,
                   inc=(mb == 1), first=(mb == 0))
            evac_copy(memT[:, c, :], banks[b][:, 0:256], [BK(b)], [("memT", c)])
        for l in range(DEPTH):
            for half in range(2):
                wsl, wkey = wload(wb_mem[l][:, :, half * 512:(half + 1) * 512], (8, 512), cast_keys(("mem", l), half * 512, (half + 1) * 512))
                if half == 0:
                    for h in range(4):
                        b = nb()
                        for c in range(8):
                            MM(banks[b][:, 0:256], wsl[:, c, h * 128:(h + 1) * 128], memT[:, c, :], c == 0, c == 7,
                               [wkey, ("memT", c)], b)
                        evac_copy(memKT[:, l, h, :], banks[b][:, 0:256], [BK(b)], [("memKT", l)])
                for mb in range(2):
                    b = nb()
                    for c in range(8):
                        MM(banks[b][:, :], memT[:, c, mb * 128:(mb + 1) * 128], wsl[:, c, :], c == 0, c == 7,
                           [wkey, ("memT", c)], b)
                    stg = tmpf[:, mb, :]
                    V(lambda v, stg=stg, b=b: v.tensor_copy(out=stg, in_=banks[b][:, :]), [BK(b)], [("tmpf", mb)])
                    odst = (o_mkp if half == 0 else o_mvp)[l, mb * 128:(mb + 1) * 128, :]
                    P.dma("sp", lambda q, odst=odst, stg=stg: q.dma_start(out=odst, in_=stg), reads=[("tmpf", mb)])
                    if half == 1:
                        A(lambda a, l=l, mb=mb, b=b: a.copy(out=memV[:, l, mb, :], in_=banks[b][:, :]), [BK(b)],
                          [("memV", l)])

        stage(3)
        def fm_group(wsl, wkey, ncks, col0, nchunks, src, srckeys, N, evac, lsel=None, after=None):
            for j in range(nchunks):
                b = nb()
                for c in range(ncks):
                    lt = wsl[:, c, col0 + j * 128:col0 + (j + 1) * 128] if lsel is None else lsel(wsl, c, j)
                    MM(banks[b][:, 0:N], lt, src(c), c == 0, c == ncks - 1, [wkey, srckeys(c)], b)
                evac(j, b, banks[b][:, 0:N])
                cast_tick()
                if after is not None:
                    after(j)

        def sq_sel(wsl, c, g):
            return wsl[:, c, :].rearrange("p (k g e) -> p g k e", k=2, g=4)[:, g]

        def xb_src(N):
            return (lambda c: xb[:, c, 0:N]), (lambda c: ("xb", c))

        def rope_tables(base, N, const_pos=False):
            ang = tmpf[:, 0, 0:N]
            kk = tmpf[:, 1, 0:N]
            cosb = rtab[:, 0, 0:N]
            sinb = rtab[:, 1, 0:N]
            if const_pos:
                V(lambda v: v.tensor_scalar(out=ang, in0=iotaf[:, 0:N], scalar1=0.0, scalar2=float(base),
                                            op0=ALU.mult, op1=ALU.add), ["cf"], [("tmpf", 0)])
                V(lambda v: v.tensor_scalar(out=ang, in0=ang, scalar1=invf, scalar2=None, op0=ALU.mult),
                  [("tmpf", 0), "cf"], [("tmpf", 0)])
            else:
                V(lambda v: v.tensor_scalar(out=ang, in0=iotaf[:, 0:N], scalar1=float(base), scalar2=invf,
                                            op0=ALU.add, op1=ALU.mult), ["cf"], [("tmpf", 0)])
            V(lambda v: v.tensor_scalar(out=kk, in0=ang, scalar1=1.0 / TWO_PI, scalar2=MAGIC,
                                        op0=ALU.mult, op1=ALU.add), [("tmpf", 0)], [("tmpf", 1)])
            V(lambda v: v.tensor_scalar(out=kk, in0=kk, scalar1=-MAGIC, scalar2=None, op0=ALU.add),
              [("tmpf", 1)], [("tmpf", 1)])
            c1 = 6.28125
            c2 = float(np.float32(TWO_PI - 6.28125))
            c3 = float(TWO_PI - 6.28125 - c2)
            for cc in (c1, c2, c3):
                V(lambda v, cc=cc: v.scalar_tensor_tensor(out=ang, in0=kk, scalar=-cc, in1=ang, op0=ALU.mult, op1=ALU.add),
                  [("tmpf", 1), ("tmpf", 0)], [("tmpf", 0)])
            V(lambda v: v.tensor_scalar(out=sinb, in0=ang, scalar1=3.1415925, scalar2=-3.1415925, op0=ALU.min, op1=ALU.max),
              [("tmpf", 0)], [("rtab", 1)])
            A(lambda a: a.activation(out=sinb, in_=sinb, func=AF.Sin), [("rtab", 1)], [("rtab", 1)])
            V(lambda v: v.tensor_scalar(out=ang, in0=ang, scalar1=math.pi / 2, scalar2=None, op0=ALU.add),
              [("tmpf", 0)], [("tmpf", 0)])
            V(lambda v: v.tensor_single_scalar(out=kk, in_=ang, scalar=math.pi, op=ALU.is_gt), [("tmpf", 0)], [("tmpf", 1)])
            V(lambda v: v.scalar_tensor_tensor(out=ang, in0=kk, scalar=-TWO_PI, in1=ang, op0=ALU.mult, op1=ALU.add),
              [("tmpf", 1), ("tmpf", 0)], [("tmpf", 0)])
            V(lambda v: v.tensor_scalar(out=cosb, in0=ang, scalar1=3.1415925, scalar2=-3.1415925, op0=ALU.min, op1=ALU.max),
              [("tmpf", 0)], [("rtab", 0)])
            A(lambda a: a.activation(out=cosb, in_=cosb, func=AF.Sin), [("rtab", 0)], [("rtab", 0)])

        def rope_apply(psum_ap, b, N, q32k, out_ap, out_keys, out32=None):
            q32 = fp[:, q32k, 0:N]
            evac_copy(q32, psum_ap, [BK(b)], [("fp", q32k)])
            b2 = nb()
            P.op("pe", lambda t: t.matmul(banks[b2][:, 0:N], lhsT=rrf, rhs=q32, start=True, stop=True),
                 reads=["cf", ("fp", q32k)], writes=[BK(b2)])
            rs = tmpf[:, 1, 0:N]
            V(lambda v: v.tensor_tensor(out=rs, in0=banks[b2][:, 0:N], in1=rtab[:, 1, 0:N], op=ALU.mult),
              [BK(b2), ("rtab", 1)], [("tmpf", 1)])
            G(lambda g: g.tensor_tensor(out=q32, in0=q32, in1=rtab[:, 0, 0:N], op=ALU.mult),
              [("fp", q32k), ("rtab", 0)], [("fp", q32k)])
            if out32 is not None:
                o32, o32k = out32
                G(lambda g: g.tensor_tensor(out=o32, in0=q32, in1=rs, op=ALU.add), [("fp", q32k), ("tmpf", 1)], [o32k])
                G(lambda g: g.tensor_copy(out=out_ap, in_=o32), [o32k], out_keys)
            else:
                G(lambda g: g.tensor_tensor(out=out_ap, in0=q32, in1=rs, op=ALU.add), [("fp", q32k), ("tmpf", 1)], out_keys)

        class LNStats:
            def __init__(self, N):
                self.N = N
                self.b1 = nb()
                reserved.add(self.b1)
                self.b2 = nb()
                reserved.add(self.b2)

            def feed(self, c):
                N = self.N
                vb = bq[:, c % 2, 0:N]
                sqb = bq[:, 2 + c % 2, 0:N]
                A(lambda a: a.copy(out=vb, in_=xres[:, c, 0:N]), [("xres", c)], [("bq", c % 2)])
                A(lambda a: a.activation(out=sqb, in_=xres[:, c, 0:N], func=AF.Square), [("xres", c)], [("bq", 2 + c % 2)])
                MM(banks[self.b1][:, 0:N], invdb[:, :], vb, c == 0, c == 7, ["invdb", ("bq", c % 2)], self.b1, inc=(c == 7))
                MM(banks[self.b2][:, 0:N], invdb[:, :], sqb, c == 0, c == 7, ["invdb", ("bq", 2 + c % 2)], self.b2, inc=(c == 7))

        def layer_norm(l, which, N, last, stats):
            kg, kb_ = (0, 1) if which == 1 else (2, 3)
            b1, b2 = stats.b1, stats.b2
            mean = fp[:, 3, 0:N]
            var = fp[:, 4, 0:N]
            rstd = fp[:, 5, 0:N]
            nmr = fp[:, 6, 0:N]
            A(lambda a: a.activation(out=mean, in_=banks[b1][:, 0:N], func=AF.Square), [BK(b1)], [("fp", 3)])
            V(lambda v: v.tensor_tensor(out=var, in0=banks[b2][:, 0:N], in1=mean, op=ALU.subtract), [BK(b2), ("fp", 3)], [("fp", 4)])
            A(lambda a: a.activation(out=var, in_=var, func=AF.Ln, bias=LN_EPS), [("fp", 4)], [("fp", 4)])
            A(lambda a: a.activation(out=rstd, in_=var, func=AF.Exp, scale=-0.5), [("fp", 4)], [("fp", 5)])
            V(lambda v: v.scalar_tensor_tensor(out=nmr, in0=banks[b1][:, 0:N], scalar=-1.0, in1=rstd, op0=ALU.mult, op1=ALU.mult),
              [BK(b1), ("fp", 5)], [("fp", 6)])
            reserved.discard(b1)
            reserved.discard(b2)
            upair = [fp[:, 0:2, 0:N], tmpf[:, :, 0:N]]
            ukeys = [[("fp", 0), ("fp", 1)], [("tmpf", 0), ("tmpf", 1)]]
            for pr in range(4):
                c0 = 2 * pr
                u2 = upair[pr % 2]
                uk = ukeys[pr % 2]
                E1, E2 = (V, G) if pr % 2 == 0 else (G, V)
                xk = [("xres", c0), ("xres", c0 + 1)]
                E1(lambda e, c0=c0, u2=u2: e.tensor_tensor(out=u2, in0=xres[:, c0:c0 + 2, 0:N],
                                                           in1=rstd.unsqueeze(1).to_broadcast([128, 2, N]), op=ALU.mult),
                   xk + [("fp", 5)], uk)
                E1(lambda e, u2=u2: e.tensor_tensor(out=u2, in0=u2, in1=nmr.unsqueeze(1).to_broadcast([128, 2, N]), op=ALU.add),
                   uk + [("fp", 6)], uk)
                for k in range(2):
                    c = c0 + k
                    u = u2[:, k, :]
                    if last:
                        A(lambda a, c=c, u=u: a.activation(out=xres[:, c, 0:N], in_=u, func=AF.Identity,
                                                           scale=lnp(l, kg, c), bias=lnp(l, kb_, c)),
                          [uk[k], "pcol"], [("xres", c)])
                    else:
                        A(lambda a, c=c, u=u: a.activation(out=xb[:, c, 0:N], in_=u, func=AF.Identity,
                                                           scale=lnp(l, kg, c), bias=lnp(l, kb_, c)),
                          [uk[k], "pcol"], [("xb", c)])
                        E2(lambda e, c=c, u=u: e.tensor_scalar(out=xres[:, c, 0:N], in0=u, scalar1=lnp(l, kg, c),
                                                               scalar2=lnp(l, kb_, c), op0=ALU.mult, op1=ALU.add),
                           [uk[k], "pcol"], [("xres", c)])

        def merge_units(l, N):
            xsrc, xkeys = xb_src(N)
            m0 = lambda c: fp[:, c, 0:N]
            gslot = {1: 0, 2: 8, 0: 0}
            units, tail = [], []

            def one_chunk(get_w, ncks, j, src, srckeys, evac):
                def u():
                    wsl, wkey = get_w()
                    b = nb()
                    for c in range(ncks):
                        MM(banks[b][:, 0:N], wsl[:, c, j * 128:(j + 1) * 128], src(c), c == 0, c == ncks - 1,
                           [wkey, srckeys(c)], b)
                    evac(b, banks[b][:, 0:N])
                    cast_tick()
                return u

            def lazy(loader):
                box = []

                def get():
                    if not box:
                        box.append(loader())
                    return box[0]
                return get

            def g_units(r):
                out = []
                for hseg in range(2):
                    a0 = C_GL + r * 1024 + hseg * 512
                    get_w = lazy(lambda a0=a0: wload(wb_in[l][:, :, a0:a0 + 512], (8, 512), cast_keys(("in", l), a0, a0 + 512)))
                    for j in range(4):
                        c = hseg * 4 + j

                        def ev(b, ps, c=c, r=r):
                            A(lambda a: a.activation(out=bq[:, gslot[r] + c, 0:N], in_=ps, func=AF.Sigmoid), [BK(b)],
                              [("bq", gslot[r] + c)])
                        out.append(one_chunk(get_w, 8, j, xsrc, xkeys, ev))
                return out

            def wb_units(r, pos):
                out = []
                for hseg in range(2):
                    def loader(r=r, hseg=hseg):
                        rk = cast_keys(("br", l, r), hseg * 512, (hseg + 1) * 512)
                        sl = ring_rr[0]
                        ring_rr[0] = (sl + 1) % NRING
                        wsl = ring[:, sl, 0:2048].rearrange("p (a b) -> p a b", a=4)
                        P.dma("sp", lambda q: q.dma_start(out=wsl, in_=wb_br[l, r][:, :, hseg * 512:(hseg + 1) * 512]),
                              reads=rk, writes=[("ring", sl)])
                        return wsl, ("ring", sl)
                    get_w = lazy(loader)
                    for j in range(4):
                        c = hseg * 4 + j

                        def ev(b, ps, c=c, r=r, pos=pos):
                            Gc = bq[:, gslot[r] + c, 0:N]
                            gk = ("bq", gslot[r] + c)
                            if pos == 0:
                                V(lambda v: v.tensor_tensor(out=m0(c), in0=ps, in1=Gc, op=ALU.mult), [BK(b), gk], [("fp", c)])
                            else:
                                tt = tmpf[:, c % 2, 0:N]
                                V(lambda v: v.tensor_tensor(out=tt, in0=ps, in1=Gc, op=ALU.mult), [BK(b), gk], [("tmpf", c % 2)])
                                if pos == 1:
                                    G(lambda g: g.tensor_tensor(out=m0(c), in0=m0(c), in1=tt, op=ALU.add),
                                      [("fp", c), ("tmpf", c % 2)], [("fp", c)])
                                else:
                                    G(lambda g: g.tensor_tensor(out=bq[:, 8 + c, 0:N], in0=m0(c), in1=tt, op=ALU.add),
                                      [("fp", c), ("tmpf", c % 2)], [("bq", 8 + c)])
                        out.append(one_chunk(get_w, 4, j, lambda cc, r=r: yabc[:, r, cc, 0:N], lambda cc, r=r: ("yabc", r, cc), ev))
                return out

            units += g_units(1) + wb_units(1, 0) + g_units(2) + wb_units(2, 1) + g_units(0)
            tail += wb_units(0, 2)
            return units, tail

        def merge_and_ffn(l, N, last, tail_units=None, after_ffn=None):
            xsrc, xkeys = xb_src(N)
            if tail_units is None:
                units, tail_units = merge_units(l, N)
                for u in units:
                    u()
            for u in tail_units:
                u()
            stage(21 + 20 * l)
            st1 = LNStats(N)
            wsl = wkey = None
            for c in range(8):
                if c % 4 == 0:
                    hs = c // 4
                    wsl, wkey = wload(wb_mix[l][:, :, hs * 512:(hs + 1) * 512], (8, 512), cast_keys(("mix", l), hs * 512, (hs + 1) * 512))
                b = nb()
                j = c % 4
                for k in range(8):
                    MM(banks[b][:, 0:N], wsl[:, k, j * 128:(j + 1) * 128], bq[:, 8 + k, 0:N], k == 0, k == 7,
                       [wkey, ("bq", 8 + k)], b)
                V(lambda v, c=c, b=b: v.scalar_tensor_tensor(out=xres[:, c, 0:N], in0=xres[:, c, 0:N], scalar=ALPHA,
                                                             in1=banks[b][:, 0:N], op0=ALU.mult, op1=ALU.add),
                  [("xres", c), BK(b)], [("xres", c)])
                if c >= 1:
                    st1.feed(c - 1)
            st1.feed(7)
            stage(22 + 20 * l)
            layer_norm(l, 1, N, False, st1)
            stage(23 + 20 * l)
            for pj in range(NFC // 2):
                s = ring_rr[0]
                ring_rr[0] = (s + 1) % NRING
                wsl = ring[:, s, 0:4096].rearrange("p (a b) -> p a b", a=8)
                f0 = pj * 256
                P.dma("sp", lambda q, wsl=wsl, f0=f0: q.dma_start(
                    out=wsl.rearrange("p c (u f) -> p c u f", u=2),
                    in_=wb_fi[l].rearrange("p c (u f) -> p c u f", u=2)[:, :, :, f0:f0 + 256]),
                      reads=cast_keys(("fi", l), f0, f0 + 256) + cast_keys(("fi", l), D_FF + f0, D_FF + f0 + 256),
                      writes=[("ring", s)])
                wkey = ("ring", s)
                for jj in range(2):
                    fc = pj * 2 + jj
                    bg = nb()
                    for c in range(8):
                        MM(banks[bg][:, 0:N], wsl[:, c, jj * 128:(jj + 1) * 128], xb[:, c, 0:N], c == 0, c == 7,
                           [wkey, ("xb", c)], bg)
                    bu = nb()
                    for c in range(8):
                        MM(banks[bu][:, 0:N], wsl[:, c, 256 + jj * 128:256 + (jj + 1) * 128], xb[:, c, 0:N], c == 0, c == 7,
                           [wkey, ("xb", c)], bu)
                    sg = tmpf[:, fc % 2, 0:N]
                    A(lambda a, sg=sg, bg=bg: a.activation(out=sg, in_=banks[bg][:, 0:N], func=AF.Silu), [BK(bg)],
                      [("tmpf", fc % 2)])
                    V(lambda v, sg=sg, bu=bu, fc=fc: v.tensor_tensor(out=bp[:, fc, 0:N], in0=banks[bu][:, 0:N], in1=sg,
                                                                     op=ALU.mult),
                      [BK(bu), ("tmpf", fc % 2)], [("bp", fc)])
                    cast_tick()
            stage(24 + 20 * l)
            st2 = LNStats(N)
            for ep in range(4):
                wk = []
                for kh in range(2):
                    wsl, wkey = wload(wb_fo[l][:, kh * 11:(kh + 1) * 11, ep * 256:(ep + 1) * 256], (11, 256),
                                      cast_keys(("fo", l, kh), ep * 256, (ep + 1) * 256))
                    wk.append((wsl, wkey))
                for jj in range(2):
                    c_out = ep * 2 + jj
                    b = nb()
                    for fc in range(NFC):
                        wsl, wkey = wk[fc // 11]
                        MM(banks[b][:, 0:N], wsl[:, fc % 11, jj * 128:(jj + 1) * 128], bp[:, fc, 0:N], fc == 0,
                           fc == NFC - 1, [wkey, ("bp", fc)], b)
                    V(lambda v, c=c_out, b=b: v.scalar_tensor_tensor(out=xres[:, c, 0:N], in0=xres[:, c, 0:N], scalar=ALPHA,
                                                                     in1=banks[b][:, 0:N], op0=ALU.mult, op1=ALU.add),
                      [("xres", c_out), BK(b)], [("xres", c_out)])
                    if c_out >= 1:
                        st2.feed(c_out - 1)
            st2.feed(7)
            if after_ffn is not None:
                after_ffn()
            stage(25 + 20 * l)
            layer_norm(l, 2, N, last, st2)

        QT, KT, OG, XQ, SQ, KTOK = 0, 4, 8, 12, 16, 20

        def prompt_mixers(l, ti):
            N = T
            xsrc, xkeys = xb_src(N)
            first_tile = (ti == 0)
            last_tile = (ti == NT - 1)
            GS = lambda s: gsm[:, s, :]
            zpre = gsm[:, 0:2, :].rearrange("p a b -> p (a b)").rearrange("p (t e) -> p t e", e=8)
            li = GS(2).rearrange("p (t h) -> p t h", h=4)
            lf = GS(3).rearrange("p (t h) -> p t h", h=4)
            Mp = GS(10)
            amm = GS(11)
            es = GS(13).rearrange("p (t h) -> p t h", h=4)
            lb = GS(14).rearrange("p (t h) -> p t h", h=4)
            aa = amm.rearrange("p (t h) -> p t h", h=4)
            gbank = {}

            def g0():
                bgt = nb()
                for tc in range(4):
                    for c in range(8):
                        MM(banks[bgt][:, tc * 8:(tc + 1) * 8], xb[:, c, tc * 128:(tc + 1) * 128], wg[:, l, c, :],
                           c == 0, c == 7, ["wg", ("xb", c)], bgt, inc=(c == 7 and tc == 3))
                V(lambda v: v.tensor_tensor(out=zpre, in0=banks[bgt][:, 0:32].rearrange("p (t e) -> p t e", e=8),
                                            in1=bgate[:, l, :].unsqueeze(1).to_broadcast([128, 4, 8]), op=ALU.add),
                  [BK(bgt), "bgate"], [("gsm", 0)])
                V(lambda v: v.tensor_copy(out=li, in_=zpre[:, :, 0:4]), [("gsm", 0)], [("gsm", 2)])
                A(lambda a: a.activation(out=lf, in_=zpre[:, :, 4:8], func=AF.Exp, scale=-1.0), [("gsm", 0)], [("gsm", 3)])
                A(lambda a: a.activation(out=lf, in_=lf, func=AF.Ln, bias=1.0), [("gsm", 3)], [("gsm", 3)])

            def g1():
                V(lambda v: v.tensor_scalar(out=GS(3), in0=GS(3), scalar1=-1.0, scalar2=None, op0=ALU.mult),
                  [("gsm", 3)], [("gsm", 3)])
                bcs = nb()
                reserved.add(bcs)
                gbank["bcs"] = bcs
                P.op("pe", lambda t: t.matmul(banks[bcs][:, 0:16], lhsT=trif, rhs=GS(3), start=True, stop=True),
                     reads=["cf", ("gsm", 3)], writes=[BK(bcs)], inc=False)
                P.op("pe", lambda t: t.matmul(banks[bcs][:, 16:32], lhsT=onesf, rhs=GS(3), start=True, stop=True),
                     reads=["cf", ("gsm", 3)], writes=[BK(bcs)], acc=True)

            def g2():
                bcs = gbank["bcs"]
                V(lambda v: v.tensor_tensor(out=GS(4), in0=GS(2), in1=banks[bcs][:, 0:16], op=ALU.subtract),
                  [("gsm", 2), BK(bcs)], [("gsm", 4)])
                V(lambda v: v.tensor_scalar(out=GS(5), in0=banks[bcs][:, 0:16], scalar1=-1.0, scalar2=None, op0=ALU.mult),
                  [BK(bcs)], [("gsm", 5)])
                V(lambda v: v.tensor_copy(out=GS(6), in_=banks[bcs][:, 16:32]), [BK(bcs)], [("gsm", 6)])
                reserved.discard(bcs)
                btr = nb()
                reserved.add(btr)
                gbank["btr"] = btr
                TR(banks[btr][0:16, 0:128], GS(4), identf, [("gsm", 4), "cf"], btr)

            def g3():
                btr = gbank["btr"]
                V(lambda v: v.tensor_reduce(out=gsm[0:16, 7, 0:1], in_=banks[btr][0:16, 0:128], axis=AX.X, op=ALU.max),
                  [BK(btr)], [("gsm", 7)])
                V(lambda v: v.tensor_scalar(out=gsm[0:16, 8, :], in0=identf[0:16, 0:16], scalar1=gsm[0:16, 7, 0:1], scalar2=None,
                                            op0=ALU.mult), [("gsm", 7), "cf"], [("gsm", 8)])
                reserved.discard(btr)
                bgm = nb()
                reserved.add(bgm)
                gbank["bgm"] = bgm
                P.op("pe", lambda t: t.matmul(banks[bgm][:, 0:16], lhsT=onesf[0:16, :], rhs=gsm[0:16, 8, :], start=True, stop=True),
                     reads=["cf", ("gsm", 8)], writes=[BK(bgm)])

            def g4():
                bgm = gbank["bgm"]
                V(lambda v: v.tensor_copy(out=GS(9), in_=banks[bgm][:, 0:16]), [BK(bgm)], [("gsm", 9)])
                reserved.discard(bgm)
                for tc in range(4):
                    sl = slice(tc * 4, tc * 4 + 4)
                    mprev = mst[:, l, :] if tc == 0 else GS(12)[:, (tc - 1) * 4:tc * 4]
                    mk_ = "mst" if tc == 0 else ("gsm", 12)
                    V(lambda v, sl=sl, mprev=mprev: v.tensor_tensor(out=Mp[:, sl], in0=GS(9)[:, sl], in1=mprev, op=ALU.max),
                      [("gsm", 9), mk_], [("gsm", 10)])
                    V(lambda v, sl=sl, mprev=mprev: v.tensor_tensor(out=amm[:, sl], in0=mprev, in1=Mp[:, sl], op=ALU.subtract),
                      [mk_, ("gsm", 10)], [("gsm", 11)])
                    V(lambda v, sl=sl: v.tensor_tensor(out=GS(12)[:, sl], in0=GS(6)[:, sl], in1=Mp[:, sl], op=ALU.add),
                      [("gsm", 6), ("gsm", 10)], [("gsm", 12)])
                V(lambda v: v.tensor_copy(out=mst[:, l, :], in_=GS(12)[:, 12:16]), [("gsm", 12)], ["mst"])
                A(lambda a: a.activation(out=amm, in_=amm, func=AF.Exp), [("gsm", 11)], [("gsm", 11)])

            def g5():
                V(lambda v: v.tensor_tensor(out=GS(13), in0=GS(4), in1=Mp, op=ALU.subtract), [("gsm", 4), ("gsm", 10)], [("gsm", 13)])
                A(lambda a: a.activation(out=GS(13), in_=GS(13), func=AF.Exp), [("gsm", 13)], [("gsm", 13)])
                V(lambda v: v.tensor_tensor(out=GS(14), in0=GS(5), in1=Mp, op=ALU.subtract), [("gsm", 5), ("gsm", 10)], [("gsm", 14)])
                A(lambda a: a.activation(out=GS(14), in_=GS(14), func=AF.Exp), [("gsm", 14)], [("gsm", 14)])

            g0()

            stage(5 + 20 * l)
            wq, wqk = wload(wb_in[l][:, :, C_MQ:C_MQ + 512], (8, 512), cast_keys(("in", l), C_MQ, C_MQ + 512))
            fm_group(wq, wqk, 8, 0, 4, xsrc, xkeys, N,
                     lambda j, b, ps: evac_copy(bp[:, QT + j, :], ps, [BK(b)], [("bp", QT + j)]),
                     after=lambda j: (g1() if j == 1 else g2() if j == 3 else None))
            wk_, wkk = wload(wb_in[l][:, :, C_MK:C_MK + 512], (8, 512), cast_keys(("in", l), C_MK, C_MK + 512))
            ksc = 128.0 ** -0.5
            fm_group(wk_, wkk, 8, 0, 4, xsrc, xkeys, N,
                     lambda j, b, ps: A(lambda a: a.activation(out=bp[:, KT + j, :], in_=ps, func=AF.Copy, scale=ksc),
                                        [BK(b)], [("bp", KT + j)]),
                     after=lambda j: (g3() if j == 1 else g4() if j == 3 else None))
            for tc in range(4):
                b = nb()
                for c in range(8):
                    MM(banks[b][:, :], xb[:, c, tc * 128:(tc + 1) * 128], wk_[:, c, :], c == 0, c == 7, [wkk, ("xb", c)], b)
                V(lambda v, tc=tc, b=b: v.tensor_scalar(out=bp[:, KTOK + tc, :], in0=banks[b][:, :], scalar1=ksc, scalar2=None,
                                                        op0=ALU.mult), [BK(b)], [("bp", KTOK + tc)])
                if tc == 1:
                    g5()
            wo_, wok = wload(wb_in[l][:, :, C_MO:C_MO + 512], (8, 512), cast_keys(("in", l), C_MO, C_MO + 512))

            def ev_og(j, b, ps):
                A(lambda a: a.activation(out=bp[:, OG + j, :], in_=ps, func=AF.Sigmoid), [BK(b)], [("bp", OG + j)])
                V(lambda v: v.tensor_scalar(out=bp[:, OG + j, :], in0=bp[:, OG + j, :], scalar1=normg(l, j), scalar2=None,
                                            op0=ALU.mult), [("bp", OG + j), "pcol"], [("bp", OG + j)])
            fm_group(wo_, wok, 8, 0, 4, xsrc, xkeys, N, ev_og)

            stage(6 + 20 * l)
            wsq, wsqk = wload(wb_in[l][:, :, C_SQ:C_SQ + 512], (8, 512), cast_keys(("in", l), C_SQ, C_SQ + 512))
            fm_group(wsq, wsqk, 8, 0, 4, xsrc, xkeys, N,
                     lambda j, b, ps: rope_apply(ps, b, N, j % 2, bp[:, SQ + j, :], [("bp", SQ + j)]))
            wkv, wkvk = wload(wb_in[l][:, :, C_SK:C_SK + 256], (8, 256), cast_keys(("in", l), C_SK, C_SK + 256))
            k32 = fp[:, 4, :]

            def ev_sk2(j, b, ps):
                q32 = fp[:, 0, 0:N]
                evac_copy(q32, ps, [BK(b)], [("fp", 0)])
                b2 = nb()
                P.op("pe", lambda t: t.matmul(banks[b2][:, 0:N], lhsT=rrf, rhs=q32, start=True, stop=True),
                     reads=["cf", ("fp", 0)], writes=[BK(b2)])
                rs = tmpf[:, 1, 0:N]
                V(lambda v: v.tensor_tensor(out=rs, in0=banks[b2][:, 0:N], in1=rtab[:, 1, 0:N], op=ALU.mult),
                  [BK(b2), ("rtab", 1)], [("tmpf", 1)])
                G(lambda g: g.tensor_tensor(out=q32, in0=q32, in1=rtab[:, 0, 0:N], op=ALU.mult), [("fp", 0), ("rtab", 0)], [("fp", 0)])
                G(lambda g: g.tensor_tensor(out=k32, in0=q32, in1=rs, op=ALU.add), [("fp", 0), ("tmpf", 1)], [("fp", 4)])
                G(lambda g: g.tensor_copy(out=skT[:, l, 1:5, :], in_=k32.rearrange("p (a b) -> p a b", a=4)),
                  [("fp", 4)], [("skT", l)])
            fm_group(wkv, wkvk, 8, 0, 1, xsrc, xkeys, N, ev_sk2)
            for tc in range(4):
                b = nb()
                for c in range(8):
                    MM(banks[b][:, 0:128], xb[:, c, tc * 128:(tc + 1) * 128], wkv[:, c, 128:256], c == 0, c == 7,
                       [wkvk, ("xb", c)], b)
                for kv in range(2):
                    evac_copy(svp[:, l, 1 + tc, kv, kv * 64:(kv + 1) * 64], banks[b][:, kv * 64:(kv + 1) * 64],
                              [BK(b)], [("svp", l)])
                if last_tile and tc == 3:
                    V(lambda v, b=b: v.tensor_copy(out=hh[:, 0, 0, :], in_=banks[b][:, 0:128]), [BK(b)], [("hh", 0)])
                    P.dma("sp", lambda q: q.dma_start(out=o_svp[l], in_=hh[:, 0, 0, :]), reads=[("hh", 0)])
            if last_tile:
                bt = nb()
                TR(banks[bt][:, 0:128], k32[:, 384:512], identf, [("fp", 4), "cf"], bt)
                V(lambda v, bt=bt: v.tensor_copy(out=hh[:, 1, 0, :], in_=banks[bt][:, 0:128]), [BK(bt)], [("hh", 1)])
                P.dma("sp", lambda q: q.dma_start(out=o_skp[l], in_=hh[:, 1, 0, :]), reads=[("hh", 1)])

            stage(7 + 20 * l)
            def swa_scores(qb):
                pbuf = qb % 2
                slots = []
                for kv in range(2):
                    for kb in range(2):
                        if kb == 0 and first_tile and qb == 0:
                            continue
                        b = nb()
                        psl = slice(kv * 64, (kv + 1) * 64)
                        rhs = bp[psl, SQ:SQ + 4, qb * 128:(qb + 1) * 128]
                        MM(banks[b][:, :].rearrange("p (g t) -> p g t", g=4), skT[psl, l, qb + kb, :], rhs, True, False,
                           [("skT", l)] + [("bp", SQ + g) for g in range(4)], b, inc=False)
                        MM(banks[b][:, :].rearrange("p (g t) -> p g t", g=4), identb[:, :], mneg[:, 1 - kb, :, :], False, True,
                           ["identb", "mneg"], b)
                        pi = pbuf * 4 + kv * 2 + kb
                        A(lambda a, b=b, pi=pi: a.activation(out=bq[:, pi, :], in_=banks[b][:, :], func=AF.Exp, scale=0.125),
                          [BK(b)], [("bq", pi)])
                        slots.append((kv, kb, pi))
                return slots

            def swa_finish(qb, slots):
                bnum = nb()
                for i, (kv, kb, pi) in enumerate(slots):
                    MM(banks[bnum][:, :], svp[:, l, qb + kb, kv, :], bq[:, pi, :], i == 0, i == len(slots) - 1,
                       [("svp", l), ("bq", pi)], bnum)
                bden = nb()
                for i, (kv, kb, pi) in enumerate(slots):
                    MM(banks[bden][:, :], onespad[:, kv, :], bq[:, pi, :], i == 0, i == len(slots) - 1,
                       ["onespad", ("bq", pi)], bden)
                dn = tmpf[:, qb % 2, :]
                V(lambda v: v.tensor_tensor(
                    out=dn.rearrange("p (g t) -> p g t", g=4), in0=banks[bden][:, :].rearrange("p (g t) -> p g t", g=4),
                    in1=sinkexp[:, l, :].unsqueeze(2).to_broadcast([128, 4, 128]), op=ALU.add),
                  [BK(bden), "sinkexp"], [("tmpf", qb % 2)])
                V(lambda v: v.reciprocal(out=dn, in_=dn), [("tmpf", qb % 2)], [("tmpf", qb % 2)])
                V(lambda v: v.tensor_tensor(
                    out=yabc[:, 1, :, qb * 128:(qb + 1) * 128], in0=banks[bnum][:, :].rearrange("p (g t) -> p g t", g=4),
                    in1=dn.rearrange("p (g t) -> p g t", g=4), op=ALU.mult),
                  [BK(bnum), ("tmpf", qb % 2)], [("yabc", 1, g) for g in range(4)])

            nxt = swa_scores(0)
            for qb in range(4):
                cur = nxt
                if qb + 1 < 4:
                    nxt = swa_scores(qb + 1)
                swa_finish(qb, cur)
            G(lambda g: g.tensor_copy(out=skT[:, l, 0, :], in_=skT[:, l, 4, :]), [("skT", l)], [("skT", l)])
            G(lambda g: g.tensor_copy(out=svp[:, l, 0], in_=svp[:, l, 4]), [("svp", l)], [("svp", l)])

            stage(8 + 20 * l)
            wxq, wxqk = wload(wb_in[l][:, :, C_XQ:C_XQ + 512], (8, 512), cast_keys(("in", l), C_XQ, C_XQ + 512))
            fm_group(wxq, wxqk, 8, 0, 4, xsrc, xkeys, N,
                     lambda j, b, ps: evac_copy(bp[:, XQ + j, :], ps, [BK(b)], [("bp", XQ + j)]))
            xsc = 128.0 ** -0.5

            def x_scores(h):
                for mb in range(2):
                    b = nb()
                    MM(banks[b][:, :], memKT[:, l, h, mb * 128:(mb + 1) * 128], bp[:, XQ + h, :], True, True,
                       [("memKT", l), ("bp", XQ + h)], b)
                    pi = 8 + (h % 2) * 2 + mb
                    A(lambda a, b=b, pi=pi: a.activation(out=bq[:, pi, :], in_=banks[b][:, :], func=AF.Exp, scale=xsc),
                      [BK(b)], [("bq", pi)])

            def x_finish(h):
                bnum = nb()
                for mb in range(2):
                    pi = 8 + (h % 2) * 2 + mb
                    MM(banks[bnum][:, :], memV[:, l, mb, h * 128:(h + 1) * 128], bq[:, pi, :], mb == 0, mb == 1,
                       [("memV", l), ("bq", pi)], bnum)
                bden = nb()
                for mb in range(2):
                    pi = 8 + (h % 2) * 2 + mb
                    MM(banks[bden][:, :], onesb[:, :], bq[:, pi, :], mb == 0, mb == 1, ["onesb", ("bq", pi)], bden)
                dn = tmpf[:, h % 2, :]
                V(lambda v: v.reciprocal(out=dn, in_=banks[bden][:, :]), [BK(bden)], [("tmpf", h % 2)])
                V(lambda v: v.tensor_tensor(out=yabc[:, 2, h, :], in0=banks[bnum][:, :], in1=dn, op=ALU.mult),
                  [BK(bnum), ("tmpf", h % 2)], [("yabc", 2, h)])

            x_scores(0)
            for h in range(4):
                if h + 1 < 4:
                    x_scores(h + 1)
                x_finish(h)

            stage(9 + 20 * l)
            wv_, wvk = wload(wb_in[l][:, :, C_MV:C_MV + 512], (8, 512), cast_keys(("in", l), C_MV, C_MV + 512))
            for tc in range(4):
                b = nb()
                for c in range(8):
                    MM(banks[b][:, :], xb[:, c, tc * 128:(tc + 1) * 128], wv_[:, c, :], c == 0, c == 7, [wvk, ("xb", c)], b)
                V(lambda v, tc=tc, b=b: v.tensor_tensor(out=vaug[:, tc, :, 0:128],
                                                        in0=banks[b][:, :].rearrange("p (h d) -> p h d", h=4),
                                                        in1=es[:, tc, :].unsqueeze(2).to_broadcast([128, 4, 128]), op=ALU.mult),
                  [BK(b), ("gsm", 13)], [("vaug", tc)])
                G(lambda g, tc=tc: g.tensor_copy(out=vaug[:, tc, :, 128:129], in_=es[:, tc, :].unsqueeze(2)),
                  [("gsm", 13), ("vaug", tc)], [("vaug", tc)])
            fill_units, tail_units = merge_units(l, N)
            per_pt = -(-len(fill_units) // 8)

            def filler(k=per_pt):
                for _ in range(k):
                    if fill_units:
                        fill_units.pop(0)()
            for tc in range(4):
                tsl = slice(tc * 128, (tc + 1) * 128)
                V(lambda v, tc=tc: v.tensor_tensor(out=CTs[:], in0=CT[:, l], in1=aa[:, tc, :].unsqueeze(2).to_broadcast([128, 4, 129]),
                                                   op=ALU.mult), [("CT", l), ("gsm", 11)], ["CTs"])
                G(lambda g: g.tensor_copy(out=CTb[:], in_=CTs[:]), ["CTs"], ["CTb"])
                bs = nb()
                for h in range(4):
                    MM(banks[bs][:, h * 128:(h + 1) * 128], bp[:, KT + h, tsl], bp[:, QT + h, tsl], True, True,
                       [("bp", KT + h), ("bp", QT + h)], bs, inc=(h == 3))
                sm = smt[:, tc % 2]
                V(lambda v, sm=sm, bs=bs: v.tensor_tensor(out=sm, in0=banks[bs][:, :].rearrange("p (h t) -> p h t", h=4),
                                                          in1=trib[:, :].unsqueeze(1).to_broadcast([128, 4, 128]), op=ALU.mult),
                  [BK(bs), "trib"], [("smt", tc % 2)])
                filler()
                bo = [nb(), nb()]
                for h in range(4):
                    bb = bo[h // 2]
                    o_ap = banks[bb][:, (h % 2) * 129:(h % 2) * 129 + 129]
                    MM(o_ap, bp[:, QT + h, tsl], CTb[:, h, :], True, False, [("bp", QT + h), "CTb"], bb, inc=False)
                    MM(o_ap, sm[:, h, :], vaug[:, tc, h, :], False, True, [("smt", tc % 2), ("vaug", tc)], bb, inc=(h % 2 == 1))
                bu = [nb(), nb()]
                for h in range(4):
                    bb = bu[h // 2]
                    MM(banks[bb][:, (h % 2) * 129:(h % 2) * 129 + 129], bp[:, KTOK + tc, h * 128:(h + 1) * 128], vaug[:, tc, h, :],
                       True, True, [("bp", KTOK + tc), ("vaug", tc)], bb, inc=(h % 2 == 1))
                for hp in range(2):
                    V(lambda v, hp=hp, bb=bu[hp]: v.tensor_tensor(
                        out=CT[:, l, 2 * hp:2 * hp + 2, :], in0=banks[bb][:, 0:258].rearrange("p (h d) -> p h d", h=2),
                        in1=CTs[:, 2 * hp:2 * hp + 2, :], op=ALU.add), [BK(bu[hp]), "CTs"], [("CT", l)])
                hb = hh[:, tc % 2]
                dn4 = gsm[:, 15, 0:4]
                for hp in range(2):
                    V(lambda v, hp=hp, bb=bo[hp]: v.tensor_copy(
                        out=gsm[:, 15, 8 + 2 * hp:8 + 2 * hp + 2].unsqueeze(2),
                        in_=banks[bb][:, 0:258].rearrange("p (h d) -> p h d", h=2)[:, :, 128:129]),
                      [BK(bo[hp])], [("gsm", 15)])
                V(lambda v: v.scalar_tensor_tensor(out=dn4, in0=gsm[:, 15, 8:12], scalar=-1.0, in1=gsm[:, 15, 8:12],
                                                   op0=ALU.mult, op1=ALU.max), [("gsm", 15)], [("gsm", 15)])
                V(lambda v, tc=tc: v.tensor_tensor(out=dn4, in0=dn4, in1=lb[:, tc, :], op=ALU.max), [("gsm", 15), ("gsm", 14)],
                  [("gsm", 15)])
                V(lambda v: v.reciprocal(out=dn4, in_=dn4), [("gsm", 15)], [("gsm", 15)])
                for hp in range(2):
                    V(lambda v, hp=hp, bb=bo[hp], hb=hb: v.tensor_tensor(
                        out=hb[:, 2 * hp:2 * hp + 2, :],
                        in0=banks[bb][:, 0:258].rearrange("p (h d) -> p h d", h=2)[:, :, 0:128],
                        in1=gsm[:, 15, 2 * hp:2 * hp + 2].unsqueeze(2).to_broadcast([128, 2, 128]), op=ALU.mult),
                      [BK(bo[hp]), ("gsm", 15)], [("hh", tc % 2)])
                for h in range(4):
                    V(lambda v, h=h, hb=hb: v.bn_stats(out=bnst[:, h, :], in_=hb[:, h, :]), [("hh", tc % 2)], ["bnst"])
                    V(lambda v, h=h: v.bn_aggr(out=bnag[:, h, :], in_=bnst[:, h, :]), ["bnst"], ["bnag"])
                rs4 = gsm[:, 15, 4:8]
                V(lambda v: v.tensor_scalar(out=rs4.unsqueeze(2), in0=bnag[:, :, 1:2], scalar1=HN_EPS, scalar2=None, op0=ALU.add),
                  ["bnag"], [("gsm", 15)])
                A(lambda a: a.activation(out=rs4, in_=rs4, func=AF.Sqrt), [("gsm", 15)], [("gsm", 15)])
                V(lambda v: v.reciprocal(out=rs4, in_=rs4), [("gsm", 15)], [("gsm", 15)])
                for h in range(4):
                    V(lambda v, h=h, hb=hb: v.tensor_scalar(out=hb[:, h, :], in0=hb[:, h, :], scalar1=bnag[:, h, 0:1],
                                                            scalar2=gsm[:, 15, 4 + h:5 + h], op0=ALU.subtract, op1=ALU.mult),
                      [("hh", tc % 2), "bnag", ("gsm", 15)], [("hh", tc % 2)])
                filler()
                bt = nb()
                for h in range(4):
                    TR(banks[bt][:, h * 128:(h + 1) * 128], hb[:, h, :], identf, [("hh", tc % 2), "cf"], bt, inc=(h == 3), first=(h == 0))
                V(lambda v, bt=bt, tsl=tsl: v.tensor_tensor(out=yabc[:, 0, :, tsl], in0=banks[bt][:, :].rearrange("p (h t) -> p h t", h=4),
                                                            in1=bp[:, OG:OG + 4, tsl], op=ALU.mult),
                  [BK(bt)] + [("bp", OG + h) for h in range(4)], [("yabc", 0, h) for h in range(4)])
            while fill_units:
                fill_units.pop(0)()
            if last_tile:
                for h in range(4):
                    bt = nb()
                    TR(banks[bt][:, 0:128], CT[:, l, h, 0:128], identf, [("CT", l), "cf"], bt)
                    V(lambda v, bt=bt, h=h: v.tensor_copy(out=hh[:, h % 2, 1, :], in_=banks[bt][:, 0:128]), [BK(bt)], [("hh", h % 2)])
                    P.dma("sp", lambda q, h=h: q.dma_start(out=o_cp[l, h], in_=hh[:, h % 2, 1, :]), reads=[("hh", h % 2)])
                bt = nb()
                TR(banks[bt][0:4, 0:128], CT[:, l, :, 128], identf, [("CT", l), "cf"], bt)
                V(lambda v, bt=bt: v.tensor_copy(out=hh[0:4, 0, 2, :], in_=banks[bt][0:4, 0:128]), [BK(bt)], [("hh", 0)])
                P.dma("sp", lambda q: q.dma_start(out=o_np[l], in_=hh[0:4, 0, 2, :]), reads=[("hh", 0)])
                P.dma("sp", lambda q: q.dma_start(out=o_mp[l:l + 1, :], in_=mst[0:1, l, :]), reads=["mst"])
            return tail_units

        xin = bp[:, 0:16, :].rearrange("p c t -> p (c t)").bitcast(F32).rearrange("p (t d) -> p t d", t=4)

        def load_x(ti):
            r0 = ti * T
            P.dma("act", lambda q: q.dma_start(out=xin, in_=xp[r0:r0 + T, :].rearrange("(t p) d -> p t d", p=128)),
                  writes=[("bp", c) for c in range(16)])

        load_x(0)
        for ti in range(NT):
            r0 = ti * T
            for c in range(8):
                b = nb()
                for tc in range(4):
                    TR(banks[b][:, tc * 128:(tc + 1) * 128], xin[:, tc, c * 128:(c + 1) * 128], identf,
                       [("bp", 4 * tc + k) for k in range(4)] + ["cf"], b, inc=(tc == 3), first=(tc == 0))
                V(lambda v, c=c, b=b: v.tensor_copy(out=xres[:, c, :], in_=banks[b][:, :]), [BK(b)], [("xres", c)])
                A(lambda a, c=c, b=b: a.copy(out=xb[:, c, :], in_=banks[b][:, :]), [BK(b)], [("xb", c)])
            rope_tables(ti * T, T)
            stage(4)
            for l in range(DEPTH):
                tail = prompt_mixers(l, ti)
                stage(20 + 20 * l)
                hook = None
                if l == DEPTH - 1 and ti + 1 < NT:
                    hook = (lambda ti=ti: load_x(ti + 1))
                merge_and_ffn(l, T, last=(l == DEPTH - 1), tail_units=tail, after_ffn=hook)
                stage(30 + 20 * l)
            yout = fp[:, :, :].rearrange("p (t a) b -> p t (a b)", t=4)
            for tc in range(4):
                for half in range(2):
                    b = nb()
                    for cc in range(4):
                        c = half * 4 + cc
                        TR(banks[b][:, cc * 128:(cc + 1) * 128], xres[:, c, tc * 128:(tc + 1) * 128], identf,
                           [("xres", c), "cf"], b, inc=(cc == 3), first=(cc == 0))
                    evac_copy(yout[:, tc, half * 512:(half + 1) * 512], banks[b][:, :], [BK(b)], [("fp", 2 * tc + half)])
                P.dma("act", lambda q, tc=tc, r0=r0: q.dma_start(out=yp[r0 + tc * 128:r0 + (tc + 1) * 128, :], in_=yout[:, tc, :]),
                      reads=[("fp", 2 * tc), ("fp", 2 * tc + 1)])


        def sample_phase():
            N = NS
            P.barrier()
            CTf = CT[:].rearrange("p l h d -> p (l h d)")
            Cs = [CTf[:, 0:512].rearrange("p (h k) -> p h k", h=4), CTf[:, 512:1024].rearrange("p (h k) -> p h k", h=4)]
            mKb = [memKT[:].rearrange("p l h m -> p (l h m)").bitcast(F32).rearrange("p (b c) -> p b c", b=2),
                   memV[:].rearrange("p l b c -> p (l b c)").bitcast(F32).rearrange("p (b c) -> p b c", b=2)]
            mVb = [memT[:, 0:4, :].rearrange("p c (b d) -> p (c b) d", d=128),
                   memT[:, 4:8, :].rearrange("p c (b d) -> p (c b) d", d=128)]
            KTc = hh[:].rearrange("p a h d -> p (a h d)").bitcast(BF16)[:, 0:1024].rearrange("p (b d) -> p b d", d=128)
            svf = svp[:, 1].rearrange("p b k d -> p (b k d)").bitcast(F32)
            Ks = [svf[:, 0:128], svf[:, 128:256]]
            Vp = [svp[:, 0, 0], svp[:, 0, 1]]
            KTs = skT[:, 0, 0, :]
            t1b = [smt[:].rearrange("p a h t -> p (a h t)").bitcast(F32).rearrange("p (h k) -> p h k", h=4),
                   CTs[:].rearrange("p h d -> p (h d)")[:, 0:512].rearrange("p (h k) -> p h k", h=4)]
            G(lambda g: g.memset(svp[:, 0, 0:2], 0.0), [], [("Vp", 0, 0), ("Vp", 0, 1), ("Vp", 1, 0), ("Vp", 1, 1)])

            S = lambda slot: sS[:, slot]
            SF = lambda slot: sS[:, slot].rearrange("p j h -> p (j h)")
            SK = lambda slot: ("sS", slot)
            a_bc = sR[:, :, 0:4]
            w_bc = sR[:, :, 4:8]
            lb_bc = sR[:, :, 8:12]

            def psv(b, c0):
                return banks[b][:, c0:c0 + 64].rearrange("p (j h) -> p j h", h=4)

            def sample_mixers(l):
                xsrc, xkeys = xb_src(N)
                bg = nb()
                for c in range(8):
                    MM(banks[bg][0:16, 0:8], xb[:, c, 0:N], wg[:, l, c, :], c == 0, c == 7, ["wg", ("xb", c)], bg)
                z = sx[:, 0:8]
                V(lambda v: v.tensor_tensor(out=z, in0=banks[bg][0:16, 0:8], in1=bgate[0:16, l, :], op=ALU.add),
                  [BK(bg), "bgate"], [("sx", 0)])
                P.dma("sp", lambda q: q.dma_start(out=sx[:, 8:12], in_=stm[l]), writes=[("sx", 1)])
                lf = sx[:, 12:16]
                A(lambda a: a.activation(out=lf, in_=z[:, 4:8], func=AF.Exp, scale=-1.0), [("sx", 0)], [("sx", 2)])
                A(lambda a: a.activation(out=lf, in_=lf, func=AF.Ln, bias=1.0), [("sx", 2)], [("sx", 2)])
                tt_ = sx[:, 16:20]
                mt = sx[:, 20:24]
                V(lambda v: v.tensor_tensor(out=tt_, in0=sx[:, 8:12], in1=lf, op=ALU.subtract), [("sx", 1), ("sx", 2)], [("sx", 3)])
                V(lambda v: v.tensor_tensor(out=mt, in0=tt_, in1=z[:, 0:4], op=ALU.max), [("sx", 3), ("sx", 0)], [("sx", 4)])
                V(lambda v: v.tensor_tensor(out=sx[:, 24:28], in0=tt_, in1=mt, op=ALU.subtract), [("sx", 3), ("sx", 4)], [("sx", 5)])
                V(lambda v: v.tensor_tensor(out=sx[:, 28:32], in0=z[:, 0:4], in1=mt, op=ALU.subtract), [("sx", 0), ("sx", 4)], [("sx", 5)])
                V(lambda v: v.tensor_scalar(out=sx[:, 32:36], in0=mt, scalar1=-1.0, scalar2=None, op0=ALU.mult), [("sx", 4)], [("sx", 5)])
                A(lambda a: a.activation(out=sx[:, 24:36], in_=sx[:, 24:36], func=AF.Exp), [("sx", 5)], [("sx", 5)])
                P.dma("sp", lambda q: q.dma_start(out=o_ms[l], in_=mt), reads=[("sx", 4)])
                V(lambda v: v.tensor_tensor(out=sD[:], in0=identf[0:16, 0:16].unsqueeze(2).to_broadcast([16, 16, 12]),
                                            in1=sx[:, 24:36].unsqueeze(1).to_broadcast([16, 16, 12]), op=ALU.mult),
                  ["cf", ("sx", 5)], ["sD"])
                br = nb()
                P.op("pe", lambda t: t.matmul(banks[br][:, 0:192], lhsT=onesf[0:16, :], rhs=sD[:].rearrange("j a q -> j (a q)"),
                                              start=True, stop=True), reads=["cf", "sD"], writes=[BK(br)])
                V(lambda v: v.tensor_copy(out=sR[:], in_=banks[br][:, 0:192].rearrange("p (a q) -> p a q", q=12)), [BK(br)], ["sR"])

                def proj(col, slot, kind):
                    wsl, wkey = wload(wb_in[l][:, :, col:col + 512], (8, 512), cast_keys(("in", l), col, col + 512))

                    def ev(j, b, ps):
                        dst = S(slot)[:, :, j]
                        if kind == "copy":
                            evac_copy(dst, ps, [BK(b)], [SK(slot)])
                        elif kind == "kscale":
                            A(lambda a: a.activation(out=dst, in_=ps, func=AF.Copy, scale=128.0 ** -0.5), [BK(b)], [SK(slot)])
                        elif kind == "og":
                            A(lambda a: a.activation(out=dst, in_=ps, func=AF.Sigmoid), [BK(b)], [SK(slot)])
                            V(lambda v: v.tensor_scalar(out=dst, in0=dst, scalar1=normg(l, j), scalar2=None, op0=ALU.mult),
                              [SK(slot), "pcol"], [SK(slot)])
                        elif kind == "rope":
                            rope_apply(ps, b, N, j % 2, dst, [SK(slot)])
                    fm_group(wsl, wkey, 8, 0, 4, xsrc, xkeys, N, ev)
                proj(C_MQ, 0, "copy")
                proj(C_MK, 1, "kscale")
                proj(C_MV, 2, "copy")
                proj(C_MO, 3, "og")
                proj(C_SQ, 4, "rope")
                wkv, wkvk = wload(wb_in[l][:, :, C_SK:C_SK + 256], (8, 256), cast_keys(("in", l), C_SK, C_SK + 256))

                def ev_kv(j, b, ps):
                    if j == 0:
                        rope_apply(ps, b, N, 0, S(6)[:, :, 0], [SK(6)])
                    else:
                        evac_copy(S(7)[:, :, 0], ps, [BK(b)], [SK(7)])
                fm_group(wkv, wkvk, 8, 0, 2, xsrc, xkeys, N, ev_kv)
                proj(C_XQ, 5, "copy")

                for wi, slot in enumerate((0, 1, 4, 5)):
                    V(lambda v, wi=wi, slot=slot: v.tensor_copy(out=sB[:, wi], in_=S(slot)), [SK(slot)], [("sB", wi)])
                P.dma("sp", lambda q: q.dma_start(out=sT[:, 0, :], in_=stn[l].rearrange("j h k -> (j h) k")), writes=[("sT", 0)])
                bt = nb()
                TR(banks[bt][:, 0:64], sT[:, 0, :], identf[0:64, 0:64], [("sT", 0), "cf"], bt)
                V(lambda v, bt=bt: v.tensor_copy(out=SF(8), in_=banks[bt][:, 0:64]), [BK(bt)], [SK(8)])

                G(lambda g: g.tensor_tensor(out=S(11), in0=S(0), in1=S(1), op=ALU.mult), [SK(0), SK(1)], [SK(11)])
                G(lambda g: g.tensor_tensor(out=S(12), in0=S(8), in1=S(0), op=ALU.mult), [SK(8), SK(0)], [SK(12)])
                bqk = nb()
                P.op("pe", lambda t: t.matmul(banks[bqk][:, 0:64], lhsT=onesf, rhs=SF(11), start=True, stop=True),
                     reads=["cf", SK(11)], writes=[BK(bqk)], inc=False)
                P.op("pe", lambda t: t.matmul(banks[bqk][:, 64:128], lhsT=onesf, rhs=SF(12), start=True, stop=True),
                     reads=["cf", SK(12)], writes=[BK(bqk)], acc=True)
                V(lambda v: v.tensor_tensor(out=S(13), in0=psv(bqk, 0), in1=w_bc, op=ALU.mult), [BK(bqk), "sR"], [SK(13)])
                V(lambda v: v.tensor_tensor(out=S(14), in0=psv(bqk, 64), in1=a_bc, op=ALU.mult), [BK(bqk), "sR"], [SK(14)])
                V(lambda v: v.tensor_tensor(out=S(14), in0=S(14), in1=S(13), op=ALU.add), [SK(14), SK(13)], [SK(14)])
                V(lambda v: v.tensor_tensor(out=S(10), in0=S(2), in1=w_bc, op=ALU.mult), [SK(2), "sR"], [SK(10)])

                acc_bank = nb()
                reserved.add(acc_bank)
                acc = banks[acc_bank]
                xsc = 128.0 ** -0.5
                for j in range(NS):
                    pb = j % 2
                    P.dma("sp", lambda q, j=j, pb=pb: q.dma_start(out=Cs[pb], in_=stc[l, j].rearrange("h v k -> v h k")),
                          writes=[("Cs", pb)])
                    P.dma("sp", lambda q, j=j, pb=pb: q.dma_start(out=Ks[pb], in_=csk[l, j]), writes=[("Ks", pb)])
                    for kv in range(2):
                        P.dma("pool", lambda q, j=j, pb=pb, kv=kv: q.dma_start(
                            out=Vp[pb][:, kv, kv * 64:(kv + 1) * 64], in_=csv[l, j][:, kv * 64:(kv + 1) * 64]),
                            writes=[("Vp", pb, kv)])
                    P.dma("sp", lambda q, j=j, pb=pb: q.dma_start(out=mKb[pb], in_=cmk[l, j].rearrange("(b m) c -> m b c", b=2)),
                          writes=[("mK", pb)])
                    for mb in range(2):
                        P.dma("pool", lambda q, j=j, pb=pb, mb=mb: q.dma_start(
                            out=mVb[pb][:, mb * 4:(mb + 1) * 4, :],
                            in_=cmv[l, j][mb * 128:(mb + 1) * 128, :].rearrange("m (h d) -> m h d", h=4)), writes=[("mV", pb, mb)])
                    bq_ = nb()
                    for h in range(4):
                        MM(banks[bq_][:, h * 128:(h + 1) * 128], sB[:, 0, j, h:h + 1].to_broadcast([128, 128]), identb[:, :], True, True,
                           [("sB", 0), "identb"], bq_, inc=(h == 3))
                    bk_ = nb()
                    for h in range(4):
                        MM(banks[bk_][:, h * 128:(h + 1) * 128], sB[:, 1, j, h:h + 1].to_broadcast([128, 128]), identb[:, :], True, True,
                           [("sB", 1), "identb"], bk_, inc=(h == 3))
                    t1 = t1b[pb]
                    V(lambda v, pb=pb, t1=t1, bq_=bq_: v.tensor_tensor(out=t1, in0=Cs[pb], in1=banks[bq_][:, :].rearrange("p (h k) -> p h k", h=4),
                                                                       op=ALU.mult), [("Cs", pb), BK(bq_)], [("t1", pb)])
                    V(lambda v, t1=t1, j=j: v.tensor_reduce(out=S(9)[:, j, :], in_=t1, axis=AX.X, op=ALU.add), [("t1", pb)], [SK(9)])
                    V(lambda v, t1=t1, bk_=bk_, j=j: v.tensor_tensor(out=t1, in0=banks[bk_][:, :].rearrange("p (h k) -> p h k", h=4),
                                                                     in1=S(10)[:, j, :].unsqueeze(2).to_broadcast([128, 4, 128]), op=ALU.mult),
                      [BK(bk_), SK(10)], [("t1", pb)])
                    G(lambda g, pb=pb, j=j: g.tensor_tensor(out=Cs[pb], in0=Cs[pb], in1=a_bc[:, j, :].unsqueeze(2).to_broadcast([128, 4, 128]),
                                                            op=ALU.mult), [("Cs", pb), "sR"], [("Cs", pb)])
                    G(lambda g, pb=pb, t1=t1: g.tensor_tensor(out=t1, in0=t1, in1=Cs[pb], op=ALU.add), [("t1", pb), ("Cs", pb)], [("t1", pb)])
                    P.dma("sp", lambda q, j=j, t1=t1: q.dma_start(out=o_cs[l, j].rearrange("h v k -> v h k"), in_=t1), reads=[("t1", pb)])
                    bt = nb()
                    TR(banks[bt][:, 0:128], Ks[pb], identf, [("Ks", pb), "cf"], bt)
                    evac_copy(KTs, banks[bt][:, 0:128], [BK(bt)], ["KTs"])
                    bs = nb()
                    for kv in range(2):
                        psl = slice(kv * 64, (kv + 1) * 64)
                        MM(banks[bs][:, kv * 4:(kv + 1) * 4], KTs[psl, :], sB[psl, 2, j, :], True, True, ["KTs", ("sB", 2)], bs, inc=(kv == 1))
                    A(lambda a, pb=pb, bs=bs: a.activation(out=sPb[:, pb, 0, :], in_=banks[bs][:, 0:8], func=AF.Exp, scale=0.125),
                      [BK(bs)], [("sP", pb, 0)])
                    for kv in range(2):
                        MM(acc[:, j * 4:(j + 1) * 4], Vp[pb][:, kv, :], sPb[:, pb, 0, kv * 4:(kv + 1) * 4], kv == 0, kv == 1,
                           [("Vp", pb, 0), ("Vp", pb, 1), ("sP", pb, 0)], acc_bank, inc=False)
                    for kv in range(2):
                        MM(acc[:, 64 + j * 4:64 + (j + 1) * 4], onespad[:, kv, :], sPb[:, pb, 0, kv * 4:(kv + 1) * 4], kv == 0, kv == 1,
                           ["onespad", ("sP", pb, 0)], acc_bank, inc=(kv == 1))
                    for mb in range(2):
                        btc = nb()
                        for h in range(4):
                            TR(banks[btc][:, h * 128:(h + 1) * 128], mKb[pb][:, mb, h * 128:(h + 1) * 128], identf, [("mK", pb), "cf"], btc,
                               inc=(h == 3), first=(h == 0))
                        evac_copy(KTc[:, mb * 4:(mb + 1) * 4, :], banks[btc][:, :].rearrange("p (h d) -> p h d", h=4), [BK(btc)], [("KTc", mb)])
                    bs2 = nb()
                    for mb in range(2):
                        for h in range(4):
                            MM(banks[bs2][:, mb * 4 + h:mb * 4 + h + 1], KTc[:, mb * 4 + h, :], sB[:, 3, j, h:h + 1], True, True,
                               [("KTc", mb), ("sB", 3)], bs2, inc=(mb == 1 and h == 3))
                    A(lambda a, pb=pb, bs2=bs2: a.activation(out=sPb[:, pb, 1, :], in_=banks[bs2][:, 0:8], func=AF.Exp, scale=xsc),
                      [BK(bs2)], [("sP", pb, 1)])
                    for h in range(4):
                        for mb in range(2):
                            MM(acc[:, 128 + j * 4 + h:128 + j * 4 + h + 1], mVb[pb][:, mb * 4 + h, :],
                               sPb[:, pb, 1, mb * 4 + h:mb * 4 + h + 1], mb == 0, mb == 1, [("mV", pb, mb), ("sP", pb, 1)], acc_bank, inc=False)
                    for mb in range(2):
                        MM(acc[:, 192 + j * 4:192 + (j + 1) * 4], onesb[:, :], sPb[:, pb, 1, mb * 4:(mb + 1) * 4], mb == 0, mb == 1,
                           ["onesb", ("sP", pb, 1)], acc_bank, inc=(mb == 1))

                V(lambda v: v.tensor_tensor(out=S(15), in0=S(13), in1=S(2), op=ALU.mult), [SK(13), SK(2)], [SK(15)])
                V(lambda v: v.tensor_tensor(out=S(16), in0=S(9), in1=a_bc, op=ALU.mult), [SK(9), "sR"], [SK(16)])
                V(lambda v: v.tensor_tensor(out=S(15), in0=S(15), in1=S(16), op=ALU.add), [SK(15), SK(16)], [SK(15)])
                V(lambda v: v.scalar_tensor_tensor(out=SF(16), in0=SF(14), scalar=-1.0, in1=SF(14), op0=ALU.mult, op1=ALU.max),
                  [SK(14)], [SK(16)])
                V(lambda v: v.tensor_tensor(out=S(16), in0=S(16), in1=lb_bc, op=ALU.max), [SK(16), "sR"], [SK(16)])
                V(lambda v: v.reciprocal(out=SF(16), in_=SF(16)), [SK(16)], [SK(16)])
                V(lambda v: v.tensor_tensor(out=S(15), in0=S(15), in1=S(16), op=ALU.mult), [SK(15), SK(16)], [SK(15)])
                A(lambda a: a.activation(out=SF(16), in_=SF(15), func=AF.Square), [SK(15)], [SK(16)])
                bhn = nb()
                P.op("pe", lambda t: t.matmul(banks[bhn][:, 0:64], lhsT=onesf, rhs=SF(15), start=True, stop=True),
                     reads=["cf", SK(15)], writes=[BK(bhn)], inc=False)
                P.op("pe", lambda t: t.matmul(banks[bhn][:, 64:128], lhsT=onesf, rhs=SF(16), start=True, stop=True),
                     reads=["cf", SK(16)], writes=[BK(bhn)], acc=True)
                V(lambda v: v.tensor_scalar(out=SF(17), in0=banks[bhn][:, 0:64], scalar1=1.0 / 128, scalar2=None, op0=ALU.mult),
                  [BK(bhn)], [SK(17)])
                V(lambda v: v.tensor_tensor(out=SF(18), in0=SF(17), in1=SF(17), op=ALU.mult), [SK(17)], [SK(18)])
                V(lambda v: v.scalar_tensor_tensor(out=SF(18), in0=banks[bhn][:, 64:128], scalar=1.0 / 128, in1=SF(18),
                                                   op0=ALU.mult, op1=ALU.subtract), [BK(bhn), SK(18)], [SK(18)])
                V(lambda v: v.tensor_scalar(out=SF(18), in0=SF(18), scalar1=0.0, scalar2=HN_EPS, op0=ALU.max, op1=ALU.add),
                  [SK(18)], [SK(18)])
                A(lambda a: a.activation(out=SF(18), in_=SF(18), func=AF.Sqrt), [SK(18)], [SK(18)])
                V(lambda v: v.reciprocal(out=SF(18), in_=SF(18)), [SK(18)], [SK(18)])
                V(lambda v: v.tensor_tensor(out=SF(15), in0=SF(15), in1=SF(17), op=ALU.subtract), [SK(15), SK(17)], [SK(15)])
                V(lambda v: v.tensor_tensor(out=SF(15), in0=SF(15), in1=SF(18), op=ALU.mult), [SK(15), SK(18)], [SK(15)])
                V(lambda v: v.tensor_tensor(out=yabc[:, 0, :, 0:N], in0=S(15).rearrange("p j h -> p h j"),
                                            in1=S(3).rearrange("p j h -> p h j"), op=ALU.mult),
                  [SK(15), SK(3)], [("yabc", 0, h) for h in range(4)])
                V(lambda v: v.tensor_tensor(out=S(16), in0=S(8), in1=a_bc, op=ALU.mult), [SK(8), "sR"], [SK(16)])
                V(lambda v: v.tensor_tensor(out=S(17), in0=S(1), in1=w_bc, op=ALU.mult), [SK(1), "sR"], [SK(17)])
                V(lambda v: v.tensor_tensor(out=S(16), in0=S(16), in1=S(17), op=ALU.add), [SK(16), SK(17)], [SK(16)])
                bt = nb()
                TR(banks[bt][0:64, 0:128], SF(16), identf, [SK(16), "cf"], bt)
                V(lambda v, bt=bt: v.tensor_copy(out=sT[:, 1, :], in_=banks[bt][0:64, 0:128]), [BK(bt)], [("sT", 1)])
                P.dma("sp", lambda q: q.dma_start(out=o_ns[l].rearrange("j h k -> (j h) k"), in_=sT[:, 1, :]), reads=[("sT", 1)])
                G(lambda g: g.tensor_tensor(out=S(11), in0=S(4), in1=S(6)[:, :, 0:1].to_broadcast([128, 16, 4]), op=ALU.mult),
                  [SK(4), SK(6)], [SK(11)])
                bsn = nb()
                P.op("pe", lambda t: t.matmul(banks[bsn][:, 0:64], lhsT=bdf[:, :], rhs=SF(11), start=True, stop=True),
                     reads=["bdf", SK(11)], writes=[BK(bsn)])
                A(lambda a: a.activation(out=SF(12), in_=banks[bsn][:, 0:64], func=AF.Exp, scale=0.125), [BK(bsn)], [SK(12)])
                V(lambda v: v.tensor_tensor(out=S(13), in0=S(12), in1=S(7)[:, :, 0:1].to_broadcast([128, 16, 4]), op=ALU.mult),
                  [SK(12), SK(7)], [SK(13)])
                V(lambda v: v.tensor_tensor(out=S(13), in0=S(13), in1=psv(acc_bank, 0), op=ALU.add), [SK(13), BK(acc_bank)], [SK(13)])
                V(lambda v: v.tensor_tensor(out=S(17), in0=S(12), in1=psv(acc_bank, 64), op=ALU.add), [SK(12), BK(acc_bank)], [SK(17)])
                V(lambda v: v.tensor_tensor(out=S(17), in0=S(17), in1=sinkexp[:, l, :].unsqueeze(1).to_broadcast([128, 16, 4]), op=ALU.add),
                  [SK(17), "sinkexp"], [SK(17)])
                V(lambda v: v.reciprocal(out=SF(17), in_=SF(17)), [SK(17)], [SK(17)])
                V(lambda v: v.tensor_tensor(out=yabc[:, 1, :, 0:N], in0=S(13).rearrange("p j h -> p h j"),
                                            in1=S(17).rearrange("p j h -> p h j"), op=ALU.mult),
                  [SK(13), SK(17)], [("yabc", 1, g) for g in range(4)])
                V(lambda v: v.reciprocal(out=SF(18), in_=banks[acc_bank][:, 192:256]), [BK(acc_bank)], [SK(18)])
                V(lambda v: v.tensor_tensor(out=yabc[:, 2, :, 0:N], in0=psv(acc_bank, 128).rearrange("p j h -> p h j"),
                                            in1=S(18).rearrange("p j h -> p h j"), op=ALU.mult),
                  [BK(acc_bank), SK(18)], [("yabc", 2, h) for h in range(4)])
                reserved.discard(acc_bank)
                P.dma("pool", lambda q: q.dma_start(out=o_sks[l][:, 0:127, :], in_=csk[l][:, 1:128, :]))
                P.dma("pool", lambda q: q.dma_start(out=o_svs[l][:, 0:127, :], in_=csv[l][:, 1:128, :]))
                for slot, odst, r in ((6, o_sks, 2), (7, o_svs, 3)):
                    bt = nb()
                    TR(banks[bt][0:16, 0:128], S(slot)[:, :, 0], identf, [SK(slot), "cf"], bt)
                    V(lambda v, bt=bt, r=r: v.tensor_copy(out=sT[0:16, r, :], in_=banks[bt][0:16, 0:128]), [BK(bt)], [("sT", r)])
                    P.dma("sp", lambda q, odst=odst, r=r: q.dma_start(out=odst[l][:, 127, :], in_=sT[0:16, r, :]), reads=[("sT", r)])

            xin_s = fp[0:16, 0:2, :].rearrange("p a b -> p (a b)")
            P.dma("sp", lambda q: q.dma_start(out=xin_s, in_=xs[:, :]), writes=[("fp", 0), ("fp", 1)])
            b = nb()
            for c in range(8):
                TR(banks[b][:, c * 16:(c + 1) * 16], xin_s[:, c * 128:(c + 1) * 128], identf[0:16, 0:16], [("fp", 0), ("fp", 1), "cf"], b,
                   inc=(c == 7), first=(c == 0))
            V(lambda v, b=b: v.tensor_copy(out=xres[:, :, 0:N], in_=banks[b][:, 0:128].rearrange("p (c j) -> p c j", c=8)),
              [BK(b)], [("xres", c) for c in range(8)])
            A(lambda a, b=b: a.copy(out=xb[:, :, 0:N], in_=banks[b][:, 0:128].rearrange("p (c j) -> p c j", c=8)),
              [BK(b)], [("xb", c) for c in range(8)])
            rope_tables(PAST_LEN, N, const_pos=True)
            for l in range(DEPTH):
                sample_mixers(l)
                merge_and_ffn(l, N, last=(l == DEPTH - 1))
            yo = fp[0:16, 0:2, :].rearrange("p a b -> p (a b)")
            for half in range(2):
                b = nb()
                for cc in range(4):
                    c = half * 4 + cc
                    TR(banks[b][0:16, cc * 128:(cc + 1) * 128], xres[:, c, 0:N], identf, [("xres", c), "cf"], b, inc=(cc == 3), first=(cc == 0))
                V(lambda v, b=b, half=half: v.tensor_copy(out=yo[:, half * 512:(half + 1) * 512], in_=banks[b][0:16, :]), [BK(b)], [("fp", half)])
            P.dma("sp", lambda q: q.dma_start(out=ys[:, :], in_=yo), reads=[("fp", 0), ("fp", 1)])

        while cast_pending:
            cast_pending.pop(0)()
        if WITH_SAMPLE:
            sample_phase()

        P.final_wait("sp")
        P.emit(block)
    return nc


_CACHE = {}


def kernel(x_prompt, x_sample, mem_prompt, cache_swa_k, cache_swa_v, cache_mem_k, cache_mem_v,
           state_mlstm_c, state_mlstm_n, state_mlstm_m, w_in, b_gates, mlstm_norm_g, swa_sinks,
           w_mem_kv, w_branch, w_mix_out, ln1_g, ln1_b, w_ffn_in, w_ffn_out, ln2_g, ln2_b, _NT=8, _WITH_SAMPLE=True,
           _STOP=None, _NCORES=8):
    f = lambda a: np.ascontiguousarray(np.asarray(a, dtype=np.float32))
    n = 8
    key = (_NT, _WITH_SAMPLE, _STOP)
    if key not in _CACHE:
        _CACHE[key] = build_program(NT=_NT, WITH_SAMPLE=_WITH_SAMPLE, STOP=_STOP)
    nc = _CACHE[key]
    cfc = make_consts()
    shared = {
        "w_in": f(w_in), "b_gates": f(b_gates), "mlstm_norm_g": f(mlstm_norm_g), "swa_sinks": f(swa_sinks),
        "w_mem_kv": f(w_mem_kv), "w_branch": f(w_branch), "w_mix_out": f(w_mix_out), "ln1_g": f(ln1_g), "ln1_b": f(ln1_b),
        "w_ffn_in": f(w_ffn_in), "w_ffn_out": f(w_ffn_out), "ln2_g": f(ln2_g), "ln2_b": f(ln2_b), "cf_in": cfc,
    }
    in_maps = []
    PCORES = [0, 1, 4, 5]
    zx = np.zeros((SEQ, D), np.float32)
    zm = np.zeros((256, D), np.float32)
    for c in range(n):
        b = PCORES.index(c) if c in PCORES else None
        s0 = c * NS
        m = dict(shared)
        m["xp"] = f(x_prompt[b]) if b is not None else zx
        m["xs"] = f(x_sample[s0:s0 + NS, 0])
        m["memp"] = f(mem_prompt[b]) if b is not None else zm
        m["csk"] = f(cache_swa_k[:, s0:s0 + NS].reshape(DEPTH, NS, 128, 128))
        m["csv"] = f(cache_swa_v[:, s0:s0 + NS].reshape(DEPTH, NS, 128, 128))
        m["cmk"] = f(cache_mem_k[:, s0:s0 + NS].reshape(DEPTH, NS, 256, 512))
        m["cmv"] = f(cache_mem_v[:, s0:s0 + NS].reshape(DEPTH, NS, 256, 512))
        m["stc"] = f(state_mlstm_c[:, s0:s0 + NS])
        m["stn"] = f(state_mlstm_n[:, s0:s0 + NS])
        m["stm"] = f(state_mlstm_m[:, s0:s0 + NS])
        in_maps.append(m)
    if _NCORES < n:
        res = run_bass_kernel_spmd(nc, in_maps[:_NCORES], core_ids=list(range(_NCORES)))
        R = [res.results[PCORES.index(i) % _NCORES if i in PCORES else i % _NCORES] for i in range(n)]
    else:
        res = run_bass_kernel_spmd(nc, in_maps, core_ids=list(range(n)))
        R = res.results
    B = 4
    RP = [R[c] for c in PCORES]
    y_p = np.stack([RP[b]["yp"] for b in range(B)])
    y_s = np.concatenate([R[c]["ys"] for c in range(n)])[:, None, :]
    skp = np.stack([RP[b]["o_skp"] for b in range(B)], 1).reshape(DEPTH, B, 128, 2, 64)
    svp_ = np.stack([RP[b]["o_svp"] for b in range(B)], 1).reshape(DEPTH, B, 128, 2, 64)
    sks = np.concatenate([R[c]["o_sks"] for c in range(n)], 1).reshape(DEPTH, 128, 128, 2, 64)
    svs = np.concatenate([R[c]["o_svs"] for c in range(n)], 1).reshape(DEPTH, 128, 128, 2, 64)
    mkp = np.stack([RP[b]["o_mkp"] for b in range(B)], 1).reshape(DEPTH, B, 256, 4, 128)
    mvp = np.stack([RP[b]["o_mvp"] for b in range(B)], 1).reshape(DEPTH, B, 256, 4, 128)
    cp = np.stack([RP[b]["o_cp"] for b in range(B)], 1)
    np_ = np.stack([RP[b]["o_np"] for b in range(B)], 1)
    mp = np.stack([RP[b]["o_mp"] for b in range(B)], 1)
    cs = np.concatenate([R[c]["o_cs"] for c in range(n)], 1)
    ns = np.concatenate([R[c]["o_ns"] for c in range(n)], 1)
    ms = np.concatenate([R[c]["o_ms"] for c in range(n)], 1)
    return (y_p, y_s, skp, svp_, sks, svs, mkp, mvp, cp, np_, mp, cs, ns, ms)
```

```python
import math
from contextlib import ExitStack

import numpy as np
import concourse.bass as bass
import concourse.mybir as mybir
from concourse.bass_utils import run_bass_kernel_spmd

F32 = mybir.dt.float32
BF16 = mybir.dt.bfloat16
AF = mybir.ActivationFunctionType
ALU = mybir.AluOpType
AX = mybir.AxisListType

D = 1024
SEQ = 4096
T = 512
DEPTH = 2
NS = 16
D_IN = 6408
D_FF = 2816
NFC = 22
PAST_LEN = 8192
ALPHA = (2 * DEPTH) ** 0.25
LN_EPS = 1e-5
HN_EPS = 1e-6
MAGIC = 12582912.0
TWO_PI = 2.0 * math.pi

ENG_NAMES = ["pe", "act", "dve", "pool", "sp"]

C_MQ, C_MK, C_MV, C_MO, C_G, C_SQ, C_SK, C_SV, C_XQ, C_GL = 0, 512, 1024, 1536, 2048, 2056, 2568, 2696, 2824, 3336


class Prog:
    def __init__(self, nc, n_dma_sems=32):
        self.nc = nc
        self.ops = {e: [] for e in ENG_NAMES}
        self.cnt = {e: 0 for e in ENG_NAMES}
        self.pending = {e: False for e in ENG_NAMES}
        self.sem = {}
        self.dma_sems = []
        self.dma_cnt = []
        self.n_dma_sems = n_dma_sems
        self.dma_rr = 0
        self.cast_rr = 0
        self.pool_rr = 0
        self.waited = {}
        self.last_w = {}
        self.readers = {}
        self.stopped = False

    def open(self, stack):
        for e in ENG_NAMES:
            self.sem[e] = stack.enter_context(self.nc.semaphore("s_" + e))
        for i in range(self.n_dma_sems):
            self.dma_sems.append(stack.enter_context(self.nc.semaphore(f"s_dma{i}")))
            self.dma_cnt.append(0)

    def _need(self, e, ev, waits):
        if ev is None:
            return
        semkey, val = ev
        if semkey == e and val > self.cnt[e]:
            assert e == "pe", (e, ev)
            return
        if self.waited.get((e, semkey), 0) >= val:
            return
        if val > waits.get(semkey, 0):
            waits[semkey] = val

    def _deps(self, e, reads, writes, acc):
        waits = {}
        for k in reads:
            self._need(e, self.last_w.get(k), waits)
            if isinstance(k, tuple) and k[0] == "bank":
                for ev in self.readers.get(k, []):
                    if ev[0] != e:
                        self._need(e, ev, waits)
        for k in writes:
            lw = self.last_w.get(k)
            if not (acc and lw is not None and lw[0] == e):
                self._need(e, lw, waits)
            for ev in self.readers.get(k, []):
                if ev[0] == e and e == "pe":
                    continue
                self._need(e, ev, waits)
        for semkey, val in waits.items():
            self.waited[(e, semkey)] = val
        return waits

    def _commit(self, ev, reads, writes):
        for k in reads:
            self.readers.setdefault(k, []).append(ev)
        for k in writes:
            self.last_w[k] = ev
            self.readers[k] = []

    def op(self, e, fn, reads=(), writes=(), acc=False, inc=True):
        if self.stopped:
            return None
        waits = self._deps(e, reads, writes, acc)
        if inc:
            self.cnt[e] += 1
            ev = (e, self.cnt[e])
            self.pending[e] = False
        else:
            ev = (e, self.cnt[e] + 1)
            self.pending[e] = True
        self.ops[e].append((fn, waits, ("eng", e) if inc else None))
        self._commit(ev, reads, writes)
        return ev

    def dma(self, q, fn, reads=(), writes=(), cast=False):
        if self.stopped:
            return None
        if cast:
            i = self.cast_rr
            self.cast_rr = (self.cast_rr + 1) % 4
        elif q == "pool":
            i = 4 + self.pool_rr
            self.pool_rr = (self.pool_rr + 1) % 4
        else:
            i = 8 + self.dma_rr
            self.dma_rr = (self.dma_rr + 1) % (self.n_dma_sems - 8)
        semkey = ("dma", i)
        waits = self._deps(q, reads, writes, False)
        if self.dma_cnt[i] > 0:
            self._need(q, (semkey, self.dma_cnt[i]), waits)
            self.waited[(q, semkey)] = max(self.waited.get((q, semkey), 0), self.dma_cnt[i])
        self.dma_cnt[i] += 16
        ev = (semkey, self.dma_cnt[i])
        self.ops[q].append((fn, waits, ("dma", i)))
        self._commit(ev, reads, writes)
        return ev

    def _semh(self, semkey):
        if isinstance(semkey, tuple):
            return self.dma_sems[semkey[1]]
        return self.sem[semkey]

    def barrier(self):
        if self.stopped:
            return
        for e in ENG_NAMES:
            assert not self.pending[e], e
        for e in ENG_NAMES:
            waits = {}
            for i in range(self.n_dma_sems):
                if self.dma_cnt[i] > 0:
                    self._need(e, (("dma", i), self.dma_cnt[i]), waits)
            for en in ENG_NAMES:
                if en != e and self.cnt[en] > 0:
                    self._need(e, (en, self.cnt[en]), waits)
            for semkey, val in waits.items():
                self.waited[(e, semkey)] = val
            self.ops[e].append((None, waits, None))

    def final_wait(self, e="sp"):
        waits = {}
        for i in range(self.n_dma_sems):
            if self.dma_cnt[i] > 0:
                waits[("dma", i)] = self.dma_cnt[i]
        for en in ENG_NAMES:
            assert not self.pending[en], en
            if en != e and self.cnt[en] > 0:
                waits[en] = self.cnt[en]
        self.ops[e].append((None, waits, None))

    def emit(self, block):
        prog = self

        def body(ename):
            def f(engine):
                for fn, waits, kind in prog.ops[ename]:
                    for semkey, val in waits.items():
                        engine.wait_ge(prog._semh(semkey), val)
                    if fn is None:
                        continue
                    inst = fn(engine)
                    if kind is None:
                        continue
                    if kind[0] == "eng":
                        inst.then_inc(prog.sem[ename], 1)
                    else:
                        inst.then_inc(prog.dma_sems[kind[1]], 16)
            return f

        block.tensor(body("pe"))
        block.scalar(body("act"))
        block.vector(body("dve"))
        block.gpsimd(body("pool"))
        block.sync(body("sp"))


CF_IDENT, CF_ONES, CF_TRI, CF_RR, CF_MLE, CF_MGE, CF_IOTA, CF_INVF = 0, 128, 256, 384, 512, 640, 768, 1280
CF_W = 1281


def make_consts():
    c = np.zeros((128, CF_W), np.float32)
    i = np.arange(128)
    c[:, CF_IDENT:CF_IDENT + 128] = np.eye(128, dtype=np.float32)
    c[:, CF_ONES:CF_ONES + 128] = 1.0
    c[:, CF_TRI:CF_TRI + 128] = (i[:, None] <= i[None, :]).astype(np.float32)
    rr = np.zeros((128, 128), np.float32)
    for base in (0, 64):
        for j in range(8):
            rr[base + j + 8, base + j] = -1.0
            rr[base + j, base + j + 8] = 1.0
    c[:, CF_RR:CF_RR + 128] = rr
    c[:, CF_MLE:CF_MLE + 128] = np.where(i[:, None] <= i[None, :], 0.0, -30000.0)
    c[:, CF_MGE:CF_MGE + 128] = np.where(i[:, None] >= i[None, :], 0.0, -30000.0)
    c[:, CF_IOTA:CF_IOTA + 512] = np.arange(512, dtype=np.float32)[None, :]
    invf = np.zeros(128, np.float32)
    for base in (0, 64):
        for j in range(8):
            f = np.float32(500000.0) ** (-np.float32(j) / np.float32(8))
            invf[base + j] = f
            invf[base + j + 8] = f
    c[:, CF_INVF] = invf
    return c


class _Stop(Exception):
    pass


def build_program(NT=8, WITH_SAMPLE=True, NRING=4, STOP=None):
    nc = bass.Bass("TRN2", target_bir_lowering=False)

    stage_ref = []

    def stage(n):
        if STOP is not None and n >= STOP:
            stage_ref[0].stopped = True

    def din(name, shape):
        return nc.dram_tensor(name, list(shape), F32, kind="ExternalInput").ap()

    def dout(name, shape):
        return nc.dram_tensor(name, list(shape), F32, kind="ExternalOutput").ap()

    def dscr(name, shape, dt=BF16):
        return nc.dram_tensor(name, list(shape), dt, kind="Internal").ap()

    xp = din("xp", [SEQ, D])
    xs = din("xs", [NS, D])
    memp = din("memp", [256, D])
    csk = din("csk", [DEPTH, NS, 128, 128])
    csv = din("csv", [DEPTH, NS, 128, 128])
    cmk = din("cmk", [DEPTH, NS, 256, 512])
    cmv = din("cmv", [DEPTH, NS, 256, 512])
    stc = din("stc", [DEPTH, NS, 4, 128, 128])
    stn = din("stn", [DEPTH, NS, 4, 128])
    stm = din("stm", [DEPTH, NS, 4])
    w_in = din("w_in", [DEPTH, D, D_IN])
    b_gates = din("b_gates", [DEPTH, 8])
    normg_d = din("mlstm_norm_g", [DEPTH, 512])
    sinks_d = din("swa_sinks", [DEPTH, 8])
    w_mem = din("w_mem_kv", [DEPTH, D, 1024])
    w_br = din("w_branch", [DEPTH, 3, 512, D])
    w_mix = din("w_mix_out", [DEPTH, D, D])
    ln1g = din("ln1_g", [DEPTH, D])
    ln1b = din("ln1_b", [DEPTH, D])
    w_fi = din("w_ffn_in", [DEPTH, D, 2 * D_FF])
    w_fo = din("w_ffn_out", [DEPTH, D_FF, D])
    ln2g = din("ln2_g", [DEPTH, D])
    ln2b = din("ln2_b", [DEPTH, D])
    cf_d = din("cf_in", [128, CF_W])

    yp = dout("yp", [SEQ, D])
    ys = dout("ys", [NS, D])
    o_skp = dout("o_skp", [DEPTH, 128, 128])
    o_svp = dout("o_svp", [DEPTH, 128, 128])
    o_sks = dout("o_sks", [DEPTH, NS, 128, 128])
    o_svs = dout("o_svs", [DEPTH, NS, 128, 128])
    o_mkp = dout("o_mkp", [DEPTH, 256, 512])
    o_mvp = dout("o_mvp", [DEPTH, 256, 512])
    o_cp = dout("o_cp", [DEPTH, 4, 128, 128])
    o_np = dout("o_np", [DEPTH, 4, 128])
    o_mp = dout("o_mp", [DEPTH, 4])
    o_cs = dout("o_cs", [DEPTH, NS, 4, 128, 128])
    o_ns = dout("o_ns", [DEPTH, NS, 4, 128])
    o_ms = dout("o_ms", [DEPTH, NS, 4])

    wb_in = dscr("wb_in", [DEPTH, 128, 8, D_IN])
    wb_squ = dscr("wb_squ", [DEPTH, 128, 8, 512])
    wb_mem = dscr("wb_mem", [DEPTH, 128, 8, 1024])
    wb_br = dscr("wb_br", [DEPTH, 3, 128, 4, D])
    wb_mix = dscr("wb_mix", [DEPTH, 128, 8, D])
    wb_fi = dscr("wb_fi", [DEPTH, 128, 8, 2 * D_FF])
    wb_fo = dscr("wb_fo", [DEPTH, 128, NFC, D])

    st = ExitStack()
    with st:
        P = Prog(nc)
        P.open(st)
        stage_ref.append(P)

        def sb(name, shape, dt=F32):
            return st.enter_context(nc.sbuf_tensor(name, list(shape), dt))

        cf = sb("cf", [128, CF_W])
        identb = sb("identb", [128, 128], BF16)
        onesb = sb("onesb", [128, 128], BF16)
        invdb = sb("invdb", [128, 128], BF16)
        trib = sb("trib", [128, 128], BF16)
        mneg = sb("mneg", [128, 2, 4, 128], BF16)
        onespad = sb("onespad", [128, 2, 128], BF16)
        prow = sb("prow", [72, 128])
        pcol = sb("pcol", [128, 72])
        bgate = sb("bgate", [128, DEPTH, 8])
        sinkexp = sb("sinkexp", [128, DEPTH, 4])
        wg = sb("wg", [128, DEPTH, 8, 8], BF16)
        memT = sb("memT", [128, 8, 256], BF16)
        memKT = sb("memKT", [128, DEPTH, 4, 256], BF16)
        memV = sb("memV", [128, DEPTH, 2, 512], BF16)
        CT = sb("CT", [128, DEPTH, 4, 129])
        CTs = sb("CTs", [128, 4, 129])
        CTb = sb("CTb", [128, 4, 129], BF16)
        mst = sb("mst", [128, DEPTH, 4])
        skT = sb("skT", [128, DEPTH, 5, 128], BF16)
        svp = sb("svp", [128, DEPTH, 5, 2, 128], BF16)
        xres = sb("xres", [128, 8, T])
        xb = sb("xb", [128, 8, T], BF16)
        fp = sb("fp", [128, 8, T])
        bp = sb("bp", [128, 24, T], BF16)
        bq = sb("bq", [128, 16, T], BF16)
        yabc = sb("yabc", [128, 3, 4, T], BF16)
        vaug = sb("vaug", [128, 4, 4, 129], BF16)
        smt = sb("smt", [128, 2, 4, 128], BF16)
        gsm = sb("gsm", [128, 16, 16])
        hh = sb("hh", [128, 2, 4, 128])
        bnst = sb("bnst", [128, 4, 6])
        bnag = sb("bnag", [128, 4, 2])
        tmpf = sb("tmpf", [128, 2, T])
        ring = sb("ring", [128, NRING, 4096], BF16)
        rtab = sb("rtab", [128, 2, T])
        onespadf = sb("onespadf", [128, 2, 128])
        bdf = sb("bdf", [128, 128])
        sx = sb("sx", [16, 48])
        sD = sb("sD", [16, 16, 12])
        sR = sb("sR", [128, 16, 12])
        sS = sb("sS", [128, 24, 16, 4])
        sP = sb("sP", [128, 2, 2, 8])
        sT = sb("sT", [64, 4, 128])
        sB = sb("sB", [128, 4, 16, 4], BF16)
        sPb = sb("sPb", [128, 2, 2, 8], BF16)

        banks = [st.enter_context(nc.psum_tensor(f"bank{i}", [128, 512], F32)) for i in range(8)]
        block = st.enter_context(nc.Block())

        bank_rr = [0]
        reserved = set()

        def nb():
            while True:
                i = bank_rr[0]
                bank_rr[0] = (i + 1) % 8
                if i not in reserved:
                    return i

        def BK(i):
            return ("bank", i)

        def V(fn, reads=(), writes=(), inc=True):
            return P.op("dve", fn, reads, writes, inc=inc)

        def A(fn, reads=(), writes=(), inc=True):
            return P.op("act", fn, reads, writes, inc=inc)

        def G(fn, reads=(), writes=(), inc=True):
            return P.op("pool", fn, reads, writes, inc=inc)

        def MM(out, lhsT, rhs, start, stop, reads, bank, inc=None):
            if inc is None:
                inc = stop
            return P.op("pe", lambda t: t.matmul(out, lhsT=lhsT, rhs=rhs, start=start, stop=stop),
                        reads=reads, writes=[BK(bank)], acc=not start, inc=inc)

        def TR(out, in_, ident, reads, bank, inc=True, first=True):
            return P.op("pe", lambda t: t.transpose(out=out, in_=in_, identity=ident),
                        reads=reads, writes=[BK(bank)], acc=not first, inc=inc)

        evac_rr = [0]

        def evac_copy(out, in_, reads, writes):
            evac_rr[0] ^= 1
            if evac_rr[0]:
                V(lambda v: v.tensor_copy(out=out, in_=in_), reads, writes)
            else:
                A(lambda a: a.copy(out=out, in_=in_), reads, writes)

        ring_rr = [0]

        cast_reg = {}
        cast_pending = []
        cast_tickc = [0]

        def reg_cast(name, c0, c1, key):
            cast_reg.setdefault(name, []).append((c0, c1, key))

        def cast_covered(name, c0, c1):
            iv = sorted((a, b) for (a, b, k) in cast_reg.get(name, []) if a < c1 and b > c0)
            pos = c0
            for a, b in iv:
                if a > pos:
                    return False
                pos = max(pos, b)
            return pos >= c1

        def cast_keys(name, c0, c1):
            while not cast_covered(name, c0, c1):
                assert cast_pending, (name, c0, c1)
                cast_pending.pop(0)()
            return [k for (a, b, k) in cast_reg[name] if a < c1 and b > c0]

        def cast_tick(every=3):
            if cast_pending:
                cast_tickc[0] += 1
                if cast_tickc[0] % every == 0:
                    cast_pending.pop(0)()

        def wload(src_ap, shape, srckeys):
            s = ring_rr[0]
            ring_rr[0] = (s + 1) % NRING
            a, b = shape
            dst = ring[:, s, 0:a * b].rearrange("p (a b) -> p a b", a=a)
            P.dma("sp", lambda q: q.dma_start(out=dst, in_=src_ap), reads=list(srckeys), writes=[("ring", s)])
            return dst, ("ring", s)

        P.dma("sp", lambda q: q.dma_start(out=cf[:], in_=cf_d[:, :]), writes=["cf"])
        identf = cf[:, CF_IDENT:CF_IDENT + 128]
        onesf = cf[:, CF_ONES:CF_ONES + 128]
        trif = cf[:, CF_TRI:CF_TRI + 128]
        rrf = cf[:, CF_RR:CF_RR + 128]
        iotaf = cf[:, CF_IOTA:CF_IOTA + 512]
        invf = cf[:, CF_INVF:CF_INVF + 1]
        V(lambda v: v.tensor_copy(out=identb[:], in_=identf), ["cf"], ["identb"])
        V(lambda v: v.tensor_copy(out=onesb[:], in_=onesf), ["cf"], ["onesb"])
        V(lambda v: v.tensor_scalar(out=invdb[:], in0=onesf, scalar1=1.0 / D, scalar2=None, op0=ALU.mult), ["cf"], ["invdb"])
        V(lambda v: v.tensor_copy(out=trib[:], in_=trif), ["cf"], ["trib"])
        for k, off in enumerate((CF_MLE, CF_MGE)):
            V(lambda v, k=k, off=off: v.tensor_copy(
                out=mneg[:, k], in_=cf[:, off:off + 128].unsqueeze(1).to_broadcast([128, 4, 128])),
              ["cf"], ["mneg"])
        G(lambda g: g.memset(onespad[:], 0.0), [], ["onespad"])
        G(lambda g: g.memset(onespad[:, 0, 0:64], 1.0), ["onespad"], ["onespad"])
        G(lambda g: g.memset(onespad[:, 1, 64:128], 1.0), ["onespad"], ["onespad"])
        G(lambda g: g.memset(onespadf[:], 0.0), [], ["onespadf"])
        G(lambda g: g.memset(onespadf[:, 0, 0:64], 1.0), ["onespadf"], ["onespadf"])
        G(lambda g: g.memset(onespadf[:, 1, 64:128], 1.0), ["onespadf"], ["onespadf"])
        G(lambda g: g.memset(bdf[:], 0.0), [], ["bdf"])
        G(lambda g: g.memset(bdf[0:64, 0:64], 1.0), ["bdf"], ["bdf"])
        G(lambda g: g.memset(bdf[64:128, 64:128], 1.0), ["bdf"], ["bdf"])
        G(lambda g: g.memset(CT[:], 0.0), [], ["CT"])
        G(lambda g: g.memset(mst[:], 0.0), [], ["mst"])
        G(lambda g: g.memset(skT[:], 0.0), [], ["skT"])
        G(lambda g: g.memset(svp[:], 0.0), [], ["svp"])

        for l in range(DEPTH):
            for k, src in enumerate((ln1g, ln1b, ln2g, ln2b)):
                r0 = (l * 4 + k) * 8
                P.dma("sp", lambda q, src=src, l=l, r0=r0: q.dma_start(
                    out=prow[r0:r0 + 8, :], in_=src[l, :].rearrange("(c p) -> c p", p=128)), writes=["prow"])
            P.dma("sp", lambda q, l=l: q.dma_start(
                out=prow[64 + l * 4:64 + l * 4 + 4, :], in_=normg_d[l, :].rearrange("(h p) -> h p", p=128)),
                writes=["prow"])
            P.dma("sp", lambda q, l=l: q.dma_start(out=bgate[:, l, :], in_=b_gates[l:l + 1, :].to_broadcast([128, 8])),
                  writes=["bgate"])
            for kv in range(2):
                P.dma("sp", lambda q, l=l, kv=kv: q.dma_start(
                    out=sinkexp[kv * 64:(kv + 1) * 64, l, :],
                    in_=sinks_d[l:l + 1, kv * 4:(kv + 1) * 4].to_broadcast([64, 4])), writes=["sinkexp"])
            P.dma("pool", lambda q, l=l: q.dma_start(
                out=wg[:, l], in_=w_in[l, :, C_G:C_G + 8].rearrange("(c p) j -> p c j", p=128)), writes=["wg"])
        A(lambda a: a.activation(out=sinkexp[:], in_=sinkexp[:], func=AF.Exp), ["sinkexp"], ["sinkexp"])
        b0 = nb()
        TR(banks[b0][:, 0:72], prow[:, :], identf[0:72, 0:72], ["prow", "cf"], b0)
        V(lambda v: v.tensor_copy(out=pcol[:], in_=banks[b0][:, 0:72]), [BK(b0)], ["pcol"])

        def lnp(l, k, c):
            i = (l * 4 + k) * 8 + c
            return pcol[:, i:i + 1]

        def normg(l, h):
            i = 64 + l * 4 + h
            return pcol[:, i:i + 1]

        stage(1)
        def cast_cols(name, dst, src, c0, c1, eager=False):
            def emit():
                key = ("cast", name, c0)
                P.dma("pool", lambda q: q.dma_start(
                    out=dst[:, :, c0:c1], in_=src[:, c0:c1].rearrange("(c p) j -> p c j", p=128)), writes=[key], cast=True)
                reg_cast(name, c0, c1, key)
            if eager:
                emit()
            else:
                cast_pending.append(emit)

        for l in range(DEPTH):
            for c0 in range(0, 1024, 512):
                cast_cols(("mem", l), wb_mem[l], w_mem[l], c0, c0 + 512, eager=True)
        for l in range(DEPTH):
            P.dma("pool", lambda q, l=l: q.dma_start(
                out=wb_squ[l], in_=w_in[l, :, C_SQ:C_SQ + 512].rearrange("(c p) j -> p c j", p=128)),
                writes=[("cast", "squ", l)], cast=True)
        for l in range(DEPTH):
            s1 = bq[:, 0:8, :]
            s2 = bq[:, 8:16, :]
            k1 = [("bq", c) for c in range(8)]
            k2 = [("bq", 8 + c) for c in range(8)]
            P.dma("pool", lambda q, l=l, s1=s1: q.dma_start(out=s1, in_=wb_squ[l]), reads=[("cast", "squ", l)], writes=k1)
            for kv in range(2):
                G(lambda g, s1=s1, s2=s2, kv=kv: g.tensor_copy(
                    out=s2.rearrange("p c (g k e) -> p c g k e", g=4, k=2)[:, :, :, kv, :],
                    in_=s1.rearrange("p c (k g e) -> p c k g e", k=2, g=4)[:, :, kv]), k1, k2)
            key = ("cast", ("in", l), C_SQ, "perm")
            P.dma("pool", lambda q, l=l, s2=s2: q.dma_start(out=wb_in[l][:, :, C_SQ:C_SQ + 512], in_=s2), reads=k2, writes=[key])
            reg_cast(("in", l), C_SQ, C_SQ + 512, key)
        for l in range(DEPTH):
            for (c0, c1) in ((0, 512), (512, 1024), (1024, 1536), (1536, 2048)):
                cast_cols(("in", l), wb_in[l], w_in[l], c0, c1)
            cast_cols(("in", l), wb_in[l], w_in[l], C_SK, C_SK + 256)
            cast_cols(("in", l), wb_in[l], w_in[l], C_XQ, C_XQ + 512)
            for a0 in range(C_GL, D_IN, 512):
                cast_cols(("in", l), wb_in[l], w_in[l], a0, a0 + 512)
            for r in range(3):
                for hseg in range(2):
                    if r != 1:
                        cast_cols(("br", l, r), wb_br[l, r], w_br[l, r], hseg * 512, (hseg + 1) * 512)
                    else:
                        def emit_b1(l=l, hseg=hseg):
                            for kv in range(2):
                                key = ("cast", ("br", l, 1), hseg, kv)
                                P.dma("pool", lambda q, kv=kv: q.dma_start(
                                    out=wb_br[l, 1][kv * 64:(kv + 1) * 64, :, hseg * 512:(hseg + 1) * 512],
                                    in_=w_br[l, 1, kv * 256:(kv + 1) * 256, hseg * 512:(hseg + 1) * 512].rearrange(
                                        "(g e) j -> e g j", g=4)), writes=[key], cast=True)
                                reg_cast(("br", l, 1), hseg * 512 + kv, (hseg + 1) * 512 if kv == 1 else hseg * 512 + 1, key)
                        cast_pending.append(emit_b1)
            for c0 in range(0, D, 512):
                cast_cols(("mix", l), wb_mix[l], w_mix[l], c0, c0 + 512)
            for c0 in range(0, 2 * D_FF, 256):
                cast_cols(("fi", l), wb_fi[l], w_fi[l], c0, c0 + 256)
            for c0 in range(0, D, 256):
                for kh in range(2):
                    def emit_fo(l=l, c0=c0, kh=kh):
                        key = ("cast", ("fo", l, kh), c0)
                        P.dma("pool", lambda q: q.dma_start(
                            out=wb_fo[l][:, kh * 11:(kh + 1) * 11, c0:c0 + 256],
                            in_=w_fo[l, kh * 11 * 128:(kh + 1) * 11 * 128, c0:c0 + 256].rearrange("(c p) j -> p c j", p=128)),
                            writes=[key], cast=True)
                        reg_cast(("fo", l, kh), c0, c0 + 256, key)
                    cast_pending.append(emit_fo)

        for _ in range(7):
            cast_pending.pop(0)()
        stage(2)
        for mb in range(2):
            P.dma("sp", lambda q, mb=mb: q.dma_start(out=fp[:, 2 * mb:2 * mb + 2, :].rearrange("p a b -> p (a b)"),
                                                     in_=memp[mb * 128:(mb + 1) * 128, :]),
                  writes=[("fp", 2 * mb), ("fp", 2 * mb + 1)])
        for c in range(8):
            b = nb()
            for mb in range(2):
                src = fp[:, 2 * mb:2 * mb + 2, :].rearrange("p a b -> p (a b)")[:, c * 128:(c + 1) * 128]
                TR(banks[b][:, mb * 128:(mb + 1) * 128], src, identf, [("fp", 2 * mb), ("fp", 2 * mb + 1), "cf"], b,
                   inc=(mb == 1), first=(mb == 0))
            evac_copy(memT[:, c, :], banks[b][:, 0:256], [BK(b)], [("memT", c)])
        for l in range(DEPTH):
            for half in range(2):
                wsl, wkey = wload(wb_mem[l][:, :, half * 512:(half + 1) * 512], (8, 512), cast_keys(("mem", l), half * 512, (half + 1) * 512))
                if half == 0:
                    for h in range(4):
                        b = nb()
                        for c in range(8):
                            MM(banks[b][:, 0:256], wsl[:, c, h * 128:(h + 1) * 128], memT[:, c, :], c == 0, c == 7,
                               [wkey, ("memT", c)], b)
                        evac_copy(memKT[:, l, h, :], banks[b][:, 0:256], [BK(b)], [("memKT", l)])
                for mb in range(2):
                    b = nb()
                    for c in range(8):
                        MM(banks[b][:, :], memT[:, c, mb * 128:(mb + 1) * 128], wsl[:, c, :], c == 0, c == 7,
                           [wkey, ("memT", c)], b)
                    stg = tmpf[:, mb, :]
                    V(lambda v, stg=stg, b=b: v.tensor_copy(out=stg, in_=banks[b][:, :]), [BK(b)], [("tmpf", mb)])
                    odst = (o_mkp if half == 0 else o_mvp)[l, mb * 128:(mb + 1) * 128, :]
                    P.dma("sp", lambda q, odst=odst, stg=stg: q.dma_start(out=odst, in_=stg), reads=[("tmpf", mb)])
                    if half == 1:
                        A(lambda a, l=l, mb=mb, b=b: a.copy(out=memV[:, l, mb, :], in_=banks[b][:, :]), [BK(b)],
                          [("memV", l)])

        stage(3)
        def fm_group(wsl, wkey, ncks, col0, nchunks, src, srckeys, N, evac, lsel=None, after=None):
            for j in range(nchunks):
                b = nb()
                for c in range(ncks):
                    lt = wsl[:, c, col0 + j * 128:col0 + (j + 1) * 128] if lsel is None else lsel(wsl, c, j)
                    MM(banks[b][:, 0:N], lt, src(c), c == 0, c == ncks - 1, [wkey, srckeys(c)], b)
                evac(j, b, banks[b][:, 0:N])
                cast_tick()
                if after is not None:
                    after(j)

        def sq_sel(wsl, c, g):
            return wsl[:, c, :].rearrange("p (k g e) -> p g k e", k=2, g=4)[:, g]

        def xb_src(N):
            return (lambda c: xb[:, c, 0:N]), (lambda c: ("xb", c))

        def rope_tables(base, N, const_pos=False):
            ang = tmpf[:, 0, 0:N]
            kk = tmpf[:, 1, 0:N]
            cosb = rtab[:, 0, 0:N]
            sinb = rtab[:, 1, 0:N]
            if const_pos:
                V(lambda v: v.tensor_scalar(out=ang, in0=iotaf[:, 0:N], scalar1=0.0, scalar2=float(base),
                                            op0=ALU.mult, op1=ALU.add), ["cf"], [("tmpf", 0)])
                V(lambda v: v.tensor_scalar(out=ang, in0=ang, scalar1=invf, scalar2=None, op0=ALU.mult),
                  [("tmpf", 0), "cf"], [("tmpf", 0)])
            else:
                V(lambda v: v.tensor_scalar(out=ang, in0=iotaf[:, 0:N], scalar1=float(base), scalar2=invf,
                                            op0=ALU.add, op1=ALU.mult), ["cf"], [("tmpf", 0)])
            V(lambda v: v.tensor_scalar(out=kk, in0=ang, scalar1=1.0 / TWO_PI, scalar2=MAGIC,
                                        op0=ALU.mult, op1=ALU.add), [("tmpf", 0)], [("tmpf", 1)])
            V(lambda v: v.tensor_scalar(out=kk, in0=kk, scalar1=-MAGIC, scalar2=None, op0=ALU.add),
              [("tmpf", 1)], [("tmpf", 1)])
            c1 = 6.28125
            c2 = float(np.float32(TWO_PI - 6.28125))
            c3 = float(TWO_PI - 6.28125 - c2)
            for cc in (c1, c2, c3):
                V(lambda v, cc=cc: v.scalar_tensor_tensor(out=ang, in0=kk, scalar=-cc, in1=ang, op0=ALU.mult, op1=ALU.add),
                  [("tmpf", 1), ("tmpf", 0)], [("tmpf", 0)])
            V(lambda v: v.tensor_scalar(out=sinb, in0=ang, scalar1=3.1415925, scalar2=-3.1415925, op0=ALU.min, op1=ALU.max),
              [("tmpf", 0)], [("rtab", 1)])
            A(lambda a: a.activation(out=sinb, in_=sinb, func=AF.Sin), [("rtab", 1)], [("rtab", 1)])
            V(lambda v: v.tensor_scalar(out=ang, in0=ang, scalar1=math.pi / 2, scalar2=None, op0=ALU.add),
              [("tmpf", 0)], [("tmpf", 0)])
            V(lambda v: v.tensor_single_scalar(out=kk, in_=ang, scalar=math.pi, op=ALU.is_gt), [("tmpf", 0)], [("tmpf", 1)])
            V(lambda v: v.scalar_tensor_tensor(out=ang, in0=kk, scalar=-TWO_PI, in1=ang, op0=ALU.mult, op1=ALU.add),
              [("tmpf", 1), ("tmpf", 0)], [("tmpf", 0)])
            V(lambda v: v.tensor_scalar(out=cosb, in0=ang, scalar1=3.1415925, scalar2=-3.1415925, op0=ALU.min, op1=ALU.max),
              [("tmpf", 0)], [("rtab", 0)])
            A(lambda a: a.activation(out=cosb, in_=cosb, func=AF.Sin), [("rtab", 0)], [("rtab", 0)])

        def rope_apply(psum_ap, b, N, q32k, out_ap, out_keys, out32=None):
            q32 = fp[:, q32k, 0:N]
            evac_copy(q32, psum_ap, [BK(b)], [("fp", q32k)])
            b2 = nb()
            P.op("pe", lambda t: t.matmul(banks[b2][:, 0:N], lhsT=rrf, rhs=q32, start=True, stop=True),
                 reads=["cf", ("fp", q32k)], writes=[BK(b2)])
            rs = tmpf[:, 1, 0:N]
            V(lambda v: v.tensor_tensor(out=rs, in0=banks[b2][:, 0:N], in1=rtab[:, 1, 0:N], op=ALU.mult),
              [BK(b2), ("rtab", 1)], [("tmpf", 1)])
            G(lambda g: g.tensor_tensor(out=q32, in0=q32, in1=rtab[:, 0, 0:N], op=ALU.mult),
              [("fp", q32k), ("rtab", 0)], [("fp", q32k)])
            if out32 is not None:
                o32, o32k = out32
                G(lambda g: g.tensor_tensor(out=o32, in0=q32, in1=rs, op=ALU.add), [("fp", q32k), ("tmpf", 1)], [o32k])
                G(lambda g: g.tensor_copy(out=out_ap, in_=o32), [o32k], out_keys)
            else:
                G(lambda g: g.tensor_tensor(out=out_ap, in0=q32, in1=rs, op=ALU.add), [("fp", q32k), ("tmpf", 1)], out_keys)

        class LNStats:
            def __init__(self, N):
                self.N = N
                self.b1 = nb()
                reserved.add(self.b1)
                self.b2 = nb()
                reserved.add(self.b2)

            def feed(self, c):
                N = self.N
                vb = bq[:, c % 2, 0:N]
                sqb = bq[:, 2 + c % 2, 0:N]
                A(lambda a: a.copy(out=vb, in_=xres[:, c, 0:N]), [("xres", c)], [("bq", c % 2)])
                A(lambda a: a.activation(out=sqb, in_=xres[:, c, 0:N], func=AF.Square), [("xres", c)], [("bq", 2 + c % 2)])
                MM(banks[self.b1][:, 0:N], invdb[:, :], vb, c == 0, c == 7, ["invdb", ("bq", c % 2)], self.b1, inc=(c == 7))
                MM(banks[self.b2][:, 0:N], invdb[:, :], sqb, c == 0, c == 7, ["invdb", ("bq", 2 + c % 2)], self.b2, inc=(c == 7))

        def layer_norm(l, which, N, last, stats):
            kg, kb_ = (0, 1) if which == 1 else (2, 3)
            b1, b2 = stats.b1, stats.b2
            mean = fp[:, 3, 0:N]
            var = fp[:, 4, 0:N]
            rstd = fp[:, 5, 0:N]
            nmr = fp[:, 6, 0:N]
            A(lambda a: a.activation(out=mean, in_=banks[b1][:, 0:N], func=AF.Square), [BK(b1)], [("fp", 3)])
            V(lambda v: v.tensor_tensor(out=var, in0=banks[b2][:, 0:N], in1=mean, op=ALU.subtract), [BK(b2), ("fp", 3)], [("fp", 4)])
            A(lambda a: a.activation(out=var, in_=var, func=AF.Ln, bias=LN_EPS), [("fp", 4)], [("fp", 4)])
            A(lambda a: a.activation(out=rstd, in_=var, func=AF.Exp, scale=-0.5), [("fp", 4)], [("fp", 5)])
            V(lambda v: v.scalar_tensor_tensor(out=nmr, in0=banks[b1][:, 0:N], scalar=-1.0, in1=rstd, op0=ALU.mult, op1=ALU.mult),
              [BK(b1), ("fp", 5)], [("fp", 6)])
            reserved.discard(b1)
            reserved.discard(b2)
            upair = [fp[:, 0:2, 0:N], tmpf[:, :, 0:N]]
            ukeys = [[("fp", 0), ("fp", 1)], [("tmpf", 0), ("tmpf", 1)]]
            for pr in range(4):
                c0 = 2 * pr
                u2 = upair[pr % 2]
                uk = ukeys[pr % 2]
                E1, E2 = (V, G) if pr < 3 else (G, V)
                xk = [("xres", c0), ("xres", c0 + 1)]
                E1(lambda e, c0=c0, u2=u2: e.tensor_tensor(out=u2, in0=xres[:, c0:c0 + 2, 0:N],
                                                           in1=rstd.unsqueeze(1).to_broadcast([128, 2, N]), op=ALU.mult),
                   xk + [("fp", 5)], uk)
                E1(lambda e, u2=u2: e.tensor_tensor(out=u2, in0=u2, in1=nmr.unsqueeze(1).to_broadcast([128, 2, N]), op=ALU.add),
                   uk + [("fp", 6)], uk)
                for k in range(2):
                    c = c0 + k
                    u = u2[:, k, :]
                    if last:
                        A(lambda a, c=c, u=u: a.activation(out=xres[:, c, 0:N], in_=u, func=AF.Identity,
                                                           scale=lnp(l, kg, c), bias=lnp(l, kb_, c)),
                          [uk[k], "pcol"], [("xres", c)])
                    else:
                        A(lambda a, c=c, u=u: a.activation(out=xb[:, c, 0:N], in_=u, func=AF.Identity,
                                                           scale=lnp(l, kg, c), bias=lnp(l, kb_, c)),
                          [uk[k], "pcol"], [("xb", c)])
                        E2(lambda e, c=c, u=u: e.tensor_scalar(out=xres[:, c, 0:N], in0=u, scalar1=lnp(l, kg, c),
                                                               scalar2=lnp(l, kb_, c), op0=ALU.mult, op1=ALU.add),
                           [uk[k], "pcol"], [("xres", c)])

        def merge_units(l, N):
            xsrc, xkeys = xb_src(N)
            m0 = lambda c: fp[:, c, 0:N]
            gslot = {1: 0, 2: 8, 0: 0}
            units, tail = [], []

            def one_chunk(get_w, ncks, j, src, srckeys, evac):
                def u():
                    wsl, wkey = get_w()
                    b = nb()
                    for c in range(ncks):
                        MM(banks[b][:, 0:N], wsl[:, c, j * 128:(j + 1) * 128], src(c), c == 0, c == ncks - 1,
                           [wkey, srckeys(c)], b)
                    evac(b, banks[b][:, 0:N])
                    cast_tick()
                return u

            def lazy(loader):
                box = []

                def get():
                    if not box:
                        box.append(loader())
                    return box[0]
                return get

            def g_units(r):
                out = []
                for hseg in range(2):
                    a0 = C_GL + r * 1024 + hseg * 512
                    get_w = lazy(lambda a0=a0: wload(wb_in[l][:, :, a0:a0 + 512], (8, 512), cast_keys(("in", l), a0, a0 + 512)))
                    for j in range(4):
                        c = hseg * 4 + j

                        def ev(b, ps, c=c, r=r):
                            A(lambda a: a.activation(out=bq[:, gslot[r] + c, 0:N], in_=ps, func=AF.Sigmoid), [BK(b)],
                              [("bq", gslot[r] + c)])
                        out.append(one_chunk(get_w, 8, j, xsrc, xkeys, ev))
                return out

            def wb_units(r, pos):
                out = []
                for hseg in range(2):
                    def loader(r=r, hseg=hseg):
                        rk = cast_keys(("br", l, r), hseg * 512, (hseg + 1) * 512)
                        sl = ring_rr[0]
                        ring_rr[0] = (sl + 1) % NRING
                        wsl = ring[:, sl, 0:2048].rearrange("p (a b) -> p a b", a=4)
                        P.dma("sp", lambda q: q.dma_start(out=wsl, in_=wb_br[l, r][:, :, hseg * 512:(hseg + 1) * 512]),
                              reads=rk, writes=[("ring", sl)])
                        return wsl, ("ring", sl)
                    get_w = lazy(loader)
                    for j in range(4):
                        c = hseg * 4 + j

                        def ev(b, ps, c=c, r=r, pos=pos):
                            Gc = bq[:, gslot[r] + c, 0:N]
                            gk = ("bq", gslot[r] + c)
                            if pos == 0:
                                V(lambda v: v.tensor_tensor(out=m0(c), in0=ps, in1=Gc, op=ALU.mult), [BK(b), gk], [("fp", c)])
                            else:
                                tt = tmpf[:, c % 2, 0:N]
                                V(lambda v: v.tensor_tensor(out=tt, in0=ps, in1=Gc, op=ALU.mult), [BK(b), gk], [("tmpf", c % 2)])
                                if pos == 1:
                                    G(lambda g: g.tensor_tensor(out=m0(c), in0=m0(c), in1=tt, op=ALU.add),
                                      [("fp", c), ("tmpf", c % 2)], [("fp", c)])
                                else:
                                    G(lambda g: g.tensor_tensor(out=bq[:, 8 + c, 0:N], in0=m0(c), in1=tt, op=ALU.add),
                                      [("fp", c), ("tmpf", c % 2)], [("bq", 8 + c)])
                        out.append(one_chunk(get_w, 4, j, lambda cc, r=r: yabc[:, r, cc, 0:N], lambda cc, r=r: ("yabc", r, cc), ev))
                return out

            units += g_units(1) + wb_units(1, 0) + g_units(2) + wb_units(2, 1) + g_units(0)
            tail += wb_units(0, 2)
            return units, tail

        def merge_and_ffn(l, N, last, tail_units=None, after_ffn=None):
            xsrc, xkeys = xb_src(N)
            if tail_units is None:
                units, tail_units = merge_units(l, N)
                for u in units:
                    u()
            for u in tail_units:
                u()
            stage(21 + 20 * l)
            st1 = LNStats(N)
            wsl = wkey = None
            for c in range(8):
                if c % 4 == 0:
                    hs = c // 4
                    wsl, wkey = wload(wb_mix[l][:, :, hs * 512:(hs + 1) * 512], (8, 512), cast_keys(("mix", l), hs * 512, (hs + 1) * 512))
                b = nb()
                j = c % 4
                for k in range(8):
                    MM(banks[b][:, 0:N], wsl[:, k, j * 128:(j + 1) * 128], bq[:, 8 + k, 0:N], k == 0, k == 7,
                       [wkey, ("bq", 8 + k)], b)
                V(lambda v, c=c, b=b: v.scalar_tensor_tensor(out=xres[:, c, 0:N], in0=xres[:, c, 0:N], scalar=ALPHA,
                                                             in1=banks[b][:, 0:N], op0=ALU.mult, op1=ALU.add),
                  [("xres", c), BK(b)], [("xres", c)])
                if c >= 1:
                    st1.feed(c - 1)
            st1.feed(7)
            stage(22 + 20 * l)
            layer_norm(l, 1, N, False, st1)
            stage(23 + 20 * l)
            for pj in range(NFC // 2):
                s = ring_rr[0]
                ring_rr[0] = (s + 1) % NRING
                wsl = ring[:, s, 0:4096].rearrange("p (a b) -> p a b", a=8)
                f0 = pj * 256
                P.dma("sp", lambda q, wsl=wsl, f0=f0: q.dma_start(
                    out=wsl.rearrange("p c (u f) -> p c u f", u=2),
                    in_=wb_fi[l].rearrange("p c (u f) -> p c u f", u=2)[:, :, :, f0:f0 + 256]),
                      reads=cast_keys(("fi", l), f0, f0 + 256) + cast_keys(("fi", l), D_FF + f0, D_FF + f0 + 256),
                      writes=[("ring", s)])
                wkey = ("ring", s)
                for jj in range(2):
                    fc = pj * 2 + jj
                    bg = nb()
                    for c in range(8):
                        MM(banks[bg][:, 0:N], wsl[:, c, jj * 128:(jj + 1) * 128], xb[:, c, 0:N], c == 0, c == 7,
                           [wkey, ("xb", c)], bg)
                    bu = nb()
                    for c in range(8):
                        MM(banks[bu][:, 0:N], wsl[:, c, 256 + jj * 128:256 + (jj + 1) * 128], xb[:, c, 0:N], c == 0, c == 7,
                           [wkey, ("xb", c)], bu)
                    sg = tmpf[:, fc % 2, 0:N]
                    A(lambda a, sg=sg, bg=bg: a.activation(out=sg, in_=banks[bg][:, 0:N], func=AF.Silu), [BK(bg)],
                      [("tmpf", fc % 2)])
                    V(lambda v, sg=sg, bu=bu, fc=fc: v.tensor_tensor(out=bp[:, fc, 0:N], in0=banks[bu][:, 0:N], in1=sg,
                                                                     op=ALU.mult),
                      [BK(bu), ("tmpf", fc % 2)], [("bp", fc)])
                    cast_tick()
            stage(24 + 20 * l)
            st2 = LNStats(N)
            for ep in range(4):
                wk = []
                for kh in range(2):
                    wsl, wkey = wload(wb_fo[l][:, kh * 11:(kh + 1) * 11, ep * 256:(ep + 1) * 256], (11, 256),
                                      cast_keys(("fo", l, kh), ep * 256, (ep + 1) * 256))
                    wk.append((wsl, wkey))
                for jj in range(2):
                    c_out = ep * 2 + jj
                    b = nb()
                    for fc in range(NFC):
                        wsl, wkey = wk[fc // 11]
                        MM(banks[b][:, 0:N], wsl[:, fc % 11, jj * 128:(jj + 1) * 128], bp[:, fc, 0:N], fc == 0,
                           fc == NFC - 1, [wkey, ("bp", fc)], b)
                    V(lambda v, c=c_out, b=b: v.scalar_tensor_tensor(out=xres[:, c, 0:N], in0=xres[:, c, 0:N], scalar=ALPHA,
                                                                     in1=banks[b][:, 0:N], op0=ALU.mult, op1=ALU.add),
                      [("xres", c_out), BK(b)], [("xres", c_out)])
                    if c_out >= 1:
                        st2.feed(c_out - 1)
            st2.feed(7)
            if after_ffn is not None:
                after_ffn()
            stage(25 + 20 * l)
            layer_norm(l, 2, N, last, st2)

        QT, KT, OG, XQ, SQ, KTOK = 0, 4, 8, 12, 16, 20

        def prompt_mixers(l, ti):
            N = T
            xsrc, xkeys = xb_src(N)
            first_tile = (ti == 0)
            last_tile = (ti == NT - 1)
            GS = lambda s: gsm[:, s, :]
            zpre = gsm[:, 0:2, :].rearrange("p a b -> p (a b)").rearrange("p (t e) -> p t e", e=8)
            li = GS(2).rearrange("p (t h) -> p t h", h=4)
            lf = GS(3).rearrange("p (t h) -> p t h", h=4)
            Mp = GS(10)
            amm = GS(11)
            es = GS(13).rearrange("p (t h) -> p t h", h=4)
            lb = GS(14).rearrange("p (t h) -> p t h", h=4)
            aa = amm.rearrange("p (t h) -> p t h", h=4)
            gbank = {}

            def g0():
                bgt = nb()
                for tc in range(4):
                    for c in range(8):
                        MM(banks[bgt][:, tc * 8:(tc + 1) * 8], xb[:, c, tc * 128:(tc + 1) * 128], wg[:, l, c, :],
                           c == 0, c == 7, ["wg", ("xb", c)], bgt, inc=(c == 7 and tc == 3))
                V(lambda v: v.tensor_tensor(out=zpre, in0=banks[bgt][:, 0:32].rearrange("p (t e) -> p t e", e=8),
                                            in1=bgate[:, l, :].unsqueeze(1).to_broadcast([128, 4, 8]), op=ALU.add),
                  [BK(bgt), "bgate"], [("gsm", 0)])
                V(lambda v: v.tensor_copy(out=li, in_=zpre[:, :, 0:4]), [("gsm", 0)], [("gsm", 2)])
                A(lambda a: a.activation(out=lf, in_=zpre[:, :, 4:8], func=AF.Exp, scale=-1.0), [("gsm", 0)], [("gsm", 3)])
                A(lambda a: a.activation(out=lf, in_=lf, func=AF.Ln, bias=1.0), [("gsm", 3)], [("gsm", 3)])

            def g1():
                V(lambda v: v.tensor_scalar(out=GS(3), in0=GS(3), scalar1=-1.0, scalar2=None, op0=ALU.mult),
                  [("gsm", 3)], [("gsm", 3)])
                bcs = nb()
                reserved.add(bcs)
                gbank["bcs"] = bcs
                P.op("pe", lambda t: t.matmul(banks[bcs][:, 0:16], lhsT=trif, rhs=GS(3), start=True, stop=True),
                     reads=["cf", ("gsm", 3)], writes=[BK(bcs)], inc=False)
                P.op("pe", lambda t: t.matmul(banks[bcs][:, 16:32], lhsT=onesf, rhs=GS(3), start=True, stop=True),
                     reads=["cf", ("gsm", 3)], writes=[BK(bcs)], acc=True)

            def g2():
                bcs = gbank["bcs"]
                V(lambda v: v.tensor_tensor(out=GS(4), in0=GS(2), in1=banks[bcs][:, 0:16], op=ALU.subtract),
                  [("gsm", 2), BK(bcs)], [("gsm", 4)])
                V(lambda v: v.tensor_scalar(out=GS(5), in0=banks[bcs][:, 0:16], scalar1=-1.0, scalar2=None, op0=ALU.mult),
                  [BK(bcs)], [("gsm", 5)])
                V(lambda v: v.tensor_copy(out=GS(6), in_=banks[bcs][:, 16:32]), [BK(bcs)], [("gsm", 6)])
                reserved.discard(bcs)
                btr = nb()
                reserved.add(btr)
                gbank["btr"] = btr
                TR(banks[btr][0:16, 0:128], GS(4), identf, [("gsm", 4), "cf"], btr)

            def g3():
                btr = gbank["btr"]
                V(lambda v: v.tensor_reduce(out=gsm[0:16, 7, 0:1], in_=banks[btr][0:16, 0:128], axis=AX.X, op=ALU.max),
                  [BK(btr)], [("gsm", 7)])
                V(lambda v: v.tensor_scalar(out=gsm[0:16, 8, :], in0=identf[0:16, 0:16], scalar1=gsm[0:16, 7, 0:1], scalar2=None,
                                            op0=ALU.mult), [("gsm", 7), "cf"], [("gsm", 8)])
                reserved.discard(btr)
                bgm = nb()
                reserved.add(bgm)
                gbank["bgm"] = bgm
                P.op("pe", lambda t: t.matmul(banks[bgm][:, 0:16], lhsT=onesf[0:16, :], rhs=gsm[0:16, 8, :], start=True, stop=True),
                     reads=["cf", ("gsm", 8)], writes=[BK(bgm)])

            def g4():
                bgm = gbank["bgm"]
                V(lambda v: v.tensor_copy(out=GS(9), in_=banks[bgm][:, 0:16]), [BK(bgm)], [("gsm", 9)])
                reserved.discard(bgm)
                for tc in range(4):
                    sl = slice(tc * 4, tc * 4 + 4)
                    mprev = mst[:, l, :] if tc == 0 else GS(12)[:, (tc - 1) * 4:tc * 4]
                    mk_ = "mst" if tc == 0 else ("gsm", 12)
                    V(lambda v, sl=sl, mprev=mprev: v.tensor_tensor(out=Mp[:, sl], in0=GS(9)[:, sl], in1=mprev, op=ALU.max),
                      [("gsm", 9), mk_], [("gsm", 10)])
                    V(lambda v, sl=sl, mprev=mprev: v.tensor_tensor(out=amm[:, sl], in0=mprev, in1=Mp[:, sl], op=ALU.subtract),
                      [mk_, ("gsm", 10)], [("gsm", 11)])
                    V(lambda v, sl=sl: v.tensor_tensor(out=GS(12)[:, sl], in0=GS(6)[:, sl], in1=Mp[:, sl], op=ALU.add),
                      [("gsm", 6), ("gsm", 10)], [("gsm", 12)])
                V(lambda v: v.tensor_copy(out=mst[:, l, :], in_=GS(12)[:, 12:16]), [("gsm", 12)], ["mst"])
                A(lambda a: a.activation(out=amm, in_=amm, func=AF.Exp), [("gsm", 11)], [("gsm", 11)])

            def g5():
                V(lambda v: v.tensor_tensor(out=GS(13), in0=GS(4), in1=Mp, op=ALU.subtract), [("gsm", 4), ("gsm", 10)], [("gsm", 13)])
                A(lambda a: a.activation(out=GS(13), in_=GS(13), func=AF.Exp), [("gsm", 13)], [("gsm", 13)])
                V(lambda v: v.tensor_tensor(out=GS(14), in0=GS(5), in1=Mp, op=ALU.subtract), [("gsm", 5), ("gsm", 10)], [("gsm", 14)])
                A(lambda a: a.activation(out=GS(14), in_=GS(14), func=AF.Exp), [("gsm", 14)], [("gsm", 14)])

            g0()

            stage(5 + 20 * l)
            wq, wqk = wload(wb_in[l][:, :, C_MQ:C_MQ + 512], (8, 512), cast_keys(("in", l), C_MQ, C_MQ + 512))
            fm_group(wq, wqk, 8, 0, 4, xsrc, xkeys, N,
                     lambda j, b, ps: evac_copy(bp[:, QT + j, :], ps, [BK(b)], [("bp", QT + j)]),
                     after=lambda j: (g1() if j == 1 else g2() if j == 3 else None))
            wk_, wkk = wload(wb_in[l][:, :, C_MK:C_MK + 512], (8, 512), cast_keys(("in", l), C_MK, C_MK + 512))
            ksc = 128.0 ** -0.5
            fm_group(wk_, wkk, 8, 0, 4, xsrc, xkeys, N,
                     lambda j, b, ps: A(lambda a: a.activation(out=bp[:, KT + j, :], in_=ps, func=AF.Copy, scale=ksc),
                                        [BK(b)], [("bp", KT + j)]),
                     after=lambda j: (g3() if j == 1 else g4() if j == 3 else None))
            for tc in range(4):
                b = nb()
                for c in range(8):
                    MM(banks[b][:, :], xb[:, c, tc * 128:(tc + 1) * 128], wk_[:, c, :], c == 0, c == 7, [wkk, ("xb", c)], b)
                V(lambda v, tc=tc, b=b: v.tensor_scalar(out=bp[:, KTOK + tc, :], in0=banks[b][:, :], scalar1=ksc, scalar2=None,
                                                        op0=ALU.mult), [BK(b)], [("bp", KTOK + tc)])
                if tc == 1:
                    g5()
            wo_, wok = wload(wb_in[l][:, :, C_MO:C_MO + 512], (8, 512), cast_keys(("in", l), C_MO, C_MO + 512))

            def ev_og(j, b, ps):
                A(lambda a: a.activation(out=bp[:, OG + j, :], in_=ps, func=AF.Sigmoid), [BK(b)], [("bp", OG + j)])
                V(lambda v: v.tensor_scalar(out=bp[:, OG + j, :], in0=bp[:, OG + j, :], scalar1=normg(l, j), scalar2=None,
                                            op0=ALU.mult), [("bp", OG + j), "pcol"], [("bp", OG + j)])
            fm_group(wo_, wok, 8, 0, 4, xsrc, xkeys, N, ev_og)

            stage(6 + 20 * l)
            wsq, wsqk = wload(wb_in[l][:, :, C_SQ:C_SQ + 512], (8, 512), cast_keys(("in", l), C_SQ, C_SQ + 512))
            fm_group(wsq, wsqk, 8, 0, 4, xsrc, xkeys, N,
                     lambda j, b, ps: rope_apply(ps, b, N, j % 2, bp[:, SQ + j, :], [("bp", SQ + j)]))
            wkv, wkvk = wload(wb_in[l][:, :, C_SK:C_SK + 256], (8, 256), cast_keys(("in", l), C_SK, C_SK + 256))
            k32 = fp[:, 4, :]

            def ev_sk2(j, b, ps):
                q32 = fp[:, 0, 0:N]
                evac_copy(q32, ps, [BK(b)], [("fp", 0)])
                b2 = nb()
                P.op("pe", lambda t: t.matmul(banks[b2][:, 0:N], lhsT=rrf, rhs=q32, start=True, stop=True),
                     reads=["cf", ("fp", 0)], writes=[BK(b2)])
                rs = tmpf[:, 1, 0:N]
                V(lambda v: v.tensor_tensor(out=rs, in0=banks[b2][:, 0:N], in1=rtab[:, 1, 0:N], op=ALU.mult),
                  [BK(b2), ("rtab", 1)], [("tmpf", 1)])
                G(lambda g: g.tensor_tensor(out=q32, in0=q32, in1=rtab[:, 0, 0:N], op=ALU.mult), [("fp", 0), ("rtab", 0)], [("fp", 0)])
                G(lambda g: g.tensor_tensor(out=k32, in0=q32, in1=rs, op=ALU.add), [("fp", 0), ("tmpf", 1)], [("fp", 4)])
                G(lambda g: g.tensor_copy(out=skT[:, l, 1:5, :], in_=k32.rearrange("p (a b) -> p a b", a=4)),
                  [("fp", 4)], [("skT", l)])
            fm_group(wkv, wkvk, 8, 0, 1, xsrc, xkeys, N, ev_sk2)
            for tc in range(4):
                b = nb()
                for c in range(8):
                    MM(banks[b][:, 0:128], xb[:, c, tc * 128:(tc + 1) * 128], wkv[:, c, 128:256], c == 0, c == 7,
                       [wkvk, ("xb", c)], b)
                for kv in range(2):
                    evac_copy(svp[:, l, 1 + tc, kv, kv * 64:(kv + 1) * 64], banks[b][:, kv * 64:(kv + 1) * 64],
                              [BK(b)], [("svp", l)])
                if last_tile and tc == 3:
                    V(lambda v, b=b: v.tensor_copy(out=hh[:, 0, 0, :], in_=banks[b][:, 0:128]), [BK(b)], [("hh", 0)])
                    P.dma("sp", lambda q: q.dma_start(out=o_svp[l], in_=hh[:, 0, 0, :]), reads=[("hh", 0)])
            if last_tile:
                bt = nb()
                TR(banks[bt][:, 0:128], k32[:, 384:512], identf, [("fp", 4), "cf"], bt)
                V(lambda v, bt=bt: v.tensor_copy(out=hh[:, 1, 0, :], in_=banks[bt][:, 0:128]), [BK(bt)], [("hh", 1)])
                P.dma("sp", lambda q: q.dma_start(out=o_skp[l], in_=hh[:, 1, 0, :]), reads=[("hh", 1)])

            stage(7 + 20 * l)
            def swa_scores(qb):
                pbuf = qb % 2
                slots = []
                for kv in range(2):
                    for kb in range(2):
                        if kb == 0 and first_tile and qb == 0:
                            continue
                        b = nb()
                        psl = slice(kv * 64, (kv + 1) * 64)
                        rhs = bp[psl, SQ:SQ + 4, qb * 128:(qb + 1) * 128]
                        MM(banks[b][:, :].rearrange("p (g t) -> p g t", g=4), skT[psl, l, qb + kb, :], rhs, True, False,
                           [("skT", l)] + [("bp", SQ + g) for g in range(4)], b, inc=False)
                        MM(banks[b][:, :].rearrange("p (g t) -> p g t", g=4), identb[:, :], mneg[:, 1 - kb, :, :], False, True,
                           ["identb", "mneg"], b)
                        pi = pbuf * 4 + kv * 2 + kb
                        A(lambda a, b=b, pi=pi: a.activation(out=bq[:, pi, :], in_=banks[b][:, :], func=AF.Exp, scale=0.125),
                          [BK(b)], [("bq", pi)])
                        slots.append((kv, kb, pi))
                return slots

            def swa_finish(qb, slots):
                bnum = nb()
                for i, (kv, kb, pi) in enumerate(slots):
                    MM(banks[bnum][:, :], svp[:, l, qb + kb, kv, :], bq[:, pi, :], i == 0, i == len(slots) - 1,
                       [("svp", l), ("bq", pi)], bnum)
                bden = nb()
                for i, (kv, kb, pi) in enumerate(slots):
                    MM(banks[bden][:, :], onespad[:, kv, :], bq[:, pi, :], i == 0, i == len(slots) - 1,
                       ["onespad", ("bq", pi)], bden)
                dn = tmpf[:, qb % 2, :]
                V(lambda v: v.tensor_tensor(
                    out=dn.rearrange("p (g t) -> p g t", g=4), in0=banks[bden][:, :].rearrange("p (g t) -> p g t", g=4),
                    in1=sinkexp[:, l, :].unsqueeze(2).to_broadcast([128, 4, 128]), op=ALU.add),
                  [BK(bden), "sinkexp"], [("tmpf", qb % 2)])
                V(lambda v: v.reciprocal(out=dn, in_=dn), [("tmpf", qb % 2)], [("tmpf", qb % 2)])
                V(lambda v: v.tensor_tensor(
                    out=yabc[:, 1, :, qb * 128:(qb + 1) * 128], in0=banks[bnum][:, :].rearrange("p (g t) -> p g t", g=4),
                    in1=dn.rearrange("p (g t) -> p g t", g=4), op=ALU.mult),
                  [BK(bnum), ("tmpf", qb % 2)], [("yabc", 1, g) for g in range(4)])

            nxt = swa_scores(0)
            for qb in range(4):
                cur = nxt
                if qb + 1 < 4:
                    nxt = swa_scores(qb + 1)
                swa_finish(qb, cur)
            G(lambda g: g.tensor_copy(out=skT[:, l, 0, :], in_=skT[:, l, 4, :]), [("skT", l)], [("skT", l)])
            G(lambda g: g.tensor_copy(out=svp[:, l, 0], in_=svp[:, l, 4]), [("svp", l)], [("svp", l)])

            stage(8 + 20 * l)
            wxq, wxqk = wload(wb_in[l][:, :, C_XQ:C_XQ + 512], (8, 512), cast_keys(("in", l), C_XQ, C_XQ + 512))
            fm_group(wxq, wxqk, 8, 0, 4, xsrc, xkeys, N,
                     lambda j, b, ps: evac_copy(bp[:, XQ + j, :], ps, [BK(b)], [("bp", XQ + j)]))
            xsc = 128.0 ** -0.5

            def x_scores(h):
                for mb in range(2):
                    b = nb()
                    MM(banks[b][:, :], memKT[:, l, h, mb * 128:(mb + 1) * 128], bp[:, XQ + h, :], True, True,
                       [("memKT", l), ("bp", XQ + h)], b)
                    pi = 8 + (h % 2) * 2 + mb
                    A(lambda a, b=b, pi=pi: a.activation(out=bq[:, pi, :], in_=banks[b][:, :], func=AF.Exp, scale=xsc),
                      [BK(b)], [("bq", pi)])

            def x_finish(h):
                bnum = nb()
                for mb in range(2):
                    pi = 8 + (h % 2) * 2 + mb
                    MM(banks[bnum][:, :], memV[:, l, mb, h * 128:(h + 1) * 128], bq[:, pi, :], mb == 0, mb == 1,
                       [("memV", l), ("bq", pi)], bnum)
                bden = nb()
                for mb in range(2):
                    pi = 8 + (h % 2) * 2 + mb
                    MM(banks[bden][:, :], onesb[:, :], bq[:, pi, :], mb == 0, mb == 1, ["onesb", ("bq", pi)], bden)
                dn = tmpf[:, h % 2, :]
                V(lambda v: v.reciprocal(out=dn, in_=banks[bden][:, :]), [BK(bden)], [("tmpf", h % 2)])
                V(lambda v: v.tensor_tensor(out=yabc[:, 2, h, :], in0=banks[bnum][:, :], in1=dn, op=ALU.mult),
                  [BK(bnum), ("tmpf", h % 2)], [("yabc", 2, h)])

            x_scores(0)
            for h in range(4):
                if h + 1 < 4:
                    x_scores(h + 1)
                x_finish(h)

            stage(9 + 20 * l)
            wv_, wvk = wload(wb_in[l][:, :, C_MV:C_MV + 512], (8, 512), cast_keys(("in", l), C_MV, C_MV + 512))
            for tc in range(4):
                b = nb()
                for c in range(8):
                    MM(banks[b][:, :], xb[:, c, tc * 128:(tc + 1) * 128], wv_[:, c, :], c == 0, c == 7, [wvk, ("xb", c)], b)
                V(lambda v, tc=tc, b=b: v.tensor_tensor(out=vaug[:, tc, :, 0:128],
                                                        in0=banks[b][:, :].rearrange("p (h d) -> p h d", h=4),
                                                        in1=es[:, tc, :].unsqueeze(2).to_broadcast([128, 4, 128]), op=ALU.mult),
                  [BK(b), ("gsm", 13)], [("vaug", tc)])
                G(lambda g, tc=tc: g.tensor_copy(out=vaug[:, tc, :, 128:129], in_=es[:, tc, :].unsqueeze(2)),
                  [("gsm", 13), ("vaug", tc)], [("vaug", tc)])
            fill_units, tail_units = merge_units(l, N)
            per_pt = -(-len(fill_units) // 8)

            def filler(k=per_pt):
                for _ in range(k):
                    if fill_units:
                        fill_units.pop(0)()
            for tc in range(4):
                tsl = slice(tc * 128, (tc + 1) * 128)
                V(lambda v, tc=tc: v.tensor_tensor(out=CTs[:], in0=CT[:, l], in1=aa[:, tc, :].unsqueeze(2).to_broadcast([128, 4, 129]),
                                                   op=ALU.mult), [("CT", l), ("gsm", 11)], ["CTs"])
                G(lambda g: g.tensor_copy(out=CTb[:], in_=CTs[:]), ["CTs"], ["CTb"])
                bs = nb()
                for h in range(4):
                    MM(banks[bs][:, h * 128:(h + 1) * 128], bp[:, KT + h, tsl], bp[:, QT + h, tsl], True, True,
                       [("bp", KT + h), ("bp", QT + h)], bs, inc=(h == 3))
                sm = smt[:, tc % 2]
                V(lambda v, sm=sm, bs=bs: v.tensor_tensor(out=sm, in0=banks[bs][:, :].rearrange("p (h t) -> p h t", h=4),
                                                          in1=trib[:, :].unsqueeze(1).to_broadcast([128, 4, 128]), op=ALU.mult),
                  [BK(bs), "trib"], [("smt", tc % 2)])
                filler()
                bo = [nb(), nb()]
                for h in range(4):
                    bb = bo[h // 2]
                    o_ap = banks[bb][:, (h % 2) * 129:(h % 2) * 129 + 129]
                    MM(o_ap, bp[:, QT + h, tsl], CTb[:, h, :], True, False, [("bp", QT + h), "CTb"], bb, inc=False)
                    MM(o_ap, sm[:, h, :], vaug[:, tc, h, :], False, True, [("smt", tc % 2), ("vaug", tc)], bb, inc=(h % 2 == 1))
                bu = [nb(), nb()]
                for h in range(4):
                    bb = bu[h // 2]
                    MM(banks[bb][:, (h % 2) * 129:(h % 2) * 129 + 129], bp[:, KTOK + tc, h * 128:(h + 1) * 128], vaug[:, tc, h, :],
                       True, True, [("bp", KTOK + tc), ("vaug", tc)], bb, inc=(h % 2 == 1))
                for hp in range(2):
                    V(lambda v, hp=hp, bb=bu[hp]: v.tensor_tensor(
                        out=CT[:, l, 2 * hp:2 * hp + 2, :], in0=banks[bb][:, 0:258].rearrange("p (h d) -> p h d", h=2),
                        in1=CTs[:, 2 * hp:2 * hp + 2, :], op=ALU.add), [BK(bu[hp]), "CTs"], [("CT", l)])
                hb = hh[:, tc % 2]
                dn4 = gsm[:, 15, 0:4]
                for hp in range(2):
                    V(lambda v, hp=hp, bb=bo[hp]: v.tensor_copy(
                        out=gsm[:, 15, 8 + 2 * hp:8 + 2 * hp + 2].unsqueeze(2),
                        in_=banks[bb][:, 0:258].rearrange("p (h d) -> p h d", h=2)[:, :, 128:129]),
                      [BK(bo[hp])], [("gsm", 15)])
                V(lambda v: v.scalar_tensor_tensor(out=dn4, in0=gsm[:, 15, 8:12], scalar=-1.0, in1=gsm[:, 15, 8:12],
                                                   op0=ALU.mult, op1=ALU.max), [("gsm", 15)], [("gsm", 15)])
                V(lambda v, tc=tc: v.tensor_tensor(out=dn4, in0=dn4, in1=lb[:, tc, :], op=ALU.max), [("gsm", 15), ("gsm", 14)],
                  [("gsm", 15)])
                V(lambda v: v.reciprocal(out=dn4, in_=dn4), [("gsm", 15)], [("gsm", 15)])
                for hp in range(2):
                    V(lambda v, hp=hp, bb=bo[hp], hb=hb: v.tensor_tensor(
                        out=hb[:, 2 * hp:2 * hp + 2, :],
                        in0=banks[bb][:, 0:258].rearrange("p (h d) -> p h d", h=2)[:, :, 0:128],
                        in1=gsm[:, 15, 2 * hp:2 * hp + 2].unsqueeze(2).to_broadcast([128, 2, 128]), op=ALU.mult),
                      [BK(bo[hp]), ("gsm", 15)], [("hh", tc % 2)])
                for h in range(4):
                    V(lambda v, h=h, hb=hb: v.bn_stats(out=bnst[:, h, :], in_=hb[:, h, :]), [("hh", tc % 2)], ["bnst"])
                    V(lambda v, h=h: v.bn_aggr(out=bnag[:, h, :], in_=bnst[:, h, :]), ["bnst"], ["bnag"])
                rs4 = gsm[:, 15, 4:8]
                V(lambda v: v.tensor_scalar(out=rs4.unsqueeze(2), in0=bnag[:, :, 1:2], scalar1=HN_EPS, scalar2=None, op0=ALU.add),
                  ["bnag"], [("gsm", 15)])
                A(lambda a: a.activation(out=rs4, in_=rs4, func=AF.Sqrt), [("gsm", 15)], [("gsm", 15)])
                V(lambda v: v.reciprocal(out=rs4, in_=rs4), [("gsm", 15)], [("gsm", 15)])
                for h in range(4):
                    V(lambda v, h=h, hb=hb: v.tensor_scalar(out=hb[:, h, :], in0=hb[:, h, :], scalar1=bnag[:, h, 0:1],
                                                            scalar2=gsm[:, 15, 4 + h:5 + h], op0=ALU.subtract, op1=ALU.mult),
                      [("hh", tc % 2), "bnag", ("gsm", 15)], [("hh", tc % 2)])
                filler()
                bt = nb()
                for h in range(4):
                    TR(banks[bt][:, h * 128:(h + 1) * 128], hb[:, h, :], identf, [("hh", tc % 2), "cf"], bt, inc=(h == 3), first=(h == 0))
                V(lambda v, bt=bt, tsl=tsl: v.tensor_tensor(out=yabc[:, 0, :, tsl], in0=banks[bt][:, :].rearrange("p (h t) -> p h t", h=4),
                                                            in1=bp[:, OG:OG + 4, tsl], op=ALU.mult),
                  [BK(bt)] + [("bp", OG + h) for h in range(4)], [("yabc", 0, h) for h in range(4)])
            while fill_units:
                fill_units.pop(0)()
            if last_tile:
                for h in range(4):
                    bt = nb()
                    TR(banks[bt][:, 0:128], CT[:, l, h, 0:128], identf, [("CT", l), "cf"], bt)
                    V(lambda v, bt=bt, h=h: v.tensor_copy(out=hh[:, h % 2, 1, :], in_=banks[bt][:, 0:128]), [BK(bt)], [("hh", h % 2)])
                    P.dma("sp", lambda q, h=h: q.dma_start(out=o_cp[l, h], in_=hh[:, h % 2, 1, :]), reads=[("hh", h % 2)])
                bt = nb()
                TR(banks[bt][0:4, 0:128], CT[:, l, :, 128], identf, [("CT", l), "cf"], bt)
                V(lambda v, bt=bt: v.tensor_copy(out=hh[0:4, 0, 2, :], in_=banks[bt][0:4, 0:128]), [BK(bt)], [("hh", 0)])
                P.dma("sp", lambda q: q.dma_start(out=o_np[l], in_=hh[0:4, 0, 2, :]), reads=[("hh", 0)])
                P.dma("sp", lambda q: q.dma_start(out=o_mp[l:l + 1, :], in_=mst[0:1, l, :]), reads=["mst"])
            return tail_units

        xin = bp[:, 0:16, :].rearrange("p c t -> p (c t)").bitcast(F32).rearrange("p (t d) -> p t d", t=4)

        def load_x(ti):
            r0 = ti * T
            P.dma("act", lambda q: q.dma_start(out=xin, in_=xp[r0:r0 + T, :].rearrange("(t p) d -> p t d", p=128)),
                  writes=[("bp", c) for c in range(16)])

        load_x(0)
        for ti in range(NT):
            r0 = ti * T
            for c in range(8):
                b = nb()
                for tc in range(4):
                    TR(banks[b][:, tc * 128:(tc + 1) * 128], xin[:, tc, c * 128:(c + 1) * 128], identf,
                       [("bp", 4 * tc + k) for k in range(4)] + ["cf"], b, inc=(tc == 3), first=(tc == 0))
                V(lambda v, c=c, b=b: v.tensor_copy(out=xres[:, c, :], in_=banks[b][:, :]), [BK(b)], [("xres", c)])
                A(lambda a, c=c, b=b: a.copy(out=xb[:, c, :], in_=banks[b][:, :]), [BK(b)], [("xb", c)])
            rope_tables(ti * T, T)
            stage(4)
            for l in range(DEPTH):
                tail = prompt_mixers(l, ti)
                stage(20 + 20 * l)
                hook = None
                if l == DEPTH - 1 and ti + 1 < NT:
                    hook = (lambda ti=ti: load_x(ti + 1))
                merge_and_ffn(l, T, last=(l == DEPTH - 1), tail_units=tail, after_ffn=hook)
                stage(30 + 20 * l)
            yout = fp[:, :, :].rearrange("p (t a) b -> p t (a b)", t=4)
            for tc in range(4):
                for half in range(2):
                    b = nb()
                    for cc in range(4):
                        c = half * 4 + cc
                        TR(banks[b][:, cc * 128:(cc + 1) * 128], xres[:, c, tc * 128:(tc + 1) * 128], identf,
                           [("xres", c), "cf"], b, inc=(cc == 3), first=(cc == 0))
                    evac_copy(yout[:, tc, half * 512:(half + 1) * 512], banks[b][:, :], [BK(b)], [("fp", 2 * tc + half)])
                P.dma("act", lambda q, tc=tc, r0=r0: q.dma_start(out=yp[r0 + tc * 128:r0 + (tc + 1) * 128, :], in_=yout[:, tc, :]),
                      reads=[("fp", 2 * tc), ("fp", 2 * tc + 1)])


        def sample_phase():
            N = NS
            P.barrier()
            CTf = CT[:].rearrange("p l h d -> p (l h d)")
            Cs = [CTf[:, 0:512].rearrange("p (h k) -> p h k", h=4), CTf[:, 512:1024].rearrange("p (h k) -> p h k", h=4)]
            mKb = [memKT[:].rearrange("p l h m -> p (l h m)").bitcast(F32).rearrange("p (b c) -> p b c", b=2),
                   memV[:].rearrange("p l b c -> p (l b c)").bitcast(F32).rearrange("p (b c) -> p b c", b=2)]
            mVb = [memT[:, 0:4, :].rearrange("p c (b d) -> p (c b) d", d=128),
                   memT[:, 4:8, :].rearrange("p c (b d) -> p (c b) d", d=128)]
            KTc = hh[:].rearrange("p a h d -> p (a h d)").bitcast(BF16)[:, 0:1024].rearrange("p (b d) -> p b d", d=128)
            svf = svp[:, 1].rearrange("p b k d -> p (b k d)").bitcast(F32)
            Ks = [svf[:, 0:128], svf[:, 128:256]]
            Vp = [svp[:, 0, 0], svp[:, 0, 1]]
            KTs = skT[:, 0, 0, :]
            t1b = [smt[:].rearrange("p a h t -> p (a h t)").bitcast(F32).rearrange("p (h k) -> p h k", h=4),
                   CTs[:].rearrange("p h d -> p (h d)")[:, 0:512].rearrange("p (h k) -> p h k", h=4)]
            G(lambda g: g.memset(svp[:, 0, 0:2], 0.0), [], [("Vp", 0, 0), ("Vp", 0, 1), ("Vp", 1, 0), ("Vp", 1, 1)])

            S = lambda slot: sS[:, slot]
            SF = lambda slot: sS[:, slot].rearrange("p j h -> p (j h)")
            SK = lambda slot: ("sS", slot)
            a_bc = sR[:, :, 0:4]
            w_bc = sR[:, :, 4:8]
            lb_bc = sR[:, :, 8:12]

            def psv(b, c0):
                return banks[b][:, c0:c0 + 64].rearrange("p (j h) -> p j h", h=4)

            def sample_mixers(l):
                xsrc, xkeys = xb_src(N)
                bg = nb()
                for c in range(8):
                    MM(banks[bg][0:16, 0:8], xb[:, c, 0:N], wg[:, l, c, :], c == 0, c == 7, ["wg", ("xb", c)], bg)
                z = sx[:, 0:8]
                V(lambda v: v.tensor_tensor(out=z, in0=banks[bg][0:16, 0:8], in1=bgate[0:16, l, :], op=ALU.add),
                  [BK(bg), "bgate"], [("sx", 0)])
                P.dma("sp", lambda q: q.dma_start(out=sx[:, 8:12], in_=stm[l]), writes=[("sx", 1)])
                lf = sx[:, 12:16]
                A(lambda a: a.activation(out=lf, in_=z[:, 4:8], func=AF.Exp, scale=-1.0), [("sx", 0)], [("sx", 2)])
                A(lambda a: a.activation(out=lf, in_=lf, func=AF.Ln, bias=1.0), [("sx", 2)], [("sx", 2)])
                tt_ = sx[:, 16:20]
                mt = sx[:, 20:24]
                V(lambda v: v.tensor_tensor(out=tt_, in0=sx[:, 8:12], in1=lf, op=ALU.subtract), [("sx", 1), ("sx", 2)], [("sx", 3)])
                V(lambda v: v.tensor_tensor(out=mt, in0=tt_, in1=z[:, 0:4], op=ALU.max), [("sx", 3), ("sx", 0)], [("sx", 4)])
                V(lambda v: v.tensor_tensor(out=sx[:, 24:28], in0=tt_, in1=mt, op=ALU.subtract), [("sx", 3), ("sx", 4)], [("sx", 5)])
                V(lambda v: v.tensor_tensor(out=sx[:, 28:32], in0=z[:, 0:4], in1=mt, op=ALU.subtract), [("sx", 0), ("sx", 4)], [("sx", 5)])
                V(lambda v: v.tensor_scalar(out=sx[:, 32:36], in0=mt, scalar1=-1.0, scalar2=None, op0=ALU.mult), [("sx", 4)], [("sx", 5)])
                A(lambda a: a.activation(out=sx[:, 24:36], in_=sx[:, 24:36], func=AF.Exp), [("sx", 5)], [("sx", 5)])
                P.dma("sp", lambda q: q.dma_start(out=o_ms[l], in_=mt), reads=[("sx", 4)])
                V(lambda v: v.tensor_tensor(out=sD[:], in0=identf[0:16, 0:16].unsqueeze(2).to_broadcast([16, 16, 12]),
                                            in1=sx[:, 24:36].unsqueeze(1).to_broadcast([16, 16, 12]), op=ALU.mult),
                  ["cf", ("sx", 5)], ["sD"])
                br = nb()
                P.op("pe", lambda t: t.matmul(banks[br][:, 0:192], lhsT=onesf[0:16, :], rhs=sD[:].rearrange("j a q -> j (a q)"),
                                              start=True, stop=True), reads=["cf", "sD"], writes=[BK(br)])
                V(lambda v: v.tensor_copy(out=sR[:], in_=banks[br][:, 0:192].rearrange("p (a q) -> p a q", q=12)), [BK(br)], ["sR"])

                def proj(col, slot, kind):
                    wsl, wkey = wload(wb_in[l][:, :, col:col + 512], (8, 512), cast_keys(("in", l), col, col + 512))

                    def ev(j, b, ps):
                        dst = S(slot)[:, :, j]
                        if kind == "copy":
                            evac_copy(dst, ps, [BK(b)], [SK(slot)])
                        elif kind == "kscale":
                            A(lambda a: a.activation(out=dst, in_=ps, func=AF.Copy, scale=128.0 ** -0.5), [BK(b)], [SK(slot)])
                        elif kind == "og":
                            A(lambda a: a.activation(out=dst, in_=ps, func=AF.Sigmoid), [BK(b)], [SK(slot)])
                            V(lambda v: v.tensor_scalar(out=dst, in0=dst, scalar1=normg(l, j), scalar2=None, op0=ALU.mult),
                              [SK(slot), "pcol"], [SK(slot)])
                        elif kind == "rope":
                            rope_apply(ps, b, N, j % 2, dst, [SK(slot)])
                    fm_group(wsl, wkey, 8, 0, 4, xsrc, xkeys, N, ev)
                proj(C_MQ, 0, "copy")
                proj(C_MK, 1, "kscale")
                proj(C_MV, 2, "copy")
                proj(C_MO, 3, "og")
                proj(C_SQ, 4, "rope")
                wkv, wkvk = wload(wb_in[l][:, :, C_SK:C_SK + 256], (8, 256), cast_keys(("in", l), C_SK, C_SK + 256))

                def ev_kv(j, b, ps):
                    if j == 0:
                        rope_apply(ps, b, N, 0, S(6)[:, :, 0], [SK(6)])
                    else:
                        evac_copy(S(7)[:, :, 0], ps, [BK(b)], [SK(7)])
                fm_group(wkv, wkvk, 8, 0, 2, xsrc, xkeys, N, ev_kv)
                proj(C_XQ, 5, "copy")

                for wi, slot in enumerate((0, 1, 4, 5)):
                    V(lambda v, wi=wi, slot=slot: v.tensor_copy(out=sB[:, wi], in_=S(slot)), [SK(slot)], [("sB", wi)])
                P.dma("sp", lambda q: q.dma_start(out=sT[:, 0, :], in_=stn[l].rearrange("j h k -> (j h) k")), writes=[("sT", 0)])
                bt = nb()
                TR(banks[bt][:, 0:64], sT[:, 0, :], identf[0:64, 0:64], [("sT", 0), "cf"], bt)
                V(lambda v, bt=bt: v.tensor_copy(out=SF(8), in_=banks[bt][:, 0:64]), [BK(bt)], [SK(8)])

                G(lambda g: g.tensor_tensor(out=S(11), in0=S(0), in1=S(1), op=ALU.mult), [SK(0), SK(1)], [SK(11)])
                G(lambda g: g.tensor_tensor(out=S(12), in0=S(8), in1=S(0), op=ALU.mult), [SK(8), SK(0)], [SK(12)])
                bqk = nb()
                P.op("pe", lambda t: t.matmul(banks[bqk][:, 0:64], lhsT=onesf, rhs=SF(11), start=True, stop=True),
                     reads=["cf", SK(11)], writes=[BK(bqk)], inc=False)
                P.op("pe", lambda t: t.matmul(banks[bqk][:, 64:128], lhsT=onesf, rhs=SF(12), start=True, stop=True),
                     reads=["cf", SK(12)], writes=[BK(bqk)], acc=True)
                V(lambda v: v.tensor_tensor(out=S(13), in0=psv(bqk, 0), in1=w_bc, op=ALU.mult), [BK(bqk), "sR"], [SK(13)])
                V(lambda v: v.tensor_tensor(out=S(14), in0=psv(bqk, 64), in1=a_bc, op=ALU.mult), [BK(bqk), "sR"], [SK(14)])
                V(lambda v: v.tensor_tensor(out=S(14), in0=S(14), in1=S(13), op=ALU.add), [SK(14), SK(13)], [SK(14)])
                V(lambda v: v.tensor_tensor(out=S(10), in0=S(2), in1=w_bc, op=ALU.mult), [SK(2), "sR"], [SK(10)])

                acc_bank = nb()
                reserved.add(acc_bank)
                acc = banks[acc_bank]
                xsc = 128.0 ** -0.5
                for j in range(NS):
                    pb = j % 2
                    P.dma("sp", lambda q, j=j, pb=pb: q.dma_start(out=Cs[pb], in_=stc[l, j].rearrange("h v k -> v h k")),
                          writes=[("Cs", pb)])
                    P.dma("sp", lambda q, j=j, pb=pb: q.dma_start(out=Ks[pb], in_=csk[l, j]), writes=[("Ks", pb)])
                    for kv in range(2):
                        P.dma("pool", lambda q, j=j, pb=pb, kv=kv: q.dma_start(
                            out=Vp[pb][:, kv, kv * 64:(kv + 1) * 64], in_=csv[l, j][:, kv * 64:(kv + 1) * 64]),
                            writes=[("Vp", pb, kv)])
                    P.dma("sp", lambda q, j=j, pb=pb: q.dma_start(out=mKb[pb], in_=cmk[l, j].rearrange("(b m) c -> m b c", b=2)),
                          writes=[("mK", pb)])
                    for mb in range(2):
                        P.dma("pool", lambda q, j=j, pb=pb, mb=mb: q.dma_start(
                            out=mVb[pb][:, mb * 4:(mb + 1) * 4, :],
                            in_=cmv[l, j][mb * 128:(mb + 1) * 128, :].rearrange("m (h d) -> m h d", h=4)), writes=[("mV", pb, mb)])
                    bq_ = nb()
                    for h in range(4):
                        MM(banks[bq_][:, h * 128:(h + 1) * 128], sB[:, 0, j, h:h + 1].to_broadcast([128, 128]), identb[:, :], True, True,
                           [("sB", 0), "identb"], bq_, inc=(h == 3))
                    bk_ = nb()
                    for h in range(4):
                        MM(banks[bk_][:, h * 128:(h + 1) * 128], sB[:, 1, j, h:h + 1].to_broadcast([128, 128]), identb[:, :], True, True,
                           [("sB", 1), "identb"], bk_, inc=(h == 3))
                    t1 = t1b[pb]
                    V(lambda v, pb=pb, t1=t1, bq_=bq_: v.tensor_tensor(out=t1, in0=Cs[pb], in1=banks[bq_][:, :].rearrange("p (h k) -> p h k", h=4),
                                                                       op=ALU.mult), [("Cs", pb), BK(bq_)], [("t1", pb)])
                    V(lambda v, t1=t1, j=j: v.tensor_reduce(out=S(9)[:, j, :], in_=t1, axis=AX.X, op=ALU.add), [("t1", pb)], [SK(9)])
                    V(lambda v, t1=t1, bk_=bk_, j=j: v.tensor_tensor(out=t1, in0=banks[bk_][:, :].rearrange("p (h k) -> p h k", h=4),
                                                                     in1=S(10)[:, j, :].unsqueeze(2).to_broadcast([128, 4, 128]), op=ALU.mult),
                      [BK(bk_), SK(10)], [("t1", pb)])
                    G(lambda g, pb=pb, j=j: g.tensor_tensor(out=Cs[pb], in0=Cs[pb], in1=a_bc[:, j, :].unsqueeze(2).to_broadcast([128, 4, 128]),
                                                            op=ALU.mult), [("Cs", pb), "sR"], [("Cs", pb)])
                    G(lambda g, pb=pb, t1=t1: g.tensor_tensor(out=t1, in0=t1, in1=Cs[pb], op=ALU.add), [("t1", pb), ("Cs", pb)], [("t1", pb)])
                    P.dma("sp", lambda q, j=j, t1=t1: q.dma_start(out=o_cs[l, j].rearrange("h v k -> v h k"), in_=t1), reads=[("t1", pb)])
                    bt = nb()
                    TR(banks[bt][:, 0:128], Ks[pb], identf, [("Ks", pb), "cf"], bt)
                    evac_copy(KTs, banks[bt][:, 0:128], [BK(bt)], ["KTs"])
                    bs = nb()
                    for kv in range(2):
                        psl = slice(kv * 64, (kv + 1) * 64)
                        MM(banks[bs][:, kv * 4:(kv + 1) * 4], KTs[psl, :], sB[psl, 2, j, :], True, True, ["KTs", ("sB", 2)], bs, inc=(kv == 1))
                    A(lambda a, pb=pb, bs=bs: a.activation(out=sPb[:, pb, 0, :], in_=banks[bs][:, 0:8], func=AF.Exp, scale=0.125),
                      [BK(bs)], [("sP", pb, 0)])
                    for kv in range(2):
                        MM(acc[:, j * 4:(j + 1) * 4], Vp[pb][:, kv, :], sPb[:, pb, 0, kv * 4:(kv + 1) * 4], kv == 0, kv == 1,
                           [("Vp", pb, 0), ("Vp", pb, 1), ("sP", pb, 0)], acc_bank, inc=False)
                    for kv in range(2):
                        MM(acc[:, 64 + j * 4:64 + (j + 1) * 4], onespad[:, kv, :], sPb[:, pb, 0, kv * 4:(kv + 1) * 4], kv == 0, kv == 1,
                           ["onespad", ("sP", pb, 0)], acc_bank, inc=(kv == 1))
                    for mb in range(2):
                        btc = nb()
                        for h in range(4):
                            TR(banks[btc][:, h * 128:(h + 1) * 128], mKb[pb][:, mb, h * 128:(h + 1) * 128], identf, [("mK", pb), "cf"], btc,
                               inc=(h == 3), first=(h == 0))
                        evac_copy(KTc[:, mb * 4:(mb + 1) * 4, :], banks[btc][:, :].rearrange("p (h d) -> p h d", h=4), [BK(btc)], [("KTc", mb)])
                    bs2 = nb()
                    for mb in range(2):
                        for h in range(4):
                            MM(banks[bs2][:, mb * 4 + h:mb * 4 + h + 1], KTc[:, mb * 4 + h, :], sB[:, 3, j, h:h + 1], True, True,
                               [("KTc", mb), ("sB", 3)], bs2, inc=(mb == 1 and h == 3))
                    A(lambda a, pb=pb, bs2=bs2: a.activation(out=sPb[:, pb, 1, :], in_=banks[bs2][:, 0:8], func=AF.Exp, scale=xsc),
                      [BK(bs2)], [("sP", pb, 1)])
                    for h in range(4):
                        for mb in range(2):
                            MM(acc[:, 128 + j * 4 + h:128 + j * 4 + h + 1], mVb[pb][:, mb * 4 + h, :],
                               sPb[:, pb, 1, mb * 4 + h:mb * 4 + h + 1], mb == 0, mb == 1, [("mV", pb, mb), ("sP", pb, 1)], acc_bank, inc=False)
                    for mb in range(2):
                        MM(acc[:, 192 + j * 4:192 + (j + 1) * 4], onesb[:, :], sPb[:, pb, 1, mb * 4:(mb + 1) * 4], mb == 0, mb == 1,
                           ["onesb", ("sP", pb, 1)], acc_bank, inc=(mb == 1))

                V(lambda v: v.tensor_tensor(out=S(15), in0=S(13), in1=S(2), op=ALU.mult), [SK(13), SK(2)], [SK(15)])
                V(lambda v: v.tensor_tensor(out=S(16), in0=S(9), in1=a_bc, op=ALU.mult), [SK(9), "sR"], [SK(16)])
                V(lambda v: v.tensor_tensor(out=S(15), in0=S(15), in1=S(16), op=ALU.add), [SK(15), SK(16)], [SK(15)])
                V(lambda v: v.scalar_tensor_tensor(out=SF(16), in0=SF(14), scalar=-1.0, in1=SF(14), op0=ALU.mult, op1=ALU.max),
                  [SK(14)], [SK(16)])
                V(lambda v: v.tensor_tensor(out=S(16), in0=S(16), in1=lb_bc, op=ALU.max), [SK(16), "sR"], [SK(16)])
                V(lambda v: v.reciprocal(out=SF(16), in_=SF(16)), [SK(16)], [SK(16)])
                V(lambda v: v.tensor_tensor(out=S(15), in0=S(15), in1=S(16), op=ALU.mult), [SK(15), SK(16)], [SK(15)])
                A(lambda a: a.activation(out=SF(16), in_=SF(15), func=AF.Square), [SK(15)], [SK(16)])
                bhn = nb()
                P.op("pe", lambda t: t.matmul(banks[bhn][:, 0:64], lhsT=onesf, rhs=SF(15), start=True, stop=True),
                     reads=["cf", SK(15)], writes=[BK(bhn)], inc=False)
                P.op("pe", lambda t: t.matmul(banks[bhn][:, 64:128], lhsT=onesf, rhs=SF(16), start=True, stop=True),
                     reads=["cf", SK(16)], writes=[BK(bhn)], acc=True)
                V(lambda v: v.tensor_scalar(out=SF(17), in0=banks[bhn][:, 0:64], scalar1=1.0 / 128, scalar2=None, op0=ALU.mult),
                  [BK(bhn)], [SK(17)])
                V(lambda v: v.tensor_tensor(out=SF(18), in0=SF(17), in1=SF(17), op=ALU.mult), [SK(17)], [SK(18)])
                V(lambda v: v.scalar_tensor_tensor(out=SF(18), in0=banks[bhn][:, 64:128], scalar=1.0 / 128, in1=SF(18),
                                                   op0=ALU.mult, op1=ALU.subtract), [BK(bhn), SK(18)], [SK(18)])
                V(lambda v: v.tensor_scalar(out=SF(18), in0=SF(18), scalar1=0.0, scalar2=HN_EPS, op0=ALU.max, op1=ALU.add),
                  [SK(18)], [SK(18)])
                A(lambda a: a.activation(out=SF(18), in_=SF(18), func=AF.Sqrt), [SK(18)], [SK(18)])
                V(lambda v: v.reciprocal(out=SF(18), in_=SF(18)), [SK(18)], [SK(18)])
                V(lambda v: v.tensor_tensor(out=SF(15), in0=SF(15), in1=SF(17), op=ALU.subtract), [SK(15), SK(17)], [SK(15)])
                V(lambda v: v.tensor_tensor(out=SF(15), in0=SF(15), in1=SF(18), op=ALU.mult), [SK(15), SK(18)], [SK(15)])
                V(lambda v: v.tensor_tensor(out=yabc[:, 0, :, 0:N], in0=S(15).rearrange("p j h -> p h j"),
                                            in1=S(3).rearrange("p j h -> p h j"), op=ALU.mult),
                  [SK(15), SK(3)], [("yabc", 0, h) for h in range(4)])
                V(lambda v: v.tensor_tensor(out=S(16), in0=S(8), in1=a_bc, op=ALU.mult), [SK(8), "sR"], [SK(16)])
                V(lambda v: v.tensor_tensor(out=S(17), in0=S(1), in1=w_bc, op=ALU.mult), [SK(1), "sR"], [SK(17)])
                V(lambda v: v.tensor_tensor(out=S(16), in0=S(16), in1=S(17), op=ALU.add), [SK(16), SK(17)], [SK(16)])
                bt = nb()
                TR(banks[bt][0:64, 0:128], SF(16), identf, [SK(16), "cf"], bt)
                V(lambda v, bt=bt: v.tensor_copy(out=sT[:, 1, :], in_=banks[bt][0:64, 0:128]), [BK(bt)], [("sT", 1)])
                P.dma("sp", lambda q: q.dma_start(out=o_ns[l].rearrange("j h k -> (j h) k"), in_=sT[:, 1, :]), reads=[("sT", 1)])
                G(lambda g: g.tensor_tensor(out=S(11), in0=S(4), in1=S(6)[:, :, 0:1].to_broadcast([128, 16, 4]), op=ALU.mult),
                  [SK(4), SK(6)], [SK(11)])
                bsn = nb()
                P.op("pe", lambda t: t.matmul(banks[bsn][:, 0:64], lhsT=bdf[:, :], rhs=SF(11), start=True, stop=True),
                     reads=["bdf", SK(11)], writes=[BK(bsn)])
                A(lambda a: a.activation(out=SF(12), in_=banks[bsn][:, 0:64], func=AF.Exp, scale=0.125), [BK(bsn)], [SK(12)])
                V(lambda v: v.tensor_tensor(out=S(13), in0=S(12), in1=S(7)[:, :, 0:1].to_broadcast([128, 16, 4]), op=ALU.mult),
                  [SK(12), SK(7)], [SK(13)])
                V(lambda v: v.tensor_tensor(out=S(13), in0=S(13), in1=psv(acc_bank, 0), op=ALU.add), [SK(13), BK(acc_bank)], [SK(13)])
                V(lambda v: v.tensor_tensor(out=S(17), in0=S(12), in1=psv(acc_bank, 64), op=ALU.add), [SK(12), BK(acc_bank)], [SK(17)])
                V(lambda v: v.tensor_tensor(out=S(17), in0=S(17), in1=sinkexp[:, l, :].unsqueeze(1).to_broadcast([128, 16, 4]), op=ALU.add),
                  [SK(17), "sinkexp"], [SK(17)])
                V(lambda v: v.reciprocal(out=SF(17), in_=SF(17)), [SK(17)], [SK(17)])
                V(lambda v: v.tensor_tensor(out=yabc[:, 1, :, 0:N], in0=S(13).rearrange("p j h -> p h j"),
                                            in1=S(17).rearrange("p j h -> p h j"), op=ALU.mult),
                  [SK(13), SK(17)], [("yabc", 1, g) for g in range(4)])
                V(lambda v: v.reciprocal(out=SF(18), in_=banks[acc_bank][:, 192:256]), [BK(acc_bank)], [SK(18)])
                V(lambda v: v.tensor_tensor(out=yabc[:, 2, :, 0:N], in0=psv(acc_bank, 128).rearrange("p j h -> p h j"),
                                            in1=S(18).rearrange("p j h -> p h j"), op=ALU.mult),
                  [BK(acc_bank), SK(18)], [("yabc", 2, h) for h in range(4)])
                reserved.discard(acc_bank)
                P.dma("pool", lambda q: q.dma_start(out=o_sks[l][:, 0:127, :], in_=csk[l][:, 1:128, :]))
                P.dma("pool", lambda q: q.dma_start(out=o_svs[l][:, 0:127, :], in_=csv[l][:, 1:128, :]))
                for slot, odst, r in ((6, o_sks, 2), (7, o_svs, 3)):
                    bt = nb()
                    TR(banks[bt][0:16, 0:128], S(slot)[:, :, 0], identf, [SK(slot), "cf"], bt)
                    V(lambda v, bt=bt, r=r: v.tensor_copy(out=sT[0:16, r, :], in_=banks[bt][0:16, 0:128]), [BK(bt)], [("sT", r)])
                    P.dma("sp", lambda q, odst=odst, r=r: q.dma_start(out=odst[l][:, 127, :], in_=sT[0:16, r, :]), reads=[("sT", r)])

            xin_s = fp[0:16, 0:2, :].rearrange("p a b -> p (a b)")
            P.dma("sp", lambda q: q.dma_start(out=xin_s, in_=xs[:, :]), writes=[("fp", 0), ("fp", 1)])
            b = nb()
            for c in range(8):
                TR(banks[b][:, c * 16:(c + 1) * 16], xin_s[:, c * 128:(c + 1) * 128], identf[0:16, 0:16], [("fp", 0), ("fp", 1), "cf"], b,
                   inc=(c == 7), first=(c == 0))
            V(lambda v, b=b: v.tensor_copy(out=xres[:, :, 0:N], in_=banks[b][:, 0:128].rearrange("p (c j) -> p c j", c=8)),
              [BK(b)], [("xres", c) for c in range(8)])
            A(lambda a, b=b: a.copy(out=xb[:, :, 0:N], in_=banks[b][:, 0:128].rearrange("p (c j) -> p c j", c=8)),
              [BK(b)], [("xb", c) for c in range(8)])
            rope_tables(PAST_LEN, N, const_pos=True)
            for l in range(DEPTH):
                sample_mixers(l)
                merge_and_ffn(l, N, last=(l == DEPTH - 1))
            yo = fp[0:16, 0:2, :].rearrange("p a b -> p (a b)")
            for half in range(2):
                b = nb()
                for cc in range(4):
                    c = half * 4 + cc
                    TR(banks[b][0:16, cc * 128:(cc + 1) * 128], xres[:, c, 0:N], identf, [("xres", c), "cf"], b, inc=(cc == 3), first=(cc == 0))
                V(lambda v, b=b, half=half: v.tensor_copy(out=yo[:, half * 512:(half + 1) * 512], in_=banks[b][0:16, :]), [BK(b)], [("fp", half)])
            P.dma("sp", lambda q: q.dma_start(out=ys[:, :], in_=yo), reads=[("fp", 0), ("fp", 1)])

        while cast_pending:
            cast_pending.pop(0)()
        if WITH_SAMPLE:
            sample_phase()

        P.final_wait("sp")
        P.emit(block)
    return nc


_CACHE = {}


def kernel(x_prompt, x_sample, mem_prompt, cache_swa_k, cache_swa_v, cache_mem_k, cache_mem_v,
           state_mlstm_c, state_mlstm_n, state_mlstm_m, w_in, b_gates, mlstm_norm_g, swa_sinks,
           w_mem_kv, w_branch, w_mix_out, ln1_g, ln1_b, w_ffn_in, w_ffn_out, ln2_g, ln2_b, _NT=8, _WITH_SAMPLE=True,
           _STOP=None, _NCORES=8):
    f = lambda a: np.ascontiguousarray(np.asarray(a, dtype=np.float32))
    n = 8
    key = (_NT, _WITH_SAMPLE, _STOP)
    if key not in _CACHE:
        _CACHE[key] = build_program(NT=_NT, WITH_SAMPLE=_WITH_SAMPLE, STOP=_STOP)
    nc = _CACHE[key]
    cfc = make_consts()
    shared = {
        "w_in": f(w_in), "b_gates": f(b_gates), "mlstm_norm_g": f(mlstm_norm_g), "swa_sinks": f(swa_sinks),
        "w_mem_kv": f(w_mem_kv), "w_branch": f(w_branch), "w_mix_out": f(w_mix_out), "ln1_g": f(ln1_g), "ln1_b": f(ln1_b),
        "w_ffn_in": f(w_ffn_in), "w_ffn_out": f(w_ffn_out), "ln2_g": f(ln2_g), "ln2_b": f(ln2_b), "cf_in": cfc,
    }
    in_maps = []
    PCORES = [0, 1, 4, 5]
    zx = np.zeros((SEQ, D), np.float32)
    zm = np.zeros((256, D), np.float32)
    for c in range(n):
        b = PCORES.index(c) if c in PCORES else None
        s0 = c * NS
        m = dict(shared)
        m["xp"] = f(x_prompt[b]) if b is not None else zx
        m["xs"] = f(x_sample[s0:s0 + NS, 0])
        m["memp"] = f(mem_prompt[b]) if b is not None else zm
        m["csk"] = f(cache_swa_k[:, s0:s0 + NS].reshape(DEPTH, NS, 128, 128))
        m["csv"] = f(cache_swa_v[:, s0:s0 + NS].reshape(DEPTH, NS, 128, 128))
        m["cmk"] = f(cache_mem_k[:, s0:s0 + NS].reshape(DEPTH, NS, 256, 512))
        m["cmv"] = f(cache_mem_v[:, s0:s0 + NS].reshape(DEPTH, NS, 256, 512))
        m["stc"] = f(state_mlstm_c[:, s0:s0 + NS])
        m["stn"] = f(state_mlstm_n[:, s0:s0 + NS])
        m["stm"] = f(state_mlstm_m[:, s0:s0 + NS])
        in_maps.append(m)
    if _NCORES < n:
        res = run_bass_kernel_spmd(nc, in_maps[:_NCORES], core_ids=list(range(_NCORES)))
        R = [res.results[PCORES.index(i) % _NCORES if i in PCORES else i % _NCORES] for i in range(n)]
    else:
        res = run_bass_kernel_spmd(nc, in_maps, core_ids=list(range(n)))
        R = res.results
    B = 4
    RP = [R[c] for c in PCORES]
    y_p = np.stack([RP[b]["yp"] for b in range(B)])
    y_s = np.concatenate([R[c]["ys"] for c in range(n)])[:, None, :]
    skp = np.stack([RP[b]["o_skp"] for b in range(B)], 1).reshape(DEPTH, B, 128, 2, 64)
    svp_ = np.stack([RP[b]["o_svp"] for b in range(B)], 1).reshape(DEPTH, B, 128, 2, 64)
    sks = np.concatenate([R[c]["o_sks"] for c in range(n)], 1).reshape(DEPTH, 128, 128, 2, 64)
    svs = np.concatenate([R[c]["o_svs"] for c in range(n)], 1).reshape(DEPTH, 128, 128, 2, 64)
    mkp = np.stack([RP[b]["o_mkp"] for b in range(B)], 1).reshape(DEPTH, B, 256, 4, 128)
    mvp = np.stack([RP[b]["o_mvp"] for b in range(B)], 1).reshape(DEPTH, B, 256, 4, 128)
    cp = np.stack([RP[b]["o_cp"] for b in range(B)], 1)
    np_ = np.stack([RP[b]["o_np"] for b in range(B)], 1)
    mp = np.stack([RP[b]["o_mp"] for b in range(B)], 1)
    cs = np.concatenate([R[c]["o_cs"] for c in range(n)], 1)
    ns = np.concatenate([R[c]["o_ns"] for c in range(n)], 1)
    ms = np.concatenate([R[c]["o_ms"] for c in range(n)], 1)
    return (y_p, y_s, skp, svp_, sks, svs, mkp, mvp, cp, np_, mp, cs, ns, ms)
```
